# Optimizing a Trainium2 kernel written in Bass

```python
import math
import jax
import jax.numpy as jnp
from jax import lax
import numpy as np

D_MODEL = 1024
BATCH = 16
SEQ = 2048
DEPTH = 2

HEAD_DIM = 64
A_HEADS = 6
B_HEADS = 5
C_HEADS = 5
MIX_WIDTH = (A_HEADS + B_HEADS + C_HEADS) * HEAD_DIM
DILATED_PATTERNS = ((128, 1), (512, 4), (2048, 16))
MLA_Q_RANK = 256
MLA_KV_RANK = 128
MLA_NOPE_DIM = 64
MLA_ROPE_DIM = 32
MLA_V_DIM = HEAD_DIM
IDX_HEADS = 8
IDX_DIM = 64
IDX_ROPE_DIM = 32
TOPK_MAX = 256
D_FF = 2816
MACARON_WEIGHT = 0.5
N_BUCKETS = 32
MAX_DISTANCE = 128
ROPE_THETA = 10000.0
QBLOCK = 128
RMS_EPS = 1e-6
N_MOD = 9
NEG_INF = -1e30

IN_SPLITS = (A_HEADS * HEAD_DIM, A_HEADS * HEAD_DIM, A_HEADS * HEAD_DIM,
             MLA_Q_RANK, MLA_KV_RANK, MLA_ROPE_DIM,
             C_HEADS * HEAD_DIM, C_HEADS * HEAD_DIM, C_HEADS * HEAD_DIM,
             IDX_HEADS * IDX_DIM, IDX_DIM, IDX_HEADS)
IN_WIDTH = sum(IN_SPLITS)

kernel_name = 'hybrid_dilated_mla_dsa_macaron_adaln'


def rmsnorm(x, g):
    xf = x.astype(jnp.float32)
    y = xf * lax.rsqrt(jnp.mean(xf * xf, axis=-1, keepdims=True) + RMS_EPS)
    return (y * g.astype(jnp.float32)).astype(x.dtype)


def modulate(x, shift, scale):
    return x * (1 + scale[:, None, :]) + shift[:, None, :]


def swiglu(x, w_gu, w_down):
    g, u = jnp.split(x @ w_gu, 2, axis=-1)
    return (jax.nn.silu(g) * u) @ w_down


def t5_bucket(dist):
    n = jnp.maximum(dist, 0)
    max_exact = N_BUCKETS // 2
    nf = jnp.maximum(n, 1).astype(jnp.float32)
    large = max_exact + (jnp.log(nf / max_exact) / math.log(MAX_DISTANCE / max_exact)
                         * (N_BUCKETS - max_exact)).astype(jnp.int32)
    large = jnp.minimum(large, N_BUCKETS - 1)
    return jnp.where(n < max_exact, n, large)


def rope(x, pos):
    half = x.shape[-1] // 2
    inv = ROPE_THETA ** (-jnp.arange(half, dtype=jnp.float32) / half)
    ang = pos.astype(jnp.float32)[..., None] * inv
    cos = jnp.cos(ang)[:, :, None, :]
    sin = jnp.sin(ang)[:, :, None, :]
    x1 = x[..., :half].astype(jnp.float32)
    x2 = x[..., half:].astype(jnp.float32)
    return jnp.concatenate([x1 * cos - x2 * sin, x2 * cos + x1 * sin], axis=-1).astype(x.dtype)


def dilated_branch(q, k, v, window, dilation, bias_tab):
    B, T, H, Dh = q.shape
    n = T // dilation
    steps = window // dilation
    n_pad = -(-n // QBLOCK) * QBLOCK
    nb = n_pad // QBLOCK

    def regroup(z):
        z = z.reshape(B, n, dilation, H, Dh).transpose(0, 2, 1, 3, 4)
        z = jnp.pad(z, ((0, 0), (0, 0), (0, n_pad - n), (0, 0), (0, 0)))
        return z.reshape(B, dilation, nb, QBLOCK, H, Dh)

    def band(z):
        prev = jnp.pad(z, ((0, 0), (0, 0), (1, 0), (0, 0), (0, 0), (0, 0)))[:, :, :-1]
        return jnp.concatenate([prev, z], axis=3)

    qb = regroup(q)
    kband = band(regroup(k))
    vband = band(regroup(v))
    logits = jnp.einsum('bdnqhe,bdnkhe->bdnhqk', qb, kband).astype(jnp.float32) * HEAD_DIM ** -0.5
    qi = jnp.arange(QBLOCK)[:, None]
    kj = jnp.arange(2 * QBLOCK)[None, :]
    step = QBLOCK + qi - kj
    bias = bias_tab[t5_bucket(step * dilation)].astype(jnp.float32).transpose(2, 0, 1)
    in_band = (step >= 0) & (step <= steps)
    key_idx = jnp.arange(nb)[:, None, None] * QBLOCK + kj[None] - QBLOCK
    valid = in_band[None] & (key_idx >= 0)
    logits = jnp.where(valid[None, None, :, None], logits + bias, NEG_INF)
    m = jnp.max(logits, axis=-1, keepdims=True)
    p = jnp.exp(logits - m)
    s = jnp.sum(p, axis=-1, keepdims=True)
    o = jnp.einsum('bdnhqk,bdnkhe->bdnqhe', (p / s).astype(v.dtype), vband)
    lse = (m + jnp.log(s))[..., 0]
    o = o.reshape(B, dilation, n_pad, H, Dh)[:, :, :n].transpose(0, 2, 1, 3, 4).reshape(B, T, H, Dh)
    lse = lse.transpose(0, 1, 2, 4, 3).reshape(B, dilation, n_pad, H)[:, :, :n]
    lse = lse.transpose(0, 2, 1, 3).reshape(B, T, H)
    return o, lse


def dilated_attention(q, k, v, bias_tab):
    results = [dilated_branch(q, k, v, w, d, bias_tab) for (w, d) in DILATED_PATTERNS]
    outs = jnp.stack([r[0] for r in results], axis=0).astype(jnp.float32)
    lses = jnp.stack([r[1] for r in results], axis=0)
    wts = jax.nn.softmax(lses, axis=0)
    return jnp.sum(wts[..., None] * outs, axis=0).astype(q.dtype)


def causal_block_attention(q, k, v, scale):
    B, T, H, _ = q.shape
    kpos = jnp.arange(T)

    def one_block(i):
        start = i * QBLOCK
        q_b = lax.dynamic_slice_in_dim(q, start, QBLOCK, axis=1)
        qpos = start + jnp.arange(QBLOCK)
        logits = jnp.einsum('bqhd,bkhd->bhqk', q_b, k).astype(jnp.float32) * scale
        logits = jnp.where(kpos[None, :] <= qpos[:, None], logits, NEG_INF)
        p = jax.nn.softmax(logits, axis=-1)
        return jnp.einsum('bhqk,bkhd->bqhd', p.astype(v.dtype), v)

    out = lax.map(one_block, jnp.arange(T // QBLOCK))
    return out.transpose(1, 0, 2, 3, 4).reshape(B, T, H, v.shape[-1])


def mla_attention(cq, ckv, k_rope_raw, pos, q_norm, w_uq, kv_norm, w_ukv):
    B, T, _ = cq.shape
    q = (rmsnorm(cq, q_norm) @ w_uq).reshape(B, T, B_HEADS, MLA_NOPE_DIM + MLA_ROPE_DIM)
    q = jnp.concatenate([q[..., :MLA_NOPE_DIM], rope(q[..., MLA_NOPE_DIM:], pos)], axis=-1)
    kv = (rmsnorm(ckv, kv_norm) @ w_ukv).reshape(B, T, B_HEADS, MLA_NOPE_DIM + MLA_V_DIM)
    k_nope = kv[..., :MLA_NOPE_DIM]
    v = kv[..., MLA_NOPE_DIM:]
    k_r = rope(k_rope_raw[:, :, None, :], pos)
    k = jnp.concatenate([k_nope, jnp.broadcast_to(k_r, (B, T, B_HEADS, MLA_ROPE_DIM))], axis=-1)
    return causal_block_attention(q, k, v, (MLA_NOPE_DIM + MLA_ROPE_DIM) ** -0.5)


def dsa_attention(q, k, v, iq, ik, iw, pos, ik_norm, bias_tab):
    B, T, H, Dh = q.shape
    n_sel = min(TOPK_MAX, T // 4)
    iq = iq.reshape(B, T, IDX_HEADS, IDX_DIM)
    iq = jnp.concatenate([rope(iq[..., :IDX_ROPE_DIM], pos), iq[..., IDX_ROPE_DIM:]], axis=-1)
    ik = rmsnorm(ik, ik_norm)[:, :, None, :]
    ik = jnp.concatenate([rope(ik[..., :IDX_ROPE_DIM], pos), ik[..., IDX_ROPE_DIM:]], axis=-1)[:, :, 0]
    iw = iw.astype(jnp.float32) * (IDX_HEADS ** -0.5 * IDX_DIM ** -0.5)
    kpos = jnp.arange(T)
    bidx = jnp.arange(B)[:, None, None]

    def one_block(i):
        start = i * QBLOCK
        qpos = start + jnp.arange(QBLOCK)
        iq_b = lax.dynamic_slice_in_dim(iq, start, QBLOCK, axis=1)
        iw_b = lax.dynamic_slice_in_dim(iw, start, QBLOCK, axis=1)
        rel = jax.nn.relu(jnp.einsum('bqhd,bkd->bqhk', iq_b, ik).astype(jnp.float32))
        score = jnp.einsum('bqhk,bqh->bqk', rel, iw_b)
        score = jnp.where(kpos[None, None, :] <= qpos[None, :, None], score, -jnp.inf)
        _, idx = lax.top_k(score, n_sel)
        dist = qpos[None, :, None] - idx
        kg = k[bidx, idx]
        vg = v[bidx, idx]
        q_b = lax.dynamic_slice_in_dim(q, start, QBLOCK, axis=1)
        logits = jnp.einsum('bqhd,bqkhd->bhqk', q_b, kg).astype(jnp.float32) * Dh ** -0.5
        logits = logits + bias_tab[t5_bucket(dist)].astype(jnp.float32).transpose(0, 3, 1, 2)
        logits = jnp.where((dist >= 0)[:, None], logits, NEG_INF)
        p = jax.nn.softmax(logits, axis=-1)
        return jnp.einsum('bhqk,bqkhd->bqhd', p.astype(v.dtype), vg)

    out = lax.map(one_block, jnp.arange(T // QBLOCK))
    return out.transpose(1, 0, 2, 3, 4).reshape(B, T, H, Dh)


def hybrid_mixer(n, pos, rel_bias, w_in, q_norm, w_uq, kv_norm, w_ukv, ik_norm, w_out):
    B, T, _ = n.shape
    offsets = np.cumsum(IN_SPLITS)[:-1].tolist()
    (qa, ka, va, cq, ckv, kr, qc, kc, vc, iq, ik, iw) = jnp.split(n @ w_in, offsets, axis=-1)
    bias_a = rel_bias[:, :A_HEADS]
    bias_c = rel_bias[:, A_HEADS:]
    oa = dilated_attention(qa.reshape(B, T, A_HEADS, HEAD_DIM), ka.reshape(B, T, A_HEADS, HEAD_DIM),
                           va.reshape(B, T, A_HEADS, HEAD_DIM), bias_a)
    ob = mla_attention(cq, ckv, kr, pos, q_norm, w_uq, kv_norm, w_ukv)
    oc = dsa_attention(qc.reshape(B, T, C_HEADS, HEAD_DIM), kc.reshape(B, T, C_HEADS, HEAD_DIM),
                       vc.reshape(B, T, C_HEADS, HEAD_DIM), iq, ik, iw, pos, ik_norm, bias_c)
    o = jnp.concatenate([oa.reshape(B, T, -1), ob.reshape(B, T, -1), oc.reshape(B, T, -1)], axis=-1)
    return o @ w_out


def setup_inputs(seed: int = 0) -> dict:
    key = jax.random.key(seed)
    ks = iter(jax.random.split(key, 32))
    f32 = jnp.float32
    L, D = DEPTH, D_MODEL

    def dense(shape, fan_in, gain=1.0):
        return gain * fan_in ** -0.5 * jax.random.normal(next(ks), shape, f32)

    def norm_gain(shape):
        return 1.0 + 0.05 * jax.random.normal(next(ks), shape, f32)

    x = jax.random.normal(next(ks), (BATCH, SEQ, D), f32)
    c = jax.random.normal(next(ks), (BATCH, D), f32)
    offset = jax.random.randint(next(ks), (BATCH, 1), 0, 4096, dtype=jnp.int32)
    positions = offset + jnp.arange(SEQ, dtype=jnp.int32)[None, :]
    rel_bias = 0.5 * jax.random.normal(next(ks), (N_BUCKETS, A_HEADS + C_HEADS), f32)
    ada_w = dense((L, D, N_MOD * D), D, 0.5)
    ada_b = 0.02 * jax.random.normal(next(ks), (L, N_MOD * D), f32)
    norm_ffn1 = norm_gain((L, D))
    ffn1_w_gu = dense((L, D, 2 * D_FF), D)
    ffn1_w_down = dense((L, D_FF, D), D_FF)
    norm_mix = norm_gain((L, D))
    w_in = dense((L, D, IN_WIDTH), D)
    mla_q_norm = norm_gain((L, MLA_Q_RANK))
    mla_w_uq = dense((L, MLA_Q_RANK, B_HEADS * (MLA_NOPE_DIM + MLA_ROPE_DIM)), MLA_Q_RANK)
    mla_kv_norm = norm_gain((L, MLA_KV_RANK))
    mla_w_ukv = dense((L, MLA_KV_RANK, B_HEADS * (MLA_NOPE_DIM + MLA_V_DIM)), MLA_KV_RANK)
    idx_k_norm = norm_gain((L, IDX_DIM))
    w_out = dense((L, MIX_WIDTH, D), MIX_WIDTH)
    norm_ffn2 = norm_gain((L, D))
    ffn2_w_gu = dense((L, D, 2 * D_FF), D)
    ffn2_w_down = dense((L, D_FF, D), D_FF)
    final_norm = norm_gain((D,))
    return {'x': x, 'c': c, 'positions': positions, 'rel_bias': rel_bias,
            'ada_w': ada_w, 'ada_b': ada_b,
            'norm_ffn1': norm_ffn1, 'ffn1_w_gu': ffn1_w_gu, 'ffn1_w_down': ffn1_w_down,
            'norm_mix': norm_mix, 'w_in': w_in,
            'mla_q_norm': mla_q_norm, 'mla_w_uq': mla_w_uq,
            'mla_kv_norm': mla_kv_norm, 'mla_w_ukv': mla_w_ukv,
            'idx_k_norm': idx_k_norm, 'w_out': w_out,
            'norm_ffn2': norm_ffn2, 'ffn2_w_gu': ffn2_w_gu, 'ffn2_w_down': ffn2_w_down,
            'final_norm': final_norm}


def reference(x, c, positions, rel_bias, ada_w, ada_b, norm_ffn1, ffn1_w_gu, ffn1_w_down,
              norm_mix, w_in, mla_q_norm, mla_w_uq, mla_kv_norm, mla_w_ukv, idx_k_norm,
              w_out, norm_ffn2, ffn2_w_gu, ffn2_w_down, final_norm):
    h = x
    cond = jax.nn.silu(c)
    for l in range(DEPTH):
        mod = cond @ ada_w[l] + ada_b[l]
        (sh1, sc1, g1, sh2, sc2, g2, sh3, sc3, g3) = jnp.split(mod, N_MOD, axis=-1)
        n1 = modulate(rmsnorm(h, norm_ffn1[l]), sh1, sc1)
        h = h + MACARON_WEIGHT * g1[:, None, :] * swiglu(n1, ffn1_w_gu[l], ffn1_w_down[l])
        n2 = modulate(rmsnorm(h, norm_mix[l]), sh2, sc2)
        h = h + g2[:, None, :] * hybrid_mixer(n2, positions, rel_bias, w_in[l], mla_q_norm[l],
                                              mla_w_uq[l], mla_kv_norm[l], mla_w_ukv[l],
                                              idx_k_norm[l], w_out[l])
        n3 = modulate(rmsnorm(h, norm_ffn2[l]), sh3, sc3)
        h = h + MACARON_WEIGHT * g3[:, None, :] * swiglu(n3, ffn2_w_gu[l], ffn2_w_down[l])
    return rmsnorm(h, final_norm)
```

```python
import numpy as np
from contextlib import ExitStack
import concourse.bass as bass
import concourse.mybir as mybir
from concourse.bass_utils import run_bass_kernel_spmd

F32 = mybir.dt.float32
BF16 = mybir.dt.bfloat16
I32 = mybir.dt.int32
AF = mybir.ActivationFunctionType
ALU = mybir.AluOpType

T = 2048
D = 1024
DFF = 2816
NL = 2
NSEQ = 2
XW = 2432
OFF = 384
NEG = -1.0e30
NDS = 24
WIE = 3656
C_QA, C_KA, C_QC, C_KC, C_IQ, C_IQS, C_KR, C_KRS, C_VA, C_VC, C_MISC = (
    0, 384, 768, 1088, 1408, 1920, 2432, 2464, 2496, 2880, 3200)
EPS = 1e-6
TWO_PI = 2.0 * np.pi
SHR = 0.999999


class K:
    def __init__(self, nc):
        self.nc = nc
        self.eng = {"pe": nc.tensor, "act": nc.scalar, "dve": nc.vector, "pool": nc.gpsimd, "sp": nc.sync}
        self.semh = {}
        self.cnt = {}
        for e in self.eng:
            self.semh[e] = nc.alloc_semaphore("s_" + e)
            self.cnt[e] = 0
        for i in range(NDS):
            self.semh["d%d" % i] = nc.alloc_semaphore("sd%d" % i)
            self.cnt["d%d" % i] = 0
        self.dnext = 0
        self.known = {e: {} for e in self.eng}
        self.writers = {}
        self.readers = {}
        self.n_ins = 0
        self.n_ops = 0
        self.limit = None
        self.floor = {}
        self.log = None

    def barrier(self):
        self.floor = {sk: v for sk, v in self.cnt.items() if v > 0}

    def _deps(self, E, reads, writes):
        deps = {sk: v for sk, v in self.floor.items() if not (E == "pe" and sk == "pe")}

        def need(d):
            for sk, v in d.items():
                if E == "pe" and sk == "pe":
                    continue
                if deps.get(sk, 0) < v:
                    deps[sk] = v

        for k in reads:
            need(self.writers.get(k, {}))
            if isinstance(k, str) and k.startswith("ps"):
                need({sk: v for sk, v in self.readers.get(k, {}).items() if sk != E})
        for k in writes:
            need(self.writers.get(k, {}))
            need(self.readers.get(k, {}))
        return deps

    def _wait(self, E, deps):
        kn = self.known[E]
        for sk, v in deps.items():
            if kn.get(sk, 0) >= v:
                continue
            self.eng[E].wait_ge(self.semh[sk], v)
            kn[sk] = v
            self.n_ins += 1

    def _record(self, me, reads, writes):
        sk, v = me
        for k in writes:
            self.writers[k] = {sk: v}
            self.readers[k] = {}
        for k in reads:
            if k in writes:
                continue
            r = self.readers.setdefault(k, {})
            r[sk] = v

    def op(self, E, reads, writes, fn):
        self.n_ops += 1
        if self.log is not None:
            self.log.append((self.n_ops, E, reads, writes))
        if self.limit is not None and self.n_ops > self.limit:
            return
        self._wait(E, self._deps(E, reads, writes))
        ins = fn(self.eng[E])
        self.cnt[E] += 1
        ins.then_inc(self.semh[E], 1)
        self.n_ins += 1
        self._record((E, self.cnt[E]), reads, writes)

    def dma(self, Q, out, in_, reads, writes):
        self.n_ops += 1
        if self.log is not None:
            self.log.append((self.n_ops, "dma-" + Q, reads, writes))
        if self.limit is not None and self.n_ops > self.limit:
            return
        k = self.dnext
        self.dnext = (k + 1) % NDS
        sk = "d%d" % k
        deps = self._deps(Q, reads, writes)
        if self.cnt[sk] > 0 and deps.get(sk, 0) < self.cnt[sk]:
            deps[sk] = self.cnt[sk]
        self._wait(Q, deps)
        ins = self.eng[Q].dma_start(out=out, in_=in_)
        self.cnt[sk] += 16
        ins.then_inc(self.semh[sk], 16)
        self.n_ins += 1
        self._record((sk, self.cnt[sk]), reads, writes)

    def finish(self):
        for E in self.eng:
            deps = {}
            for sk, v in self.cnt.items():
                if v > 0:
                    deps[sk] = v
            self._wait(E, deps)


def t5_bucket_np(dist):
    n = np.maximum(dist, 0)
    nf = np.maximum(n, 1).astype(np.float32)
    large = 16 + (np.log(nf / np.float32(16)) / np.float32(np.log(128 / 16)) * np.float32(16)).astype(np.int32)
    large = np.minimum(large, 31)
    return np.where(n < 16, n, large)


EXPERIMENT = None
LOG = False


def build(stop=None, dbg_out=(), limit=None):
    nc = bass.Bass("TRN2", target_bir_lowering=False)

    def din(name, shape, dt=F32):
        return nc.dram_tensor(name, list(shape), dt, kind="ExternalInput").ap()

    def dscr(name, shape, dt):
        kind = "ExternalOutput" if name in dbg_out else "Internal"
        return nc.dram_tensor(name, list(shape), dt, kind=kind).ap()

    x_in = din("x", [NSEQ, T, D])
    cT_in = din("cT", [128, 8, NSEQ])
    pos_in = din("posr", [NSEQ, 128, T], I32)
    adaw_in = din("ada_w", [NL, D, 9 * D])
    adab_in = din("ada_bT", [128, NL, 72])
    gains_in = din("gainsT", [128, NL, 3, 8])
    fin_in = din("fin_bc", [128, D])
    wgu_in = [din("wgu1", [NL, D, 2 * DFF]), None, din("wgu2", [NL, D, 2 * DFF])]
    wd_in = [din("wd1", [NL, DFF, D]), None, din("wd2", [NL, DFF, D])]
    wie_in = din("w_in_ext", [NL, D, WIE])
    wuq_in = din("w_uq2", [NL, 256, 960])
    wukv_in = din("w_ukv2", [NL, 128, 640])
    qn_in = din("qnT", [128, NL, 2])
    kvn_in = din("kvnT", [128, NL])
    ikn_in = din("iknT", [64, NL, 2])
    wout_in = din("w_out", [NL, D, D])
    bstrip_in = din("bstrip", [11, 128, XW])
    cnts_in = din("cnts", [2, 128, XW])
    ropec_in = din("ropec", [128, 8])
    ident_in = din("ident", [128, 128])
    negm_in = din("negmask", [128, 128])
    out_d = nc.dram_tensor("out", [NSEQ, T, D], F32, kind="ExternalOutput").ap()

    H = dscr("H", [NSEQ, T, D], F32)
    QaT = dscr("QaT", [NSEQ, 384, T], BF16)
    KaT = dscr("KaT", [NSEQ, 384, T], BF16)
    QcT = dscr("QcT", [NSEQ, 320, T], BF16)
    KcT = dscr("KcT", [NSEQ, 320, T], BF16)
    IqT = dscr("IqT", [NSEQ, 512, T], BF16)
    IkT = dscr("IkT", [NSEQ, 64, T], BF16)
    KrT = dscr("KrT", [NSEQ, 32, T], BF16)
    Va = dscr("Va", [NSEQ, T, 6 * 66], BF16)
    Vc = dscr("Vc", [NSEQ, T, 5 * 66], BF16)
    Vb = dscr("Vb", [NSEQ, T, 5 * 66], BF16)
    Iw = dscr("Iw", [NSEQ, T, 8], F32)
    QbT = dscr("QbT", [NSEQ, 5, 96, T], BF16)
    KnT = dscr("KnT", [NSEQ, 320, T], BF16)
    Oall = dscr("Oall", [NSEQ, T, D], BF16)
    SelT = dscr("SelT", [NSEQ, T, T], BF16)
    Gs = dscr("Gs", [12, 128, XW], BF16)

    k = K(nc)
    k.log = [] if LOG else None
    k.limit = limit
    es_top = ExitStack()

    uniq = [0]

    def sb(es, name, shape, dt):
        uniq[0] += 1
        return es.enter_context(nc.sbuf_tensor("t%d_%s" % (uniq[0], name), list(shape), dt))

    ps = [es_top.enter_context(nc.psum_tensor("ps%d" % i, [128, 512], F32)) for i in range(6)]
    psb = [es_top.enter_context(nc.psum_tensor("psb%d" % i, [128, 1024], BF16)) for i in range(2)]
    PS = ["ps%d" % i for i in range(6)]
    PSB = ["psb0", "psb1"]
    BK = [[p] for p in PS]

    ident = sb(es_top, "ident", [128, 128], F32)
    identb = sb(es_top, "identb", [128, 128], BF16)
    ones = sb(es_top, "ones", [128, 128], F32)
    negm = sb(es_top, "negm", [128, 128], F32)
    modT = sb(es_top, "modT", [128, NL, 72, NSEQ], F32)
    Amod = sb(es_top, "Amod", [128, NL, 3, NSEQ, 8], F32)
    gains = sb(es_top, "gains", [128, NL, 3, 8], F32)
    ropec = sb(es_top, "ropec", [128, 8], F32)
    qn = sb(es_top, "qn", [128, NL, 2], F32)
    kvn = sb(es_top, "kvn", [128, NL], F32)
    ikn = sb(es_top, "ikn", [64, NL, 2], F32)
    epsb = sb(es_top, "epsb", [128, 1], F32)

    k.dma("sp", ident[:], ident_in[:, :], [], ["ident"])
    k.dma("sp", negm[:], negm_in[:, :], [], ["negm"])
    k.dma("sp", gains[:], gains_in[:, :, :, :], [], ["gains"])
    k.dma("sp", ropec[:], ropec_in[:, :], [], ["ropec"])
    k.dma("sp", qn[:], qn_in[:, :, :], [], ["qn"])
    k.dma("sp", kvn[:], kvn_in[:, :], [], ["kvn"])
    k.dma("sp", ikn[:], ikn_in[:, :, :], [], ["ikn"])
    k.op("dve", ["ident"], ["identb"], lambda e: e.tensor_copy(out=identb[:], in_=ident[:]))
    k.op("dve", [], ["ones"], lambda e: e.memset(ones[:], 1.0))
    k.op("dve", [], ["epsb"], lambda e: e.memset(epsb[:], EPS))

    rot = {"ps": 0, "cast": 0}
    dumped = set()

    def dump(name, ap, shape, dt, keys):
        if name not in dbg_out or name in dumped:
            return
        dumped.add(name)
        dst = nc.dram_tensor("dbg_" + name, list(shape), dt, kind="ExternalOutput").ap()
        k.dma("sp", dst, ap, keys, [("dbg", name)])

    with ExitStack() as es:
        cnt_t = [sb(es, "cnt%d" % i, [128, XW], F32) for i in range(2)]
        st32 = [sb(es, "st32_%d" % i, [128, XW], F32) for i in range(2)]
        gb = [sb(es, "gb%d" % i, [128, XW], BF16) for i in range(2)]
        for i in range(2):
            k.dma("sp", cnt_t[i][:], cnts_in[i], [], ["cnt%d" % i])
        for h in range(12):
            b = h % 2
            if h < 11:
                grp = 0 if h < 6 else 1
                k.dma("sp", st32[b][:], bstrip_in[h], [], ["st32_%d" % b])
                k.op("act", ["st32_%d" % b], ["st32_%d" % b],
                     lambda e, b=b: e.activation(out=st32[b][:], in_=st32[b][:], func=AF.Exp))
                k.op("dve", ["st32_%d" % b, "cnt%d" % grp], ["gb%d" % b],
                     lambda e, b=b, grp=grp: e.tensor_tensor(out=gb[b][:], in0=st32[b][:], in1=cnt_t[grp][:], op=ALU.mult))
            else:
                k.op("dve", ["cnt1"], ["gb%d" % b], lambda e, b=b: e.tensor_copy(out=gb[b][:], in_=cnt_t[1][:]))
            k.dma("pool", Gs[h], gb[b][:], ["gb%d" % b], [("Gs", h)])

    k.barrier()
    if stop == "p0a":
        k.finish()
        es_top.close()
        return nc, k

    with ExitStack() as es:
        cT = sb(es, "cT", [128, 8, NSEQ], F32)
        condT = sb(es, "condT", [128, 8, NSEQ], F32)
        adab = sb(es, "adab", [128, NL, 72], F32)
        aw = [sb(es, "aw%d" % i, [128, 8, 1152], F32) for i in range(2)]
        k.dma("sp", cT[:], cT_in[:, :, :], [], ["cT"])
        k.dma("sp", adab[:], adab_in[:, :, :], [], ["adab"])
        k.op("act", ["cT"], ["condT"], lambda e: e.activation(out=condT[:], in_=cT[:], func=AF.Silu))
        slab_i = 0
        for l in range(NL):
            pm = ps[l]
            for slab in range(8):
                b = slab_i % 2
                slab_i += 1
                src = adaw_in[l].rearrange("(kc p) n -> p kc n", p=128)[:, :, slab * 1152:(slab + 1) * 1152]
                k.dma("sp", aw[b][:], src, [], ["aw%d" % b])
                for j in range(9):
                    ch = slab * 9 + j
                    for kc in range(8):
                        k.op("pe", ["aw%d" % b, "condT"], BK[l],
                             lambda e, b=b, j=j, kc=kc, ch=ch, pm=pm: e.matmul(
                                 pm[:, ch * 2:ch * 2 + 2], lhsT=aw[b][:, kc, j * 128:(j + 1) * 128],
                                 rhs=condT[:, kc, :], start=(kc == 0), stop=(kc == 7)))
            for s in range(NSEQ):
                pv = pm[:, 0:144].rearrange("p (c s) -> p c s", s=2)[:, :, s]
                k.op("dve", BK[l] + ["adab"], ["modT"],
                     lambda e, l=l, s=s, pv=pv: e.tensor_tensor(out=modT[:, l, :, s], in0=pv, in1=adab[:, l, :], op=ALU.add))
        for l in range(NL):
            for w in range(3):
                for s in range(NSEQ):
                    sc = modT[:, l, (3 * w + 1) * 8:(3 * w + 2) * 8, s]
                    k.op("dve", ["modT", "gains"], ["Amod"],
                         lambda e, l=l, w=w, s=s, sc=sc: e.scalar_tensor_tensor(
                             out=Amod[:, l, w, s, :], in0=sc, scalar=1.0, in1=gains[:, l, w, :],
                             op0=ALU.add, op1=ALU.mult))

    k.barrier()
    dump("modT", modT[:], [128, NL, 72, NSEQ], F32, ["modT"])
    dump("Amod", Amod[:], [128, NL, 3, NSEQ, 8], F32, ["Amod"])
    if stop == "p0b":
        k.finish()
        es_top.close()
        return nc, k

    def next_ps(lo=2, hi=6):
        i = lo + rot["ps"] % (hi - lo)
        rot["ps"] += 1
        return i

    def cast_eng():
        e = ("act", "dve", "pool")[rot["cast"] % 3]
        rot["cast"] += 1
        return e

    def emit_cast(E, dst, src, reads, writes):
        if E == "act":
            k.op("act", reads, writes, lambda e: e.copy(out=dst, in_=src))
        else:
            k.op(E, reads, writes, lambda e: e.tensor_copy(out=dst, in_=src))

    def gate_bcast(gt, gkey, l, w, s, mult):
        for half in range(2):
            pi = next_ps()
            for q in range(4):
                kc = half * 4 + q
                col = (3 * w + 2) * 8 + kc
                dt_key = "dtmp%d" % (kc % 2)
                dtile = dtmp[kc % 2]
                k.op("dve", ["ident", "modT"], [dt_key],
                     lambda e, dtile=dtile, col=col: e.tensor_scalar(
                         out=dtile[:], in0=ident[:], scalar1=modT[:, l, col, s:s + 1], scalar2=None, op0=ALU.mult))
                k.op("pe", [dt_key, "ones"], [PS[pi]],
                     lambda e, pi=pi, q=q, dtile=dtile: e.matmul(
                         ps[pi][:, q * 128:(q + 1) * 128], lhsT=ones[:], rhs=dtile[:], start=True, stop=True))
            k.op("act", [PS[pi]], [gkey],
                 lambda e, pi=pi, half=half: e.activation(
                     out=gt[:, half * 512:(half + 1) * 512], in_=ps[pi][:], func=AF.Copy, scale=float(mult)))

    dtmp = [sb(es_top, "dtmp%d" % i, [128, 128], F32) for i in range(2)]

    def norm_chunk(hc, hkey, nt, ytiles, xT, xkey, l, w, s, ssq, std, rstd, junk):
        for t in range(nt):
            k.op("act", [hkey], ["junk", "ssq"],
                 lambda e, t=t: e.activation(out=junk[:], in_=hc[:, t, :], func=AF.Square, accum_out=ssq[:, t:t + 1]))
        k.op("act", ["ssq", "epsb"], ["std"],
             lambda e: e.activation(out=std[:, 0:nt], in_=ssq[:, 0:nt], func=AF.Sqrt, bias=epsb[:], scale=1.0 / D))
        k.op("dve", ["std"], ["rstd"], lambda e: e.reciprocal(out=rstd[:, 0:nt], in_=std[:, 0:nt]))
        W = nt * 128
        per_bank = 512 // W
        gsz = 2 * per_bank
        for t in range(nt):
            k.op("dve", [hkey, "rstd"], ["y%d" % t],
                 lambda e, t=t: e.tensor_scalar(out=ytiles[t][:], in0=hc[:, t, :], scalar1=rstd[:, t:t + 1],
                                                scalar2=None, op0=ALU.mult))
        for g0 in range(0, 8, gsz):
            for kc in range(g0, g0 + gsz):
                bank = ((kc - g0) // per_bank) % 2
                if EXPERIMENT == "banks" and g0 > 0:
                    bank += 2
                slot = kc % per_bank
                for t in range(nt):
                    col = slot * W + t * 128
                    k.op("pe", ["y%d" % t, "ident"], [PS[bank]],
                         lambda e, bank=bank, col=col, kc=kc, t=t: e.transpose(
                             out=ps[bank][:, col:col + 128], in_=ytiles[t][:, kc * 128:(kc + 1) * 128],
                             identity=ident[:]))
            for kc in range(g0, g0 + gsz):
                bank = ((kc - g0) // per_bank) % 2
                if EXPERIMENT == "banks" and g0 > 0:
                    bank += 2
                slot = kc % per_bank
                col = slot * W
                k.op("act", [PS[bank], "Amod", "modT"], [xkey],
                     lambda e, bank=bank, col=col, kc=kc: e.activation(
                         out=xT[:, kc, 0:W], in_=ps[bank][:, col:col + W], func=AF.Identity,
                         bias=modT[:, l, 3 * w * 8 + kc, s:s + 1], scale=Amod[:, l, w, s, kc:kc + 1]))


    def ffn_phase(l, w, src, final):
        with ExitStack() as es:
            wgu = sb(es, "wgu", [128, 8, 2 * DFF], BF16)
            wd = sb(es, "wd", [128, 22, D], BF16)
            stg = [sb(es, "stg%d" % i, [128, 1024], F32) for i in range(2)]
            hc = [sb(es, "hc%d" % i, [128, 2, D], F32) for i in range(2)]
            yt = [sb(es, "y%d" % i, [128, D], F32) for i in range(2)]
            xT = sb(es, "xT", [128, 8, 256], BF16)
            hT = sb(es, "hT", [128, 22, 256], BF16)
            sg = [sb(es, "sg%d" % i, [128, 256], F32) for i in range(2)]
            tmp = [sb(es, "tmp%d" % i, [128, 512], F32) for i in range(2)]
            gbc = [sb(es, "gbc%d" % i, [128, D], F32) for i in range(2)]
            junk = sb(es, "junk", [128, D], BF16)
            ssq = sb(es, "ssq", [128, 4], F32)
            std = sb(es, "std", [128, 4], F32)
            rstd = sb(es, "rstd", [128, 4], F32)
            finb = sb(es, "finb", [128, D], F32) if final else None
            if final:
                k.dma("sp", finb[:], fin_in[:, :], [], ["finb"])
            si = 0
            for kc in range(8):
                for c0 in range(0, 2 * DFF, 1024):
                    n = min(1024, 2 * DFF - c0)
                    b = si % 2
                    si += 1
                    k.dma("sp", stg[b][:, 0:n], wgu_in[w][l, kc * 128:(kc + 1) * 128, c0:c0 + n], [], ["stg%d" % b])
                    emit_cast(cast_eng(), wgu[:, kc, c0:c0 + n], stg[b][:, 0:n], ["stg%d" % b], ["wgu"])
            for j in range(22):
                b = si % 2
                si += 1
                k.dma("sp", stg[b][:], wd_in[w][l, j * 128:(j + 1) * 128, :], [], ["stg%d" % b])
                emit_cast(cast_eng(), wd[:, j, :], stg[b][:], ["stg%d" % b], ["wd"])
            for s in range(NSEQ):
                gate_bcast(gbc[s], "gbc%d" % s, l, w, s, 0.5)
            dump("gbc0", gbc[0][:], [128, D], F32, ["gbc0"])
            dump("wgu", wgu[:], [128, 8, 2 * DFF], BF16, ["wgu"])
            ci = 0
            for s in range(NSEQ):
                for c in range(8):
                    b = ci % 2
                    ci += 1
                    hk = "hc%d" % b
                    hsrc = src[s][c * 256:(c + 1) * 256, :].rearrange("(t p) f -> p t f", p=128)
                    k.dma("sp", hc[b][:], hsrc, [("H", s, c // 2)], [hk])
                    norm_chunk(hc[b], hk, 2, yt, xT, "xT", l, w, s, ssq, std, rstd, junk)
                    dump("xT", xT[:], [128, 8, 256], BF16, ["xT"])
                    dump("rstd", rstd[:], [128, 4], F32, ["rstd"])
                    for j in range(22):
                        pg = 2 + (j % 2)
                        pu = 4 + (j % 2)
                        for kc in range(8):
                            k.op("pe", ["wgu", "xT"], [PS[pg]],
                                 lambda e, pg=pg, j=j, kc=kc: e.matmul(
                                     ps[pg][:, 0:256], lhsT=wgu[:, kc, j * 128:(j + 1) * 128], rhs=xT[:, kc, :],
                                     start=(kc == 0), stop=(kc == 7)))
                        for kc in range(8):
                            k.op("pe", ["wgu", "xT"], [PS[pu]],
                                 lambda e, pu=pu, j=j, kc=kc: e.matmul(
                                     ps[pu][:, 0:256], lhsT=wgu[:, kc, DFF + j * 128:DFF + (j + 1) * 128],
                                     rhs=xT[:, kc, :], start=(kc == 0), stop=(kc == 7)))
                        k.op("act", [PS[pg]], ["sg%d" % (j % 2)],
                             lambda e, pg=pg, j=j: e.activation(out=sg[j % 2][:], in_=ps[pg][:, 0:256], func=AF.Silu))
                        k.op("dve", [PS[pu], "sg%d" % (j % 2)], [("hT", j)],
                             lambda e, pu=pu, j=j: e.tensor_tensor(out=hT[:, j, :], in0=ps[pu][:, 0:256],
                                                                   in1=sg[j % 2][:], op=ALU.mult))
                    hT_keys = [("hT", j) for j in range(22)]
                    dump("hT", hT[:], [128, 22, 256], BF16, hT_keys)
                    oi = 0
                    for t in range(2):
                        for half in range(2):
                            py = oi % 2
                            oi += 1
                            pykeys = BK[py]
                            for j in range(22):
                                k.op("pe", ["wd"] + hT_keys, pykeys,
                                     lambda e, py=py, j=j, t=t, half=half: e.matmul(
                                         ps[py][:, :], lhsT=hT[:, j, t * 128:(t + 1) * 128],
                                         rhs=wd[:, j, half * 512:(half + 1) * 512], start=(j == 0), stop=(j == 21)))
                            k.op("dve", pykeys + ["gbc%d" % s], ["tmp%d" % py],
                                 lambda e, py=py, half=half, s=s: e.tensor_tensor(
                                     out=tmp[py][:], in0=ps[py][:, :], in1=gbc[s][:, half * 512:(half + 1) * 512],
                                     op=ALU.mult))
                            k.op("pool", [hk, "tmp%d" % py], [hk],
                                 lambda e, py=py, t=t, half=half, b=b: e.tensor_tensor(
                                     out=hc[b][:, t, half * 512:(half + 1) * 512],
                                     in0=hc[b][:, t, half * 512:(half + 1) * 512], in1=tmp[py][:], op=ALU.add))
                    if not final:
                        hdst = H[s][c * 256:(c + 1) * 256, :].rearrange("(t p) f -> p t f", p=128)
                        k.dma("pool", hdst, hc[b][:], [hk], [("H", s, c // 2)])
                    else:
                        for t in range(2):
                            k.op("act", [hk], ["junk", "ssq"],
                                 lambda e, t=t, b=b: e.activation(out=junk[:], in_=hc[b][:, t, :], func=AF.Square,
                                                                  accum_out=ssq[:, t:t + 1]))
                        k.op("act", ["ssq", "epsb"], ["std"],
                             lambda e: e.activation(out=std[:, 0:2], in_=ssq[:, 0:2], func=AF.Sqrt, bias=epsb[:],
                                                    scale=1.0 / D))
                        k.op("dve", ["std"], ["rstd"], lambda e: e.reciprocal(out=rstd[:, 0:2], in_=std[:, 0:2]))
                        for t in range(2):
                            k.op("dve", [hk, "rstd", "finb"], [hk],
                                 lambda e, t=t, b=b: e.scalar_tensor_tensor(
                                     out=hc[b][:, t, :], in0=hc[b][:, t, :], scalar=rstd[:, t:t + 1], in1=finb[:],
                                     op0=ALU.mult, op1=ALU.mult))
                        odst = out_d[s][c * 256:(c + 1) * 256, :].rearrange("(t p) f -> p t f", p=128)
                        k.dma("pool", odst, hc[b][:], [hk], [("OUT", s, c)])

    def mixproj_phase(l):
        w = 1
        with ExitStack() as es:
            wie = sb(es, "wie", [128, 8, WIE], BF16)
            wuq = sb(es, "wuq", [128, 2, 960], BF16)
            wukv = sb(es, "wukv", [128, 640], BF16)
            stg = [sb(es, "stg%d" % i, [128, 1024], F32) for i in range(2)]
            hc = [sb(es, "hc%d" % i, [128, 4, D], F32) for i in range(2)]
            yt = [sb(es, "y%d" % i, [128, D], F32) for i in range(4)]
            xT = sb(es, "xT", [128, 8, 512], BF16)
            junk = sb(es, "junk", [128, D], BF16)
            ssq = sb(es, "ssq", [128, 4], F32)
            std = sb(es, "std", [128, 4], F32)
            rstd = sb(es, "rstd", [128, 4], F32)
            posi = sb(es, "posi", [128, 512], I32)
            posf = sb(es, "posf", [128, 512], F32)
            u_ = sb(es, "u_", [128, 512], F32)
            ki = sb(es, "ki", [128, 512], I32)
            kf = sb(es, "kf", [128, 512], F32)
            tabs = {n: sb(es, n, [128, 512], F32) for n in ("C128", "S128", "CB", "SB")}
            t1 = [sb(es, "t1_%d" % i, [128, 512], F32) for i in range(2)]
            t2 = [sb(es, "t2_%d" % i, [128, 512], F32) for i in range(2)]
            ost = [sb(es, "ost%d" % i, [128, 512], BF16) for i in range(4)]
            mtile = sb(es, "mtile", [128, 456], F32)
            mn = sb(es, "mn", [128, 512], F32)
            ss3 = sb(es, "ss3", [128, 4], F32)
            sd3 = sb(es, "sd3", [128, 4], F32)
            rs3 = sb(es, "rs3", [128, 4], F32)
            cqnT = sb(es, "cqnT", [128, 2, 512], BF16)
            ckvnT = sb(es, "ckvnT", [128, 512], BF16)
            vst = [sb(es, "vst%d" % i, [128, 6, 66], BF16) for i in range(3)]
            iwst = sb(es, "iwst", [128, 4, 8], F32)
            ikst = sb(es, "ikst", [64, 512], BF16)
            ti = [sb(es, "ti%d" % i, [64, 128], F32) for i in range(2)]
            for i in range(3):
                k.op("pool", [], ["vst%d" % i], lambda e, i=i: e.memset(vst[i][:], 1.0))
            si = 0
            for kc in range(8):
                for c0 in range(0, WIE, 1024):
                    n = min(1024, WIE - c0)
                    b = si % 2
                    si += 1
                    k.dma("sp", stg[b][:, 0:n], wie_in[l, kc * 128:(kc + 1) * 128, c0:c0 + n], [], ["stg%d" % b])
                    emit_cast(cast_eng(), wie[:, kc, c0:c0 + n], stg[b][:, 0:n], ["stg%d" % b], ["wie"])
            for kc in range(2):
                b = si % 2
                si += 1
                k.dma("sp", stg[b][:, 0:960], wuq_in[l, kc * 128:(kc + 1) * 128, :], [], ["stg%d" % b])
                emit_cast(cast_eng(), wuq[:, kc, :], stg[b][:, 0:960], ["stg%d" % b], ["wuq"])
            b = si % 2
            si += 1
            k.dma("sp", stg[b][:, 0:640], wukv_in[l, :, :], [], ["stg%d" % b])
            emit_cast(cast_eng(), wukv[:, :], stg[b][:, 0:640], ["stg%d" % b], ["wukv"])

            ctr = {"o": 0, "t": 0, "v": 0, "e": 0}

            def evac_copy(dst, src, reads, writes, allow_act=True):
                ctr["e"] += 1
                if allow_act and ctr["e"] % 2 == 0:
                    k.op("act", reads, writes, lambda e: e.activation(out=dst, in_=src, func=AF.Copy))
                else:
                    k.op("dve", reads, writes, lambda e: e.tensor_copy(out=dst, in_=src))

            def make_table(name, invcol, addc, sccol, bscol):
                k.op("dve", ["posf", "ropec"], ["u_"],
                     lambda e: e.tensor_scalar(out=u_[:], in0=posf[:], scalar1=ropec[:, invcol:invcol + 1],
                                               scalar2=float(addc), op0=ALU.mult, op1=ALU.add))
                k.op("dve", ["u_"], ["ki"], lambda e: e.tensor_copy(out=ki[:], in_=u_[:]))
                k.op("dve", ["ki"], ["kf"], lambda e: e.tensor_copy(out=kf[:], in_=ki[:]))
                k.op("dve", ["u_", "kf"], ["u_"],
                     lambda e: e.tensor_tensor(out=u_[:], in0=u_[:], in1=kf[:], op=ALU.subtract))
                k.op("dve", ["u_"], ["kf"],
                     lambda e: e.scalar_tensor_tensor(out=kf[:], in0=u_[:], scalar=0.0, in1=u_[:],
                                                      op0=ALU.is_lt, op1=ALU.add))
                k.op("act", ["kf", "ropec"], [name],
                     lambda e: e.activation(out=tabs[name][:], in_=kf[:], func=AF.Sin,
                                            bias=ropec[:, bscol:bscol + 1], scale=ropec[:, sccol:sccol + 1]))

            ci = 0
            for s in range(NSEQ):
                for c in range(4):
                    b = ci % 2
                    ci += 1
                    hk = "hc%d" % b
                    csl = slice(c * 512, (c + 1) * 512)
                    hsrc = H[s][csl, :].rearrange("(t p) f -> p t f", p=128)
                    k.dma("sp", hc[b][:], hsrc, [("H", s, c)], [hk])
                    k.dma("sp", posi[:], pos_in[s][:, csl], [], ["posi"])
                    norm_chunk(hc[b], hk, 4, yt, xT, "xT", l, w, s, ssq, std, rstd, junk)
                    k.op("dve", ["posi"], ["posf"], lambda e: e.tensor_copy(out=posf[:], in_=posi[:]))
                    make_table("C128", 0, 0.25, 6, 7)
                    make_table("S128", 0, 0.0, 2, 3)
                    make_table("CB", 1, 0.25, 6, 7)
                    make_table("SB", 1, 0.0, 4, 5)

                    def fm_proj(col0, M, pi):
                        for kc in range(8):
                            k.op("pe", ["wie", "xT"], [PS[pi]],
                                 lambda e, kc=kc: e.matmul(ps[pi][0:M, :], lhsT=wie[:, kc, col0:col0 + M],
                                                           rhs=xT[:, kc, :], start=(kc == 0), stop=(kc == 7)))

                    plain = []
                    for i in range(3):
                        plain.append((QaT, "QaT", i * 128, C_QA + i * 128, 128))
                    for i in range(3):
                        plain.append((KaT, "KaT", i * 128, C_KA + i * 128, 128))
                    for i, M in enumerate((128, 128, 64)):
                        plain.append((QcT, "QcT", i * 128, C_QC + i * 128, M))
                    for i, M in enumerate((128, 128, 64)):
                        plain.append((KcT, "KcT", i * 128, C_KC + i * 128, M))
                    for (dst, dname, r0, col0, M) in plain:
                        pi = next_ps()
                        fm_proj(col0, M, pi)
                        o = ctr["o"] % 4
                        ctr["o"] += 1
                        evac_copy(ost[o][0:M, :], ps[pi][0:M, :], [PS[pi]], ["ost%d" % o])
                        k.dma("pool", dst[s][r0:r0 + M, csl], ost[o][0:M, :], ["ost%d" % o], [(dname, s, r0, c)])
                    roped = [(IqT, "IqT", i * 128, C_IQ + i * 128, C_IQS + i * 128, 128) for i in range(4)]
                    roped.append((KrT, "KrT", 0, C_KR, C_KRS, 32))
                    for (dst, dname, r0, col0, col1, M) in roped:
                        p1 = next_ps()
                        fm_proj(col0, M, p1)
                        p2 = next_ps()
                        fm_proj(col1, M, p2)
                        tb = ctr["t"] % 2
                        ctr["t"] += 1
                        o = ctr["o"] % 4
                        ctr["o"] += 1
                        k.op("dve", [PS[p1], "C128"], ["t1_%d" % tb],
                             lambda e, p1=p1, tb=tb, M=M: e.tensor_tensor(out=t1[tb][0:M, :], in0=ps[p1][0:M, :],
                                                                          in1=tabs["C128"][0:M, :], op=ALU.mult))
                        k.op("dve", [PS[p2], "S128"], ["t2_%d" % tb],
                             lambda e, p2=p2, tb=tb, M=M: e.tensor_tensor(out=t2[tb][0:M, :], in0=ps[p2][0:M, :],
                                                                          in1=tabs["S128"][0:M, :], op=ALU.mult))
                        k.op("pool", ["t1_%d" % tb, "t2_%d" % tb], ["ost%d" % o],
                             lambda e, tb=tb, o=o, M=M: e.tensor_tensor(out=ost[o][0:M, :], in0=t1[tb][0:M, :],
                                                                        in1=t2[tb][0:M, :], op=ALU.add))
                        k.dma("pool", dst[s][r0:r0 + M, csl], ost[o][0:M, :], ["ost%d" % o], [(dname, s, r0, c)])
                    for t in range(4):
                        r0 = c * 512 + t * 128
                        tsl = slice(t * 128, (t + 1) * 128)
                        for (dst, dname, col0, nh) in ((Va, "Va", C_VA, 6), (Vc, "Vc", C_VC, 5)):
                            pi = next_ps()
                            for kc in range(8):
                                k.op("pe", ["wie", "xT"], [PS[pi]],
                                     lambda e, kc=kc, pi=pi, col0=col0, nh=nh: e.matmul(
                                         ps[pi][:, 0:nh * 64], lhsT=xT[:, kc, tsl], rhs=wie[:, kc, col0:col0 + nh * 64],
                                         start=(kc == 0), stop=(kc == 7)))
                            v = ctr["v"] % 3
                            ctr["v"] += 1
                            evac_copy(vst[v][:, 0:nh, 0:64],
                                      ps[pi][:, 0:nh * 64].rearrange("p (h e) -> p h e", e=64), [PS[pi]], ["vst%d" % v],
                                      allow_act=False)
                            k.dma("pool", dst[s][r0:r0 + 128, :],
                                  vst[v][:, 0:nh, :].rearrange("p h e -> p (h e)"), ["vst%d" % v], [(dname, s, r0)])
                        pi = next_ps()
                        for kc in range(8):
                            k.op("pe", ["wie", "xT"], [PS[pi]],
                                 lambda e, kc=kc, pi=pi: e.matmul(
                                     ps[pi][:, 0:456], lhsT=xT[:, kc, tsl], rhs=wie[:, kc, C_MISC:C_MISC + 456],
                                     start=(kc == 0), stop=(kc == 7)))
                        k.op("act", [PS[pi]], ["mtile"], lambda e, pi=pi: e.copy(out=mtile[:], in_=ps[pi][:, 0:456]))
                        segs = ((0, 256), (256, 384), (384, 448))
                        for j, (a0, a1) in enumerate(segs):
                            k.op("act", ["mtile"], ["junk", "ss3"],
                                 lambda e, j=j, a0=a0, a1=a1: e.activation(
                                     out=junk[:, 0:a1 - a0], in_=mtile[:, a0:a1], func=AF.Square,
                                     accum_out=ss3[:, j:j + 1]))
                        for j, (a0, a1) in enumerate(segs):
                            k.op("act", ["ss3", "epsb"], ["sd3"],
                                 lambda e, j=j, a0=a0, a1=a1: e.activation(
                                     out=sd3[:, j:j + 1], in_=ss3[:, j:j + 1], func=AF.Sqrt, bias=epsb[:],
                                     scale=1.0 / (a1 - a0)))
                        k.op("dve", ["sd3"], ["rs3"], lambda e: e.reciprocal(out=rs3[:, 0:3], in_=sd3[:, 0:3]))
                        for j, (a0, a1) in enumerate(segs):
                            k.op("dve", ["mtile", "rs3"], ["mn"],
                                 lambda e, j=j, a0=a0, a1=a1: e.tensor_scalar(
                                     out=mn[:, a0:a1], in0=mtile[:, a0:a1], scalar1=rs3[:, j:j + 1], scalar2=None,
                                     op0=ALU.mult))
                        for (d0, s0, n) in ((448, 400, 16), (464, 384, 16), (480, 416, 32)):
                            k.op("dve", ["mn"], ["mn"],
                                 lambda e, d0=d0, s0=s0, n=n: e.tensor_copy(out=mn[:, d0:d0 + n], in_=mn[:, s0:s0 + n]))
                        k.op("pool", ["mtile"], ["iwst"],
                             lambda e, t=t: e.tensor_copy(out=iwst[:, t, :], in_=mtile[:, 448:456]))
                        pa = next_ps()
                        pb = next_ps()
                        for j in range(3):
                            k.op("pe", ["mn", "ident"], [PS[pa]],
                                 lambda e, j=j, pa=pa: e.transpose(out=ps[pa][:, j * 128:(j + 1) * 128],
                                                                   in_=mn[:, j * 128:(j + 1) * 128], identity=ident[:]))
                        k.op("pe", ["mn", "ident"], [PS[pa]],
                             lambda e, pa=pa: e.transpose(out=ps[pa][0:64, 384:512], in_=mn[:, 384:448], identity=ident[:]))
                        k.op("pe", ["mn", "ident"], [PS[pb]],
                             lambda e, pb=pb: e.transpose(out=ps[pb][0:64, 0:128], in_=mn[:, 448:512], identity=ident[:]))
                        for j in range(2):
                            k.op("act", [PS[pa], "qn"], ["cqnT"],
                                 lambda e, j=j, pa=pa: e.activation(out=cqnT[:, j, tsl], in_=ps[pa][:, j * 128:(j + 1) * 128],
                                                                    func=AF.Copy, scale=qn[:, l, j:j + 1]))
                        k.op("act", [PS[pa], "kvn"], ["ckvnT"],
                             lambda e, pa=pa: e.activation(out=ckvnT[:, tsl], in_=ps[pa][:, 256:384], func=AF.Copy,
                                                           scale=kvn[:, l:l + 1]))
                        k.op("dve", [PS[pa], "ikn", "C128"], ["ti0"],
                             lambda e, pa=pa: e.scalar_tensor_tensor(
                                 out=ti[0][:], in0=ps[pa][0:64, 384:512], scalar=ikn[:, l, 0:1],
                                 in1=tabs["C128"][0:64, tsl], op0=ALU.mult, op1=ALU.mult))
                        k.op("dve", [PS[pb], "ikn", "S128"], ["ti1"],
                             lambda e, pb=pb: e.scalar_tensor_tensor(
                                 out=ti[1][:], in0=ps[pb][0:64, 0:128], scalar=ikn[:, l, 1:2],
                                 in1=tabs["S128"][0:64, tsl], op0=ALU.mult, op1=ALU.mult))
                        k.op("pool", ["ti0", "ti1"], ["ikst"],
                             lambda e: e.tensor_tensor(out=ikst[:, tsl], in0=ti[0][:], in1=ti[1][:], op=ALU.add))
                    k.dma("pool", IkT[s][:, csl], ikst[:], ["ikst"], [("IkT", s, c)])
                    k.dma("pool", Iw[s][csl, :].rearrange("(t p) h -> p t h", p=128), iwst[:], ["iwst"], [("Iw", s, c)])
                    for h in range(5):
                        p1 = next_ps()
                        p2 = next_ps()
                        for (pp, cbase) in ((p1, 0), (p2, 480)):
                            for kc in range(2):
                                k.op("pe", ["wuq", "cqnT"], [PS[pp]],
                                     lambda e, kc=kc, pp=pp, cbase=cbase, h=h: e.matmul(
                                         ps[pp][0:96, :], lhsT=wuq[:, kc, cbase + h * 96:cbase + (h + 1) * 96],
                                         rhs=cqnT[:, kc, :], start=(kc == 0), stop=(kc == 1)))
                        tb = ctr["t"] % 2
                        ctr["t"] += 1
                        o = ctr["o"] % 4
                        ctr["o"] += 1
                        k.op("dve", [PS[p1], "CB"], ["t1_%d" % tb],
                             lambda e, p1=p1, tb=tb: e.tensor_tensor(out=t1[tb][0:96, :], in0=ps[p1][0:96, :],
                                                                     in1=tabs["CB"][0:96, :], op=ALU.mult))
                        k.op("dve", [PS[p2], "SB"], ["t2_%d" % tb],
                             lambda e, p2=p2, tb=tb: e.tensor_tensor(out=t2[tb][0:96, :], in0=ps[p2][0:96, :],
                                                                     in1=tabs["SB"][0:96, :], op=ALU.mult))
                        k.op("pool", ["t1_%d" % tb, "t2_%d" % tb], ["ost%d" % o],
                             lambda e, tb=tb, o=o: e.tensor_tensor(out=ost[o][0:96, :], in0=t1[tb][0:96, :],
                                                                   in1=t2[tb][0:96, :], op=ALU.add))
                        k.dma("pool", QbT[s][h][:, csl], ost[o][0:96, :], ["ost%d" % o], [("QbT", s, h, c)])
                    for i, M in enumerate((128, 128, 64)):
                        pi = next_ps()
                        k.op("pe", ["wukv", "ckvnT"], [PS[pi]],
                             lambda e, pi=pi, i=i, M=M: e.matmul(ps[pi][0:M, :], lhsT=wukv[:, i * 128:i * 128 + M],
                                                                 rhs=ckvnT[:, :], start=True, stop=True))
                        o = ctr["o"] % 4
                        ctr["o"] += 1
                        evac_copy(ost[o][0:M, :], ps[pi][0:M, :], [PS[pi]], ["ost%d" % o])
                        k.dma("pool", KnT[s][i * 128:i * 128 + M, csl], ost[o][0:M, :], ["ost%d" % o],
                              [("KnT", s, i, c)])
                    for t in range(4):
                        r0 = c * 512 + t * 128
                        tsl = slice(t * 128, (t + 1) * 128)
                        pi = next_ps()
                        k.op("pe", ["wukv", "ckvnT"], [PS[pi]],
                             lambda e, pi=pi, tsl=tsl: e.matmul(ps[pi][:, 0:320], lhsT=ckvnT[:, tsl], rhs=wukv[:, 320:640],
                                                                start=True, stop=True))
                        v = ctr["v"] % 3
                        ctr["v"] += 1
                        evac_copy(vst[v][:, 0:5, 0:64], ps[pi][:, 0:320].rearrange("p (h e) -> p h e", e=64),
                                  [PS[pi]], ["vst%d" % v], allow_act=False)
                        k.dma("pool", Vb[s][r0:r0 + 128, :], vst[v][:, 0:5, :].rearrange("p h e -> p (h e)"),
                              ["vst%d" % v], [("Vb", s, r0)])

    SC_IDX = float(8 ** -0.5 * 64 ** -0.5)

    def att_phase(l, only_heads=None, do_idx=True):
        with ExitStack() as es:
            qt = [sb(es, "qt%d" % i, [128, T], BF16) for i in range(2)]
            kt = [sb(es, "kt%d" % i, [128, T], BF16) for i in range(2)]
            vt = [sb(es, "vt%d" % i, [128, 16, 66], BF16) for i in range(2)]
            gt = [sb(es, "gt%d" % i, [128, XW], BF16) for i in range(2)]
            et = [sb(es, "et%d" % i, [128, 512], BF16) for i in range(2)]
            pt = [sb(es, "pt%d" % i, [128, 512], BF16) for i in range(2)]
            p2 = [sb(es, "p2%d" % i, [128, 512], BF16) for i in range(2)]
            stl = [sb(es, "stl%d" % i, [128, 512], BF16) for i in range(2)]
            ostg = [sb(es, "ostg%d" % i, [128, 4, 64], BF16) for i in range(2)]
            rc = [sb(es, "rc%d" % i, [128, 4], F32) for i in range(2)]
            iq_sb = sb(es, "iq_sb", [128, 4, T], BF16)
            ik2 = sb(es, "ik2", [128, T], BF16)
            iw_sb = sb(es, "iw_sb", [128, 16, 8], F32)
            NZ = 4
            scoreZ = [sb(es, "score%d" % i, [128, T], F32) for i in range(NZ)]
            workZ = [sb(es, "work%d" % i, [128, T], F32) for i in range(NZ)]
            m8Z = [sb(es, "m8_%d" % i, [128, 8], F32) for i in range(NZ)]
            selqZ = [sb(es, "selq%d" % i, [128, T], BF16) for i in range(NZ)]
            rr = [sb(es, "rr%d" % i, [128, 512], F32) for i in range(2)]
            m8 = sb(es, "m8", [128, 8], F32)
            selq = sb(es, "selq", [128, T], BF16)
            sstg = [sb(es, "sstg%d" % i, [128, 512], BF16) for i in range(2)]
            cn = {"h": 0, "i": 0, "t": 0, "o": 0}

            def idx_phase(s):
                k.dma("sp", iq_sb[:], IqT[s].rearrange("(j p) t -> p j t", p=128), [], ["iq_sb"])
                k.dma("sp", ik2[0:64, :], IkT[s], [], ["ik2"])
                k.dma("sp", ik2[64:128, :], IkT[s], [], ["ik2b"])
                k.dma("sp", iw_sb[:], Iw[s].rearrange("(t p) h -> p t h", p=128), [], ["iw_sb"])

                def scores(i, z):
                    sc, sk_ = scoreZ[z], "score%d" % z
                    nk = (i + 1) * 128
                    for kc in range((nk + 511) // 512):
                        n = min(512, nk - kc * 512)
                        ksl = slice(kc * 512, kc * 512 + n)
                        for h8 in range(8):
                            x = cn["i"] % 2
                            cn["i"] += 1
                            pview = psb[x][:, :].bitcast(F32)
                            r0 = (h8 % 2) * 64
                            k.op("pe", ["iq_sb", "ik2", "ik2b"], [PSB[x]],
                                 lambda e, pview=pview, r0=r0, h8=h8, n=n, ksl=ksl: e.matmul(
                                     pview[:, 0:n], lhsT=iq_sb[r0:r0 + 64, h8 // 2, i * 128:(i + 1) * 128],
                                     rhs=ik2[r0:r0 + 64, ksl], start=True, stop=True))
                            k.op("act", [PSB[x]], ["rr%d" % x],
                                 lambda e, pview=pview, x=x, n=n: e.activation(out=rr[x][:, 0:n], in_=pview[:, 0:n],
                                                                               func=AF.Relu, scale=SC_IDX))
                            if h8 == 0:
                                k.op("dve", ["rr%d" % x, "iw_sb"], [sk_],
                                     lambda e, x=x, n=n, ksl=ksl: e.tensor_scalar(
                                         out=sc[:, ksl], in0=rr[x][:, 0:n], scalar1=iw_sb[:, i, 0:1], scalar2=None,
                                         op0=ALU.mult))
                            else:
                                k.op("dve", ["rr%d" % x, "iw_sb", sk_], [sk_],
                                     lambda e, x=x, n=n, ksl=ksl, h8=h8: e.scalar_tensor_tensor(
                                         out=sc[:, ksl], in0=rr[x][:, 0:n], scalar=iw_sb[:, i, h8:h8 + 1],
                                         in1=sc[:, ksl], op0=ALU.mult, op1=ALU.add))
                    dsl = slice(i * 128, (i + 1) * 128)
                    k.op("dve", [sk_, "negm"], [sk_],
                         lambda e, dsl=dsl: e.tensor_tensor(out=sc[:, dsl], in0=sc[:, dsl], in1=negm[:], op=ALU.add))

                def select_and_store(i, z, use_thr):
                    sc, sk_ = scoreZ[z], "score%d" % z
                    sq, qk_ = selqZ[z], "selq%d" % z
                    nk = (i + 1) * 128
                    if use_thr:
                        k.op("dve", [sk_, "m8_%d" % z], [qk_],
                             lambda e: e.tensor_scalar(out=sq[:, 0:nk], in0=sc[:, 0:nk], scalar1=m8Z[z][:, 7:8],
                                                       scalar2=None, op0=ALU.is_ge))
                    else:
                        k.op("dve", [sk_], [qk_],
                             lambda e: e.tensor_scalar(out=sq[:, 0:nk], in0=sc[:, 0:nk], scalar1=-1.0e29,
                                                       scalar2=None, op0=ALU.is_ge))
                    for kb0 in range(0, i + 1, 4):
                        nb = min(4, i + 1 - kb0)
                        x = cn["t"] % 2
                        cn["t"] += 1
                        for j in range(nb):
                            k.op("pe", [qk_, "identb"], [PSB[x]],
                                 lambda e, x=x, j=j, kb0=kb0: e.transpose(
                                     out=psb[x][:, j * 128:(j + 1) * 128], in_=sq[:, (kb0 + j) * 128:(kb0 + j + 1) * 128],
                                     identity=identb[:]))
                        k.op("act", [PSB[x]], ["sstg%d" % x],
                             lambda e, x=x, nb=nb: e.activation(out=sstg[x][:, 0:nb * 128], in_=psb[x][:, 0:nb * 128],
                                                                func=AF.Copy))
                        dst = SelT[s][kb0 * 128:(kb0 + nb) * 128, i * 128:(i + 1) * 128].rearrange("(k p) q -> p k q", p=128)
                        k.dma("pool", dst, sstg[x][:, 0:nb * 128].rearrange("p (k q) -> p k q", q=128),
                              ["sstg%d" % x], [("SelT", s, i, kb0 // 4)])

                for i in (0, 1):
                    scores(i, i)
                    select_and_store(i, i, False)
                    yield
                for i0 in range(2, 16, NZ):
                    zs = list(range(min(NZ, 16 - i0)))
                    for z in zs:
                        nk = (i0 + z + 1) * 128
                        scores(i0 + z, z)
                        k.op("pool", ["score%d" % z], ["work%d" % z],
                             lambda e, z=z, nk=nk: e.tensor_copy(out=workZ[z][:, 0:nk], in_=scoreZ[z][:, 0:nk]))
                        yield
                    for r in range(32):
                        for z in zs:
                            nk = (i0 + z + 1) * 128
                            k.op("dve", ["work%d" % z], ["m8_%d" % z],
                                 lambda e, z=z, nk=nk: e.max(out=m8Z[z][:], in_=workZ[z][:, 0:nk]))
                        yield
                        if r < 31:
                            for z in zs:
                                nk = (i0 + z + 1) * 128
                                k.op("dve", ["work%d" % z, "m8_%d" % z], ["work%d" % z],
                                     lambda e, z=z, nk=nk: e.match_replace(
                                         out=workZ[z][:, 0:nk], in_to_replace=m8Z[z][:], in_values=workZ[z][:, 0:nk],
                                         imm_value=NEG))
                            yield
                    for z in zs:
                        select_and_store(i0 + z, z, True)
                        yield

            def head_attn(s, head):
                b = cn["h"] % 2
                cn["h"] += 1
                if head < 6:
                    d, hh = 64, head
                    qsrc = [(QaT[s][hh * 64:(hh + 1) * 64, :], 0, 64)]
                    ksrc = [(KaT[s][hh * 64:(hh + 1) * 64, :], 0, 64)]
                    vsrc = Va[s].rearrange("(t p) (h e) -> p t h e", p=128, e=66)[:, :, hh, :]
                    gidx, maskall, sel = hh, True, False
                elif head < 11:
                    d, hh = 96, head - 6
                    qsrc = [(QbT[s][hh], 0, 96)]
                    ksrc = [(KnT[s][hh * 64:(hh + 1) * 64, :], 0, 64), (KrT[s], 64, 96)]
                    vsrc = Vb[s].rearrange("(t p) (h e) -> p t h e", p=128, e=66)[:, :, hh, :]
                    gidx, maskall, sel = 11, False, False
                else:
                    d, hh = 64, head - 11
                    qsrc = [(QcT[s][hh * 64:(hh + 1) * 64, :], 0, 64)]
                    ksrc = [(KcT[s][hh * 64:(hh + 1) * 64, :], 0, 64)]
                    vsrc = Vc[s].rearrange("(t p) (h e) -> p t h e", p=128, e=66)[:, :, hh, :]
                    gidx, maskall, sel = 6 + hh, True, True
                scale = float(d ** -0.5)
                qk, kk_, vk, gk = "qt%d" % b, "kt%d" % b, "vt%d" % b, "gt%d" % b
                for (src, p0, p1) in qsrc:
                    k.dma("sp", qt[b][p0:p1, :], src, [], [qk])
                kkeys = []
                for j, (src, p0, p1) in enumerate(ksrc):
                    kkeys.append(kk_ + "_%d" % j)
                    k.dma("sp", kt[b][p0:p1, :], src, [], [kk_ + "_%d" % j])
                k.dma("sp", vt[b][:], vsrc, [], [vk])
                k.dma("sp", gt[b][:], Gs[gidx], [], [gk])
                for c in range(4):
                    nkb = 4 * c + 4
                    qsl = slice(c * 512, (c + 1) * 512)

                    def emit_S(kb):
                        pS = 4 + kb % 2
                        k.op("pe", [qk] + kkeys, [PS[pS]],
                             lambda e, pS=pS, kb=kb: e.matmul(ps[pS][:, :], lhsT=kt[b][0:d, kb * 128:(kb + 1) * 128],
                                                              rhs=qt[b][0:d, qsl], start=True, stop=True))

                    emit_S(0)
                    for kb in range(nkb):
                        if kb + 1 < nkb:
                            emit_S(kb + 1)
                        pS = 4 + kb % 2
                        eb = kb % 2
                        k.op("act", [PS[pS]], ["et%d" % eb],
                             lambda e, pS=pS, eb=eb: e.activation(out=et[eb][:], in_=ps[pS][:, :], func=AF.Exp, scale=scale))
                        cur, curk = et[eb], "et%d" % eb
                        if sel:
                            x0 = c * 512 - kb * 128 + OFF
                            skeys = [("SelT", s, 4 * c + q, kb // 4) for q in range(4) if kb <= 4 * c + q]
                            k.dma("sp", stl[eb][:], SelT[s][kb * 128:(kb + 1) * 128, qsl], skeys, ["stl%d" % eb])
                            k.op("dve", [curk, gk], ["pt%d" % eb],
                                 lambda e, eb=eb, x0=x0, cur=cur: e.tensor_tensor(out=pt[eb][:], in0=cur[:],
                                                                                  in1=gt[b][:, x0:x0 + 512], op=ALU.mult))
                            k.op("dve", ["pt%d" % eb, "stl%d" % eb], ["p2%d" % eb],
                                 lambda e, eb=eb: e.tensor_tensor(out=p2[eb][:], in0=pt[eb][:], in1=stl[eb][:],
                                                                  op=ALU.mult))
                            cur, curk = p2[eb], "p2%d" % eb
                        elif maskall or kb >= 4 * c:
                            x0 = c * 512 - kb * 128 + OFF
                            k.op("dve", [curk, gk], ["pt%d" % eb],
                                 lambda e, eb=eb, x0=x0, cur=cur: e.tensor_tensor(out=pt[eb][:], in0=cur[:],
                                                                                  in1=gt[b][:, x0:x0 + 512], op=ALU.mult))
                            cur, curk = pt[eb], "pt%d" % eb
                        for q in range(4):
                            i = 4 * c + q
                            if kb <= i:
                                k.op("pe", [curk, vk], [PS[q]],
                                     lambda e, q=q, kb=kb, i=i, cur=cur: e.matmul(
                                         ps[q][:, 0:65], lhsT=cur[:, q * 128:(q + 1) * 128], rhs=vt[b][:, kb, 0:65],
                                         start=(kb == 0), stop=(kb == i)))
                        yield
                    ob = cn["o"] % 2
                    cn["o"] += 1
                    for q in range(4):
                        k.op("dve", [PS[q]], ["rc%d" % ob],
                             lambda e, q=q, ob=ob: e.reciprocal(out=rc[ob][:, q:q + 1], in_=ps[q][:, 64:65]))
                        k.op("dve", [PS[q], "rc%d" % ob], ["ostg%d" % ob],
                             lambda e, q=q, ob=ob: e.tensor_scalar(out=ostg[ob][:, q, :], in0=ps[q][:, 0:64],
                                                                   scalar1=rc[ob][:, q:q + 1], scalar2=None, op0=ALU.mult))
                    dst = Oall[s][qsl, head * 64:(head + 1) * 64].rearrange("(t p) e -> p t e", p=128)
                    k.dma("pool", dst, ostg[ob][:], ["ostg%d" % ob], [("Oall", s, c, head)])

            def drain(g):
                for _ in g:
                    pass

            for s in range(NSEQ):
                gi = idx_phase(s) if do_idx else iter(())
                for head in range(16):
                    if only_heads is not None and head not in only_heads:
                        continue
                    if head >= 11:
                        drain(gi)
                    for _ in head_attn(s, head):
                        next(gi, None)
                drain(gi)

    def outproj_phase(l):
        with ExitStack() as es:
            wout = sb(es, "wout", [128, 8, D], BF16)
            stg = [sb(es, "stg%d" % i, [128, 1024], F32) for i in range(2)]
            gbc = [sb(es, "gbc%d" % i, [128, D], F32) for i in range(2)]
            oc = [sb(es, "oc%d" % i, [128, 4, D], BF16) for i in range(2)]
            hc = [sb(es, "hc%d" % i, [128, 4, D], F32) for i in range(2)]
            oT = sb(es, "oT", [128, 8, 512], BF16)
            tmp = [sb(es, "tmp%d" % i, [128, 512], F32) for i in range(2)]
            for kc in range(8):
                b = kc % 2
                k.dma("sp", stg[b][:], wout_in[l, kc * 128:(kc + 1) * 128, :], [], ["stg%d" % b])
                emit_cast(cast_eng(), wout[:, kc, :], stg[b][:], ["stg%d" % b], ["wout"])
            for s in range(NSEQ):
                gate_bcast(gbc[s], "gbc%d" % s, l, 1, s, 1.0)
            ci = 0
            ti_ = 0
            for s in range(NSEQ):
                for c in range(4):
                    b = ci % 2
                    ci += 1
                    csl = slice(c * 512, (c + 1) * 512)
                    k.dma("sp", oc[b][:], Oall[s][csl, :].rearrange("(t p) f -> p t f", p=128), [], ["oc%d" % b])
                    k.dma("sp", hc[b][:], H[s][csl, :].rearrange("(t p) f -> p t f", p=128), [("H", s, c)], ["hc%d" % b])
                    for kc in range(8):
                        x = kc % 2
                        for t in range(4):
                            k.op("pe", ["oc%d" % b, "identb"], [PSB[x]],
                                 lambda e, x=x, t=t, kc=kc, b=b: e.transpose(
                                     out=psb[x][:, t * 128:(t + 1) * 128], in_=oc[b][:, t, kc * 128:(kc + 1) * 128],
                                     identity=identb[:]))
                        if kc % 2 == 0:
                            k.op("act", [PSB[x]], ["oT"],
                                 lambda e, x=x, kc=kc: e.activation(out=oT[:, kc, :], in_=psb[x][:, 0:512], func=AF.Copy))
                        else:
                            k.op("dve", [PSB[x]], ["oT"],
                                 lambda e, x=x, kc=kc: e.tensor_copy(out=oT[:, kc, :], in_=psb[x][:, 0:512]))
                    for t in range(4):
                        for half in range(2):
                            pi = next_ps()
                            for kc in range(8):
                                k.op("pe", ["oT", "wout"], [PS[pi]],
                                     lambda e, pi=pi, kc=kc, t=t, half=half: e.matmul(
                                         ps[pi][:, :], lhsT=oT[:, kc, t * 128:(t + 1) * 128],
                                         rhs=wout[:, kc, half * 512:(half + 1) * 512], start=(kc == 0), stop=(kc == 7)))
                            x = ti_ % 2
                            ti_ += 1
                            k.op("dve", [PS[pi], "gbc%d" % s], ["tmp%d" % x],
                                 lambda e, pi=pi, x=x, half=half, s=s: e.tensor_tensor(
                                     out=tmp[x][:], in0=ps[pi][:, :], in1=gbc[s][:, half * 512:(half + 1) * 512], op=ALU.mult))
                            k.op("pool", ["hc%d" % b, "tmp%d" % x], ["hc%d" % b],
                                 lambda e, x=x, t=t, half=half, b=b: e.tensor_tensor(
                                     out=hc[b][:, t, half * 512:(half + 1) * 512],
                                     in0=hc[b][:, t, half * 512:(half + 1) * 512], in1=tmp[x][:], op=ALU.add))
                    k.dma("pool", H[s][csl, :].rearrange("(t p) f -> p t f", p=128), hc[b][:], ["hc%d" % b], [("H", s, c)])

    stops = {"ffn1": 1, "mixproj": 2, "att": 3, "outproj": 4, "ffn2": 5}

    def done():
        k.finish()
        es_top.close()
        return nc, k

    for l in range(NL):
        ffn_phase(l, 0, x_in if l == 0 else H, False)
        k.barrier()
        if stop == "ffn1":
            return done()
        mixproj_phase(l)
        k.barrier()
        if stop == "mixproj":
            return done()
        if stop in ("attA", "attB", "attC"):
            heads = {"attA": [0], "attB": [6], "attC": [11]}[stop]
            att_phase(l, only_heads=heads, do_idx=(stop == "attC"))
            return done()
        att_phase(l)
        k.barrier()
        if stop == "att":
            return done()
        outproj_phase(l)
        k.barrier()
        if stop == "outproj":
            return done()
        ffn_phase(l, 2, H, l == NL - 1)
        k.barrier()
        if stop == "ffn2":
            return done()
    return done()


def host_prep(inputs):
    f32 = np.float32
    x = np.asarray(inputs["x"], f32)
    c = np.asarray(inputs["c"], f32)
    pos = np.asarray(inputs["positions"], np.int32)
    rel_bias = np.asarray(inputs["rel_bias"], f32)
    w_in = np.asarray(inputs["w_in"], f32)
    offs = np.cumsum([0, 384, 384, 384, 256, 128, 32, 320, 320, 320, 512, 64, 8])
    qa, ka, va, cq, ckv, kr, qc, kc_, vc, iq, ik, iw = [w_in[:, :, offs[i]:offs[i + 1]] for i in range(12)]
    iq_h = iq.reshape(NL, D, 8, 64)
    iq_sw = np.concatenate([iq_h[..., 16:32], iq_h[..., 0:16], iq_h[..., 32:64]], axis=-1).reshape(NL, D, 512)
    kr_sw = np.concatenate([kr[..., 16:32], kr[..., 0:16]], axis=-1)
    w_in_ext = np.ascontiguousarray(np.concatenate(
        [qa, ka, qc, kc_, iq, iq_sw, kr, kr_sw, va, vc, cq, ckv, ik, iw], axis=-1))
    assert w_in_ext.shape[-1] == WIE
    w_uq = np.asarray(inputs["mla_w_uq"], f32).reshape(NL, 256, 5, 96)
    w_uq_sw = np.concatenate([w_uq[..., 0:64], w_uq[..., 80:96], w_uq[..., 64:80]], axis=-1)
    w_uq2 = np.ascontiguousarray(np.concatenate([w_uq.reshape(NL, 256, 480), w_uq_sw.reshape(NL, 256, 480)], axis=-1))
    w_ukv = np.asarray(inputs["mla_w_ukv"], f32).reshape(NL, 128, 5, 128)
    w_ukv2 = np.ascontiguousarray(np.concatenate(
        [w_ukv[..., 0:64].reshape(NL, 128, 320), w_ukv[..., 64:128].reshape(NL, 128, 320)], axis=-1))
    gains = np.stack([np.asarray(inputs[n], f32) for n in ("norm_ffn1", "norm_mix", "norm_ffn2")], axis=1)
    gainsT = np.ascontiguousarray(gains.reshape(NL, 3, 8, 128).transpose(3, 0, 1, 2))
    fin_bc = np.ascontiguousarray(np.broadcast_to(np.asarray(inputs["final_norm"], f32)[None, :], (128, D)))
    ada_bT = np.ascontiguousarray(np.asarray(inputs["ada_b"], f32).reshape(NL, 72, 128).transpose(2, 0, 1))
    qnT = np.ascontiguousarray(np.asarray(inputs["mla_q_norm"], f32).reshape(NL, 2, 128).transpose(2, 0, 1))
    kvnT = np.ascontiguousarray(np.asarray(inputs["mla_kv_norm"], f32).T)
    ikn = np.asarray(inputs["idx_k_norm"], f32)
    ikn_sw = np.concatenate([ikn[:, 16:32], ikn[:, 0:16], ikn[:, 32:64]], axis=-1)
    iknT = np.ascontiguousarray(np.stack([ikn, ikn_sw], axis=-1).transpose(1, 0, 2))
    xs = np.arange(XW)[None, :]
    pp = np.arange(128)[:, None]
    delta = xs - pp - OFF
    bucket = t5_bucket_np(delta)
    bstrip = np.ascontiguousarray(rel_bias[bucket, :].transpose(2, 0, 1))
    dpos = delta >= 0
    cntA = (dpos & (delta <= 128)).astype(f32) + (dpos & (delta % 4 == 0) & (delta <= 512)).astype(f32) \
        + (dpos & (delta % 16 == 0) & (delta <= 2048)).astype(f32)
    cntC = dpos.astype(f32)
    cnts = np.ascontiguousarray(np.stack([cntA, cntC], axis=0))
    inv = (10000.0 ** (-np.arange(16, dtype=np.float64) / 16.0))
    p = np.arange(128)
    ropec = np.zeros((128, 8), f32)
    r64 = p % 64
    isx1 = r64 < 16
    isx2 = (r64 >= 16) & (r64 < 32)
    ropec[:, 0] = np.where(isx1 | isx2, inv[p % 16] / TWO_PI, 0.0)
    sgnA = np.where(isx1, -1.0, np.where(isx2, 1.0, 1.0))
    ropec[:, 2] = -TWO_PI * SHR * sgnA
    ropec[:, 3] = np.pi * SHR * sgnA
    bx1 = (p >= 64) & (p < 80)
    bx2 = (p >= 80) & (p < 96)
    ropec[:, 1] = np.where(bx1 | bx2, inv[p % 16] / TWO_PI, 0.0)
    sgnB = np.where(bx1, -1.0, 1.0)
    ropec[:, 4] = -TWO_PI * SHR * sgnB
    ropec[:, 5] = np.pi * SHR * sgnB
    ropec[:, 6] = -TWO_PI * SHR
    ropec[:, 7] = np.pi * SHR
    ident = np.eye(128, dtype=f32)
    qq = np.arange(128)[:, None]
    kk = np.arange(128)[None, :]
    negmask = np.where(kk <= qq, 0.0, NEG).astype(f32)
    shared = {
        "ada_w": np.asarray(inputs["ada_w"], f32), "ada_bT": ada_bT, "gainsT": gainsT, "fin_bc": fin_bc,
        "wgu1": np.asarray(inputs["ffn1_w_gu"], f32), "wd1": np.asarray(inputs["ffn1_w_down"], f32),
        "wgu2": np.asarray(inputs["ffn2_w_gu"], f32), "wd2": np.asarray(inputs["ffn2_w_down"], f32),
        "w_in_ext": w_in_ext, "w_uq2": w_uq2, "w_ukv2": w_ukv2, "qnT": qnT, "kvnT": kvnT, "iknT": iknT,
        "w_out": np.asarray(inputs["w_out"], f32), "bstrip": bstrip, "cnts": cnts, "ropec": ropec,
        "ident": ident, "negmask": negmask,
    }
    in_maps = []
    for r in range(8):
        b0 = 2 * r
        m = dict(shared)
        m["x"] = np.ascontiguousarray(x[b0:b0 + 2])
        m["cT"] = np.ascontiguousarray(c[b0:b0 + 2].reshape(2, 8, 128).transpose(2, 1, 0))
        m["posr"] = np.ascontiguousarray(np.broadcast_to(pos[b0:b0 + 2][:, None, :], (2, 128, T)))
        in_maps.append(m)
    return in_maps


def kernel(**inputs):
    in_maps = host_prep(inputs)
    nc, _ = build()
    res = run_bass_kernel_spmd(nc, in_maps, core_ids=list(range(8)))
    out = np.concatenate([np.asarray(r["out"]) for r in res.results], axis=0)
    return out.astype(np.float32)
```

```python
import numpy as np
from contextlib import ExitStack
import concourse.bass as bass
import concourse.mybir as mybir
from concourse.bass_utils import run_bass_kernel_spmd

F32 = mybir.dt.float32
BF16 = mybir.dt.bfloat16
I32 = mybir.dt.int32
AF = mybir.ActivationFunctionType
ALU = mybir.AluOpType

T = 2048
D = 1024
DFF = 2816
NL = 2
NSEQ = 2
XW = 2432
OFF = 384
NEG = -1.0e30
NDS = 24
WIE = 3656
C_QA, C_KA, C_QC, C_KC, C_IQ, C_IQS, C_KR, C_KRS, C_VA, C_VC, C_MISC = (
    0, 384, 768, 1088, 1408, 1920, 2432, 2464, 2496, 2880, 3200)
EPS = 1e-6
TWO_PI = 2.0 * np.pi
SHR = 0.999999


class K:
    def __init__(self, nc):
        self.nc = nc
        self.eng = {"pe": nc.tensor, "act": nc.scalar, "dve": nc.vector, "pool": nc.gpsimd, "sp": nc.sync}
        self.semh = {}
        self.cnt = {}
        for e in self.eng:
            self.semh[e] = nc.alloc_semaphore("s_" + e)
            self.cnt[e] = 0
        for i in range(NDS):
            self.semh["d%d" % i] = nc.alloc_semaphore("sd%d" % i)
            self.cnt["d%d" % i] = 0
        self.dnext = 0
        self.known = {e: {} for e in self.eng}
        self.writers = {}
        self.readers = {}
        self.n_ins = 0
        self.n_ops = 0
        self.limit = None
        self.floor = {}
        self.log = None

    def barrier(self):
        self.floor = {sk: v for sk, v in self.cnt.items() if v > 0}

    def _deps(self, E, reads, writes):
        deps = {sk: v for sk, v in self.floor.items() if not (E == "pe" and sk == "pe")}

        def need(d):
            for sk, v in d.items():
                if E == "pe" and sk == "pe":
                    continue
                if deps.get(sk, 0) < v:
                    deps[sk] = v

        for k in reads:
            need(self.writers.get(k, {}))
            if isinstance(k, str) and k.startswith("ps"):
                need({sk: v for sk, v in self.readers.get(k, {}).items() if sk != E})
        for k in writes:
            need(self.writers.get(k, {}))
            need(self.readers.get(k, {}))
        return deps

    def _wait(self, E, deps):
        kn = self.known[E]
        for sk, v in deps.items():
            if kn.get(sk, 0) >= v:
                continue
            self.eng[E].wait_ge(self.semh[sk], v)
            kn[sk] = v
            self.n_ins += 1

    def _record(self, me, reads, writes):
        sk, v = me
        for k in writes:
            self.writers[k] = {sk: v}
            self.readers[k] = {}
        for k in reads:
            if k in writes:
                continue
            r = self.readers.setdefault(k, {})
            r[sk] = v

    def op(self, E, reads, writes, fn):
        self.n_ops += 1
        if self.log is not None:
            self.log.append((self.n_ops, E, reads, writes))
        if self.limit is not None and self.n_ops > self.limit:
            return
        self._wait(E, self._deps(E, reads, writes))
        ins = fn(self.eng[E])
        self.cnt[E] += 1
        ins.then_inc(self.semh[E], 1)
        self.n_ins += 1
        self._record((E, self.cnt[E]), reads, writes)

    def dma(self, Q, out, in_, reads, writes):
        self.n_ops += 1
        if self.log is not None:
            self.log.append((self.n_ops, "dma-" + Q, reads, writes))
        if self.limit is not None and self.n_ops > self.limit:
            return
        k = self.dnext
        self.dnext = (k + 1) % NDS
        sk = "d%d" % k
        deps = self._deps(Q, reads, writes)
        if self.cnt[sk] > 0 and deps.get(sk, 0) < self.cnt[sk]:
            deps[sk] = self.cnt[sk]
        self._wait(Q, deps)
        ins = self.eng[Q].dma_start(out=out, in_=in_)
        self.cnt[sk] += 16
        ins.then_inc(self.semh[sk], 16)
        self.n_ins += 1
        self._record((sk, self.cnt[sk]), reads, writes)

    def finish(self):
        for E in self.eng:
            deps = {}
            for sk, v in self.cnt.items():
                if v > 0:
                    deps[sk] = v
            self._wait(E, deps)


def t5_bucket_np(dist):
    n = np.maximum(dist, 0)
    nf = np.maximum(n, 1).astype(np.float32)
    large = 16 + (np.log(nf / np.float32(16)) / np.float32(np.log(128 / 16)) * np.float32(16)).astype(np.int32)
    large = np.minimum(large, 31)
    return np.where(n < 16, n, large)


EXPERIMENT = None
LOG = False


def build(stop=None, dbg_out=(), limit=None):
    nc = bass.Bass("TRN2", target_bir_lowering=False)

    def din(name, shape, dt=F32):
        return nc.dram_tensor(name, list(shape), dt, kind="ExternalInput").ap()

    def dscr(name, shape, dt):
        kind = "ExternalOutput" if name in dbg_out else "Internal"
        return nc.dram_tensor(name, list(shape), dt, kind=kind).ap()

    x_in = din("x", [NSEQ, T, D])
    cT_in = din("cT", [128, 8, NSEQ])
    pos_in = din("posr", [NSEQ, 128, T], I32)
    adaw_in = din("ada_w", [NL, D, 9 * D])
    adab_in = din("ada_bT", [128, NL, 72])
    gains_in = din("gainsT", [128, NL, 3, 8])
    fin_in = din("fin_bc", [128, D])
    wgu_in = [din("wgu1", [NL, D, 2 * DFF]), None, din("wgu2", [NL, D, 2 * DFF])]
    wd_in = [din("wd1", [NL, DFF, D]), None, din("wd2", [NL, DFF, D])]
    wie_in = din("w_in_ext", [NL, D, WIE])
    wuq_in = din("w_uq2", [NL, 256, 960])
    wukv_in = din("w_ukv2", [NL, 128, 640])
    qn_in = din("qnT", [128, NL, 2])
    kvn_in = din("kvnT", [128, NL])
    ikn_in = din("iknT", [64, NL, 2])
    wout_in = din("w_out", [NL, D, D])
    bstrip_in = din("bstrip", [11, 128, XW])
    cnts_in = din("cnts", [2, 128, XW])
    ropec_in = din("ropec", [128, 8])
    ident_in = din("ident", [128, 128])
    negm_in = din("negmask", [128, 128])
    out_d = nc.dram_tensor("out", [NSEQ, T, D], F32, kind="ExternalOutput").ap()

    H = dscr("H", [NSEQ, T, D], F32)
    QaT = dscr("QaT", [NSEQ, 384, T], BF16)
    KaT = dscr("KaT", [NSEQ, 384, T], BF16)
    QcT = dscr("QcT", [NSEQ, 320, T], BF16)
    KcT = dscr("KcT", [NSEQ, 320, T], BF16)
    IqT = dscr("IqT", [NSEQ, 512, T], BF16)
    IkT = dscr("IkT", [NSEQ, 64, T], BF16)
    KrT = dscr("KrT", [NSEQ, 32, T], BF16)
    Va = dscr("Va", [NSEQ, T, 6 * 66], BF16)
    Vc = dscr("Vc", [NSEQ, T, 5 * 66], BF16)
    Vb = dscr("Vb", [NSEQ, T, 5 * 66], BF16)
    Iw = dscr("Iw", [NSEQ, T, 8], F32)
    QbT = dscr("QbT", [NSEQ, 5, 96, T], BF16)
    KnT = dscr("KnT", [NSEQ, 320, T], BF16)
    Oall = dscr("Oall", [NSEQ, T, D], BF16)
    SelT = dscr("SelT", [NSEQ, T, T], BF16)
    Gs = dscr("Gs", [12, 128, XW], BF16)

    k = K(nc)
    k.log = [] if LOG else None
    k.limit = limit
    es_top = ExitStack()

    uniq = [0]

    def sb(es, name, shape, dt):
        uniq[0] += 1
        return es.enter_context(nc.sbuf_tensor("t%d_%s" % (uniq[0], name), list(shape), dt))

    ps = [es_top.enter_context(nc.psum_tensor("ps%d" % i, [128, 512], F32)) for i in range(6)]
    psb = [es_top.enter_context(nc.psum_tensor("psb%d" % i, [128, 1024], BF16)) for i in range(2)]
    PS = ["ps%d" % i for i in range(6)]
    PSB = ["psb0", "psb1"]
    BK = [[p] for p in PS]

    ident = sb(es_top, "ident", [128, 128], F32)
    identb = sb(es_top, "identb", [128, 128], BF16)
    ones = sb(es_top, "ones", [128, 128], F32)
    negm = sb(es_top, "negm", [128, 128], F32)
    modT = sb(es_top, "modT", [128, NL, 72, NSEQ], F32)
    Amod = sb(es_top, "Amod", [128, NL, 3, NSEQ, 8], F32)
    gains = sb(es_top, "gains", [128, NL, 3, 8], F32)
    ropec = sb(es_top, "ropec", [128, 8], F32)
    qn = sb(es_top, "qn", [128, NL, 2], F32)
    kvn = sb(es_top, "kvn", [128, NL], F32)
    ikn = sb(es_top, "ikn", [64, NL, 2], F32)
    epsb = sb(es_top, "epsb", [128, 1], F32)

    k.dma("sp", ident[:], ident_in[:, :], [], ["ident"])
    k.dma("sp", negm[:], negm_in[:, :], [], ["negm"])
    k.dma("sp", gains[:], gains_in[:, :, :, :], [], ["gains"])
    k.dma("sp", ropec[:], ropec_in[:, :], [], ["ropec"])
    k.dma("sp", qn[:], qn_in[:, :, :], [], ["qn"])
    k.dma("sp", kvn[:], kvn_in[:, :], [], ["kvn"])
    k.dma("sp", ikn[:], ikn_in[:, :, :], [], ["ikn"])
    k.op("dve", ["ident"], ["identb"], lambda e: e.tensor_copy(out=identb[:], in_=ident[:]))
    k.op("dve", [], ["ones"], lambda e: e.memset(ones[:], 1.0))
    k.op("dve", [], ["epsb"], lambda e: e.memset(epsb[:], EPS))

    rot = {"ps": 0, "cast": 0}
    dumped = set()

    def dump(name, ap, shape, dt, keys):
        if name not in dbg_out or name in dumped:
            return
        dumped.add(name)
        dst = nc.dram_tensor("dbg_" + name, list(shape), dt, kind="ExternalOutput").ap()
        k.dma("sp", dst, ap, keys, [("dbg", name)])

    with ExitStack() as es:
        cnt_t = [sb(es, "cnt%d" % i, [128, XW], F32) for i in range(2)]
        st32 = [sb(es, "st32_%d" % i, [128, XW], F32) for i in range(2)]
        gb = [sb(es, "gb%d" % i, [128, XW], BF16) for i in range(2)]
        for i in range(2):
            k.dma("sp", cnt_t[i][:], cnts_in[i], [], ["cnt%d" % i])
        for h in range(12):
            b = h % 2
            if h < 11:
                grp = 0 if h < 6 else 1
                k.dma("sp", st32[b][:], bstrip_in[h], [], ["st32_%d" % b])
                k.op("act", ["st32_%d" % b], ["st32_%d" % b],
                     lambda e, b=b: e.activation(out=st32[b][:], in_=st32[b][:], func=AF.Exp))
                k.op("dve", ["st32_%d" % b, "cnt%d" % grp], ["gb%d" % b],
                     lambda e, b=b, grp=grp: e.tensor_tensor(out=gb[b][:], in0=st32[b][:], in1=cnt_t[grp][:], op=ALU.mult))
            else:
                k.op("dve", ["cnt1"], ["gb%d" % b], lambda e, b=b: e.tensor_copy(out=gb[b][:], in_=cnt_t[1][:]))
            k.dma("pool", Gs[h], gb[b][:], ["gb%d" % b], [("Gs", h)])

    k.barrier()
    if stop == "p0a":
        k.finish()
        es_top.close()
        return nc, k

    with ExitStack() as es:
        cT = sb(es, "cT", [128, 8, NSEQ], F32)
        condT = sb(es, "condT", [128, 8, NSEQ], F32)
        adab = sb(es, "adab", [128, NL, 72], F32)
        aw = [sb(es, "aw%d" % i, [128, 8, 1152], F32) for i in range(2)]
        k.dma("sp", cT[:], cT_in[:, :, :], [], ["cT"])
        k.dma("sp", adab[:], adab_in[:, :, :], [], ["adab"])
        k.op("act", ["cT"], ["condT"], lambda e: e.activation(out=condT[:], in_=cT[:], func=AF.Silu))
        slab_i = 0
        for l in range(NL):
            pm = ps[l]
            for slab in range(8):
                b = slab_i % 2
                slab_i += 1
                src = adaw_in[l].rearrange("(kc p) n -> p kc n", p=128)[:, :, slab * 1152:(slab + 1) * 1152]
                k.dma("sp", aw[b][:], src, [], ["aw%d" % b])
                for j in range(9):
                    ch = slab * 9 + j
                    for kc in range(8):
                        k.op("pe", ["aw%d" % b, "condT"], BK[l],
                             lambda e, b=b, j=j, kc=kc, ch=ch, pm=pm: e.matmul(
                                 pm[:, ch * 2:ch * 2 + 2], lhsT=aw[b][:, kc, j * 128:(j + 1) * 128],
                                 rhs=condT[:, kc, :], start=(kc == 0), stop=(kc == 7)))
            for s in range(NSEQ):
                pv = pm[:, 0:144].rearrange("p (c s) -> p c s", s=2)[:, :, s]
                k.op("dve", BK[l] + ["adab"], ["modT"],
                     lambda e, l=l, s=s, pv=pv: e.tensor_tensor(out=modT[:, l, :, s], in0=pv, in1=adab[:, l, :], op=ALU.add))
        for l in range(NL):
            for w in range(3):
                for s in range(NSEQ):
                    sc = modT[:, l, (3 * w + 1) * 8:(3 * w + 2) * 8, s]
                    k.op("dve", ["modT", "gains"], ["Amod"],
                         lambda e, l=l, w=w, s=s, sc=sc: e.scalar_tensor_tensor(
                             out=Amod[:, l, w, s, :], in0=sc, scalar=1.0, in1=gains[:, l, w, :],
                             op0=ALU.add, op1=ALU.mult))

    k.barrier()
    dump("modT", modT[:], [128, NL, 72, NSEQ], F32, ["modT"])
    dump("Amod", Amod[:], [128, NL, 3, NSEQ, 8], F32, ["Amod"])
    if stop == "p0b":
        k.finish()
        es_top.close()
        return nc, k

    def next_ps(lo=2, hi=6):
        i = lo + rot["ps"] % (hi - lo)
        rot["ps"] += 1
        return i

    def cast_eng():
        e = ("act", "dve", "pool")[rot["cast"] % 3]
        rot["cast"] += 1
        return e

    def emit_cast(E, dst, src, reads, writes):
        if E == "act":
            k.op("act", reads, writes, lambda e: e.copy(out=dst, in_=src))
        else:
            k.op(E, reads, writes, lambda e: e.tensor_copy(out=dst, in_=src))

    def gate_bcast(gt, gkey, l, w, s, mult):
        for half in range(2):
            pi = next_ps()
            for q in range(4):
                kc = half * 4 + q
                col = (3 * w + 2) * 8 + kc
                dt_key = "dtmp%d" % (kc % 2)
                dtile = dtmp[kc % 2]
                k.op("dve", ["ident", "modT"], [dt_key],
                     lambda e, dtile=dtile, col=col: e.tensor_scalar(
                         out=dtile[:], in0=ident[:], scalar1=modT[:, l, col, s:s + 1], scalar2=None, op0=ALU.mult))
                k.op("pe", [dt_key, "ones"], [PS[pi]],
                     lambda e, pi=pi, q=q, dtile=dtile: e.matmul(
                         ps[pi][:, q * 128:(q + 1) * 128], lhsT=ones[:], rhs=dtile[:], start=True, stop=True))
            k.op("act", [PS[pi]], [gkey],
                 lambda e, pi=pi, half=half: e.activation(
                     out=gt[:, half * 512:(half + 1) * 512], in_=ps[pi][:], func=AF.Copy, scale=float(mult)))

    dtmp = [sb(es_top, "dtmp%d" % i, [128, 128], F32) for i in range(2)]

    def norm_chunk(hc, hkey, nt, ytiles, xT, xkey, l, w, s, ssq, std, rstd, junk):
        for t in range(nt):
            k.op("act", [hkey], ["junk", "ssq"],
                 lambda e, t=t: e.activation(out=junk[:], in_=hc[:, t, :], func=AF.Square, accum_out=ssq[:, t:t + 1]))
        k.op("act", ["ssq", "epsb"], ["std"],
             lambda e: e.activation(out=std[:, 0:nt], in_=ssq[:, 0:nt], func=AF.Sqrt, bias=epsb[:], scale=1.0 / D))
        k.op("dve", ["std"], ["rstd"], lambda e: e.reciprocal(out=rstd[:, 0:nt], in_=std[:, 0:nt]))
        W = nt * 128
        per_bank = 512 // W
        gsz = 2 * per_bank
        for t in range(nt):
            k.op("dve", [hkey, "rstd"], ["y%d" % t],
                 lambda e, t=t: e.tensor_scalar(out=ytiles[t][:], in0=hc[:, t, :], scalar1=rstd[:, t:t + 1],
                                                scalar2=None, op0=ALU.mult))
        for g0 in range(0, 8, gsz):
            for kc in range(g0, g0 + gsz):
                bank = ((kc - g0) // per_bank) % 2
                if EXPERIMENT == "banks" and g0 > 0:
                    bank += 2
                slot = kc % per_bank
                for t in range(nt):
                    col = slot * W + t * 128
                    k.op("pe", ["y%d" % t, "ident"], [PS[bank]],
                         lambda e, bank=bank, col=col, kc=kc, t=t: e.transpose(
                             out=ps[bank][:, col:col + 128], in_=ytiles[t][:, kc * 128:(kc + 1) * 128],
                             identity=ident[:]))
            for kc in range(g0, g0 + gsz):
                bank = ((kc - g0) // per_bank) % 2
                if EXPERIMENT == "banks" and g0 > 0:
                    bank += 2
                slot = kc % per_bank
                col = slot * W
                k.op("act", [PS[bank], "Amod", "modT"], [xkey],
                     lambda e, bank=bank, col=col, kc=kc: e.activation(
                         out=xT[:, kc, 0:W], in_=ps[bank][:, col:col + W], func=AF.Identity,
                         bias=modT[:, l, 3 * w * 8 + kc, s:s + 1], scale=Amod[:, l, w, s, kc:kc + 1]))


    def ffn_phase(l, w, src, final):
        with ExitStack() as es:
            wgu = sb(es, "wgu", [128, 8, 2 * DFF], BF16)
            wd = sb(es, "wd", [128, 22, D], BF16)
            stg = [sb(es, "stg%d" % i, [128, 1024], F32) for i in range(2)]
            hc = [sb(es, "hc%d" % i, [128, 2, D], F32) for i in range(2)]
            yt = [sb(es, "y%d" % i, [128, D], F32) for i in range(2)]
            xT = sb(es, "xT", [128, 8, 256], BF16)
            hT = sb(es, "hT", [128, 22, 256], BF16)
            sg = [sb(es, "sg%d" % i, [128, 256], F32) for i in range(2)]
            tmp = [sb(es, "tmp%d" % i, [128, 512], F32) for i in range(2)]
            gbc = [sb(es, "gbc%d" % i, [128, D], F32) for i in range(2)]
            junk = sb(es, "junk", [128, D], BF16)
            ssq = sb(es, "ssq", [128, 4], F32)
            std = sb(es, "std", [128, 4], F32)
            rstd = sb(es, "rstd", [128, 4], F32)
            finb = sb(es, "finb", [128, D], F32) if final else None
            if final:
                k.dma("sp", finb[:], fin_in[:, :], [], ["finb"])
            si = 0
            for kc in range(8):
                for c0 in range(0, 2 * DFF, 1024):
                    n = min(1024, 2 * DFF - c0)
                    b = si % 2
                    si += 1
                    k.dma("sp", stg[b][:, 0:n], wgu_in[w][l, kc * 128:(kc + 1) * 128, c0:c0 + n], [], ["stg%d" % b])
                    emit_cast(cast_eng(), wgu[:, kc, c0:c0 + n], stg[b][:, 0:n], ["stg%d" % b], ["wgu"])
            for j in range(22):
                b = si % 2
                si += 1
                k.dma("sp", stg[b][:], wd_in[w][l, j * 128:(j + 1) * 128, :], [], ["stg%d" % b])
                emit_cast(cast_eng(), wd[:, j, :], stg[b][:], ["stg%d" % b], ["wd"])
            for s in range(NSEQ):
                gate_bcast(gbc[s], "gbc%d" % s, l, w, s, 0.5)
            dump("gbc0", gbc[0][:], [128, D], F32, ["gbc0"])
            dump("wgu", wgu[:], [128, 8, 2 * DFF], BF16, ["wgu"])
            ci = 0
            for s in range(NSEQ):
                for c in range(8):
                    b = ci % 2
                    ci += 1
                    hk = "hc%d" % b
                    hsrc = src[s][c * 256:(c + 1) * 256, :].rearrange("(t p) f -> p t f", p=128)
                    k.dma("sp", hc[b][:], hsrc, [("H", s, c // 2)], [hk])
                    norm_chunk(hc[b], hk, 2, yt, xT, "xT", l, w, s, ssq, std, rstd, junk)
                    dump("xT", xT[:], [128, 8, 256], BF16, ["xT"])
                    dump("rstd", rstd[:], [128, 4], F32, ["rstd"])
                    for j in range(22):
                        pg = 2 + (j % 2)
                        pu = 4 + (j % 2)
                        for kc in range(8):
                            k.op("pe", ["wgu", "xT"], [PS[pg]],
                                 lambda e, pg=pg, j=j, kc=kc: e.matmul(
                                     ps[pg][:, 0:256], lhsT=wgu[:, kc, j * 128:(j + 1) * 128], rhs=xT[:, kc, :],
                                     start=(kc == 0), stop=(kc == 7)))
                        for kc in range(8):
                            k.op("pe", ["wgu", "xT"], [PS[pu]],
                                 lambda e, pu=pu, j=j, kc=kc: e.matmul(
                                     ps[pu][:, 0:256], lhsT=wgu[:, kc, DFF + j * 128:DFF + (j + 1) * 128],
                                     rhs=xT[:, kc, :], start=(kc == 0), stop=(kc == 7)))
                        k.op("act", [PS[pg]], ["sg%d" % (j % 2)],
                             lambda e, pg=pg, j=j: e.activation(out=sg[j % 2][:], in_=ps[pg][:, 0:256], func=AF.Silu))
                        k.op("dve", [PS[pu], "sg%d" % (j % 2)], [("hT", j)],
                             lambda e, pu=pu, j=j: e.tensor_tensor(out=hT[:, j, :], in0=ps[pu][:, 0:256],
                                                                   in1=sg[j % 2][:], op=ALU.mult))
                    hT_keys = [("hT", j) for j in range(22)]
                    dump("hT", hT[:], [128, 22, 256], BF16, hT_keys)
                    oi = 0
                    for t in range(2):
                        for half in range(2):
                            py = oi % 2
                            oi += 1
                            pykeys = BK[py]
                            for j in range(22):
                                k.op("pe", ["wd"] + hT_keys, pykeys,
                                     lambda e, py=py, j=j, t=t, half=half: e.matmul(
                                         ps[py][:, :], lhsT=hT[:, j, t * 128:(t + 1) * 128],
                                         rhs=wd[:, j, half * 512:(half + 1) * 512], start=(j == 0), stop=(j == 21)))
                            k.op("dve", pykeys + ["gbc%d" % s], ["tmp%d" % py],
                                 lambda e, py=py, half=half, s=s: e.tensor_tensor(
                                     out=tmp[py][:], in0=ps[py][:, :], in1=gbc[s][:, half * 512:(half + 1) * 512],
                                     op=ALU.mult))
                            k.op("pool", [hk, "tmp%d" % py], [hk],
                                 lambda e, py=py, t=t, half=half, b=b: e.tensor_tensor(
                                     out=hc[b][:, t, half * 512:(half + 1) * 512],
                                     in0=hc[b][:, t, half * 512:(half + 1) * 512], in1=tmp[py][:], op=ALU.add))
                    if not final:
                        hdst = H[s][c * 256:(c + 1) * 256, :].rearrange("(t p) f -> p t f", p=128)
                        k.dma("pool", hdst, hc[b][:], [hk], [("H", s, c // 2)])
                    else:
                        for t in range(2):
                            k.op("act", [hk], ["junk", "ssq"],
                                 lambda e, t=t, b=b: e.activation(out=junk[:], in_=hc[b][:, t, :], func=AF.Square,
                                                                  accum_out=ssq[:, t:t + 1]))
                        k.op("act", ["ssq", "epsb"], ["std"],
                             lambda e: e.activation(out=std[:, 0:2], in_=ssq[:, 0:2], func=AF.Sqrt, bias=epsb[:],
                                                    scale=1.0 / D))
                        k.op("dve", ["std"], ["rstd"], lambda e: e.reciprocal(out=rstd[:, 0:2], in_=std[:, 0:2]))
                        for t in range(2):
                            k.op("dve", [hk, "rstd", "finb"], [hk],
                                 lambda e, t=t, b=b: e.scalar_tensor_tensor(
                                     out=hc[b][:, t, :], in0=hc[b][:, t, :], scalar=rstd[:, t:t + 1], in1=finb[:],
                                     op0=ALU.mult, op1=ALU.mult))
                        odst = out_d[s][c * 256:(c + 1) * 256, :].rearrange("(t p) f -> p t f", p=128)
                        k.dma("pool", odst, hc[b][:], [hk], [("OUT", s, c)])

    def mixproj_phase(l):
        w = 1
        with ExitStack() as es:
            wie = sb(es, "wie", [128, 8, WIE], BF16)
            wuq = sb(es, "wuq", [128, 2, 960], BF16)
            wukv = sb(es, "wukv", [128, 640], BF16)
            stg = [sb(es, "stg%d" % i, [128, 1024], F32) for i in range(2)]
            hc = [sb(es, "hc%d" % i, [128, 4, D], F32) for i in range(2)]
            yt = [sb(es, "y%d" % i, [128, D], F32) for i in range(4)]
            xT = sb(es, "xT", [128, 8, 512], BF16)
            junk = sb(es, "junk", [128, D], BF16)
            ssq = sb(es, "ssq", [128, 4], F32)
            std = sb(es, "std", [128, 4], F32)
            rstd = sb(es, "rstd", [128, 4], F32)
            posi = sb(es, "posi", [128, 512], I32)
            posf = sb(es, "posf", [128, 512], F32)
            u_ = sb(es, "u_", [128, 512], F32)
            ki = sb(es, "ki", [128, 512], I32)
            kf = sb(es, "kf", [128, 512], F32)
            tabs = {n: sb(es, n, [128, 512], F32) for n in ("C128", "S128", "CB", "SB")}
            t1 = [sb(es, "t1_%d" % i, [128, 512], F32) for i in range(2)]
            t2 = [sb(es, "t2_%d" % i, [128, 512], F32) for i in range(2)]
            ost = [sb(es, "ost%d" % i, [128, 512], BF16) for i in range(4)]
            mtile = sb(es, "mtile", [128, 456], F32)
            mn = sb(es, "mn", [128, 512], F32)
            ss3 = sb(es, "ss3", [128, 4], F32)
            sd3 = sb(es, "sd3", [128, 4], F32)
            rs3 = sb(es, "rs3", [128, 4], F32)
            cqnT = sb(es, "cqnT", [128, 2, 512], BF16)
            ckvnT = sb(es, "ckvnT", [128, 512], BF16)
            vst = [sb(es, "vst%d" % i, [128, 6, 66], BF16) for i in range(3)]
            iwst = sb(es, "iwst", [128, 4, 8], F32)
            ikst = sb(es, "ikst", [64, 512], BF16)
            ti = [sb(es, "ti%d" % i, [64, 128], F32) for i in range(2)]
            for i in range(3):
                k.op("pool", [], ["vst%d" % i], lambda e, i=i: e.memset(vst[i][:], 1.0))
            si = 0
            for kc in range(8):
                for c0 in range(0, WIE, 1024):
                    n = min(1024, WIE - c0)
                    b = si % 2
                    si += 1
                    k.dma("sp", stg[b][:, 0:n], wie_in[l, kc * 128:(kc + 1) * 128, c0:c0 + n], [], ["stg%d" % b])
                    emit_cast(cast_eng(), wie[:, kc, c0:c0 + n], stg[b][:, 0:n], ["stg%d" % b], ["wie"])
            for kc in range(2):
                b = si % 2
                si += 1
                k.dma("sp", stg[b][:, 0:960], wuq_in[l, kc * 128:(kc + 1) * 128, :], [], ["stg%d" % b])
                emit_cast(cast_eng(), wuq[:, kc, :], stg[b][:, 0:960], ["stg%d" % b], ["wuq"])
            b = si % 2
            si += 1
            k.dma("sp", stg[b][:, 0:640], wukv_in[l, :, :], [], ["stg%d" % b])
            emit_cast(cast_eng(), wukv[:, :], stg[b][:, 0:640], ["stg%d" % b], ["wukv"])

            ctr = {"o": 0, "t": 0, "v": 0, "e": 0}

            def evac_copy(dst, src, reads, writes, allow_act=True):
                ctr["e"] += 1
                if allow_act and ctr["e"] % 2 == 0:
                    k.op("act", reads, writes, lambda e: e.activation(out=dst, in_=src, func=AF.Copy))
                else:
                    k.op("dve", reads, writes, lambda e: e.tensor_copy(out=dst, in_=src))

            def make_table(name, invcol, addc, sccol, bscol):
                k.op("dve", ["posf", "ropec"], ["u_"],
                     lambda e: e.tensor_scalar(out=u_[:], in0=posf[:], scalar1=ropec[:, invcol:invcol + 1],
                                               scalar2=float(addc), op0=ALU.mult, op1=ALU.add))
                k.op("dve", ["u_"], ["ki"], lambda e: e.tensor_copy(out=ki[:], in_=u_[:]))
                k.op("dve", ["ki"], ["kf"], lambda e: e.tensor_copy(out=kf[:], in_=ki[:]))
                k.op("dve", ["u_", "kf"], ["u_"],
                     lambda e: e.tensor_tensor(out=u_[:], in0=u_[:], in1=kf[:], op=ALU.subtract))
                k.op("dve", ["u_"], ["kf"],
                     lambda e: e.scalar_tensor_tensor(out=kf[:], in0=u_[:], scalar=0.0, in1=u_[:],
                                                      op0=ALU.is_lt, op1=ALU.add))
                k.op("act", ["kf", "ropec"], [name],
                     lambda e: e.activation(out=tabs[name][:], in_=kf[:], func=AF.Sin,
                                            bias=ropec[:, bscol:bscol + 1], scale=ropec[:, sccol:sccol + 1]))

            ci = 0
            for s in range(NSEQ):
                for c in range(4):
                    b = ci % 2
                    ci += 1
                    hk = "hc%d" % b
                    csl = slice(c * 512, (c + 1) * 512)
                    hsrc = H[s][csl, :].rearrange("(t p) f -> p t f", p=128)
                    k.dma("sp", hc[b][:], hsrc, [("H", s, c)], [hk])
                    k.dma("sp", posi[:], pos_in[s][:, csl], [], ["posi"])
                    norm_chunk(hc[b], hk, 4, yt, xT, "xT", l, w, s, ssq, std, rstd, junk)
                    k.op("dve", ["posi"], ["posf"], lambda e: e.tensor_copy(out=posf[:], in_=posi[:]))
                    make_table("C128", 0, 0.25, 6, 7)
                    make_table("S128", 0, 0.0, 2, 3)
                    make_table("CB", 1, 0.25, 6, 7)
                    make_table("SB", 1, 0.0, 4, 5)

                    def fm_proj(col0, M, pi):
                        for kc in range(8):
                            k.op("pe", ["wie", "xT"], [PS[pi]],
                                 lambda e, kc=kc: e.matmul(ps[pi][0:M, :], lhsT=wie[:, kc, col0:col0 + M],
                                                           rhs=xT[:, kc, :], start=(kc == 0), stop=(kc == 7)))

                    plain = []
                    for i in range(3):
                        plain.append((QaT, "QaT", i * 128, C_QA + i * 128, 128))
                    for i in range(3):
                        plain.append((KaT, "KaT", i * 128, C_KA + i * 128, 128))
                    for i, M in enumerate((128, 128, 64)):
                        plain.append((QcT, "QcT", i * 128, C_QC + i * 128, M))
                    for i, M in enumerate((128, 128, 64)):
                        plain.append((KcT, "KcT", i * 128, C_KC + i * 128, M))
                    for (dst, dname, r0, col0, M) in plain:
                        pi = next_ps()
                        fm_proj(col0, M, pi)
                        o = ctr["o"] % 4
                        ctr["o"] += 1
                        evac_copy(ost[o][0:M, :], ps[pi][0:M, :], [PS[pi]], ["ost%d" % o])
                        k.dma("pool", dst[s][r0:r0 + M, csl], ost[o][0:M, :], ["ost%d" % o], [(dname, s, r0, c)])
                    roped = [(IqT, "IqT", i * 128, C_IQ + i * 128, C_IQS + i * 128, 128) for i in range(4)]
                    roped.append((KrT, "KrT", 0, C_KR, C_KRS, 32))
                    for (dst, dname, r0, col0, col1, M) in roped:
                        p1 = next_ps()
                        fm_proj(col0, M, p1)
                        p2 = next_ps()
                        fm_proj(col1, M, p2)
                        tb = ctr["t"] % 2
                        ctr["t"] += 1
                        o = ctr["o"] % 4
                        ctr["o"] += 1
                        k.op("dve", [PS[p1], "C128"], ["t1_%d" % tb],
                             lambda e, p1=p1, tb=tb, M=M: e.tensor_tensor(out=t1[tb][0:M, :], in0=ps[p1][0:M, :],
                                                                          in1=tabs["C128"][0:M, :], op=ALU.mult))
                        k.op("dve", [PS[p2], "S128"], ["t2_%d" % tb],
                             lambda e, p2=p2, tb=tb, M=M: e.tensor_tensor(out=t2[tb][0:M, :], in0=ps[p2][0:M, :],
                                                                          in1=tabs["S128"][0:M, :], op=ALU.mult))
                        k.op("pool", ["t1_%d" % tb, "t2_%d" % tb], ["ost%d" % o],
                             lambda e, tb=tb, o=o, M=M: e.tensor_tensor(out=ost[o][0:M, :], in0=t1[tb][0:M, :],
                                                                        in1=t2[tb][0:M, :], op=ALU.add))
                        k.dma("pool", dst[s][r0:r0 + M, csl], ost[o][0:M, :], ["ost%d" % o], [(dname, s, r0, c)])
                    for t in range(4):
                        r0 = c * 512 + t * 128
                        tsl = slice(t * 128, (t + 1) * 128)
                        for (dst, dname, col0, nh) in ((Va, "Va", C_VA, 6), (Vc, "Vc", C_VC, 5)):
                            pi = next_ps()
                            for kc in range(8):
                                k.op("pe", ["wie", "xT"], [PS[pi]],
                                     lambda e, kc=kc, pi=pi, col0=col0, nh=nh: e.matmul(
                                         ps[pi][:, 0:nh * 64], lhsT=xT[:, kc, tsl], rhs=wie[:, kc, col0:col0 + nh * 64],
                                         start=(kc == 0), stop=(kc == 7)))
                            v = ctr["v"] % 3
                            ctr["v"] += 1
                            evac_copy(vst[v][:, 0:nh, 0:64],
                                      ps[pi][:, 0:nh * 64].rearrange("p (h e) -> p h e", e=64), [PS[pi]], ["vst%d" % v],
                                      allow_act=False)
                            k.dma("pool", dst[s][r0:r0 + 128, :],
                                  vst[v][:, 0:nh, :].rearrange("p h e -> p (h e)"), ["vst%d" % v], [(dname, s, r0)])
                        pi = next_ps()
                        for kc in range(8):
                            k.op("pe", ["wie", "xT"], [PS[pi]],
                                 lambda e, kc=kc, pi=pi: e.matmul(
                                     ps[pi][:, 0:456], lhsT=xT[:, kc, tsl], rhs=wie[:, kc, C_MISC:C_MISC + 456],
                                     start=(kc == 0), stop=(kc == 7)))
                        k.op("act", [PS[pi]], ["mtile"], lambda e, pi=pi: e.copy(out=mtile[:], in_=ps[pi][:, 0:456]))
                        segs = ((0, 256), (256, 384), (384, 448))
                        for j, (a0, a1) in enumerate(segs):
                            k.op("act", ["mtile"], ["junk", "ss3"],
                                 lambda e, j=j, a0=a0, a1=a1: e.activation(
                                     out=junk[:, 0:a1 - a0], in_=mtile[:, a0:a1], func=AF.Square,
                                     accum_out=ss3[:, j:j + 1]))
                        for j, (a0, a1) in enumerate(segs):
                            k.op("act", ["ss3", "epsb"], ["sd3"],
                                 lambda e, j=j, a0=a0, a1=a1: e.activation(
                                     out=sd3[:, j:j + 1], in_=ss3[:, j:j + 1], func=AF.Sqrt, bias=epsb[:],
                                     scale=1.0 / (a1 - a0)))
                        k.op("dve", ["sd3"], ["rs3"], lambda e: e.reciprocal(out=rs3[:, 0:3], in_=sd3[:, 0:3]))
                        for j, (a0, a1) in enumerate(segs):
                            k.op("dve", ["mtile", "rs3"], ["mn"],
                                 lambda e, j=j, a0=a0, a1=a1: e.tensor_scalar(
                                     out=mn[:, a0:a1], in0=mtile[:, a0:a1], scalar1=rs3[:, j:j + 1], scalar2=None,
                                     op0=ALU.mult))
                        for (d0, s0, n) in ((448, 400, 16), (464, 384, 16), (480, 416, 32)):
                            k.op("dve", ["mn"], ["mn"],
                                 lambda e, d0=d0, s0=s0, n=n: e.tensor_copy(out=mn[:, d0:d0 + n], in_=mn[:, s0:s0 + n]))
                        k.op("pool", ["mtile"], ["iwst"],
                             lambda e, t=t: e.tensor_copy(out=iwst[:, t, :], in_=mtile[:, 448:456]))
                        pa = next_ps()
                        pb = next_ps()
                        for j in range(3):
                            k.op("pe", ["mn", "ident"], [PS[pa]],
                                 lambda e, j=j, pa=pa: e.transpose(out=ps[pa][:, j * 128:(j + 1) * 128],
                                                                   in_=mn[:, j * 128:(j + 1) * 128], identity=ident[:]))
                        k.op("pe", ["mn", "ident"], [PS[pa]],
                             lambda e, pa=pa: e.transpose(out=ps[pa][0:64, 384:512], in_=mn[:, 384:448], identity=ident[:]))
                        k.op("pe", ["mn", "ident"], [PS[pb]],
                             lambda e, pb=pb: e.transpose(out=ps[pb][0:64, 0:128], in_=mn[:, 448:512], identity=ident[:]))
                        for j in range(2):
                            k.op("act", [PS[pa], "qn"], ["cqnT"],
                                 lambda e, j=j, pa=pa: e.activation(out=cqnT[:, j, tsl], in_=ps[pa][:, j * 128:(j + 1) * 128],
                                                                    func=AF.Copy, scale=qn[:, l, j:j + 1]))
                        k.op("act", [PS[pa], "kvn"], ["ckvnT"],
                             lambda e, pa=pa: e.activation(out=ckvnT[:, tsl], in_=ps[pa][:, 256:384], func=AF.Copy,
                                                           scale=kvn[:, l:l + 1]))
                        k.op("dve", [PS[pa], "ikn", "C128"], ["ti0"],
                             lambda e, pa=pa: e.scalar_tensor_tensor(
                                 out=ti[0][:], in0=ps[pa][0:64, 384:512], scalar=ikn[:, l, 0:1],
                                 in1=tabs["C128"][0:64, tsl], op0=ALU.mult, op1=ALU.mult))
                        k.op("dve", [PS[pb], "ikn", "S128"], ["ti1"],
                             lambda e, pb=pb: e.scalar_tensor_tensor(
                                 out=ti[1][:], in0=ps[pb][0:64, 0:128], scalar=ikn[:, l, 1:2],
                                 in1=tabs["S128"][0:64, tsl], op0=ALU.mult, op1=ALU.mult))
                        k.op("pool", ["ti0", "ti1"], ["ikst"],
                             lambda e: e.tensor_tensor(out=ikst[:, tsl], in0=ti[0][:], in1=ti[1][:], op=ALU.add))
                    k.dma("pool", IkT[s][:, csl], ikst[:], ["ikst"], [("IkT", s, c)])
                    k.dma("pool", Iw[s][csl, :].rearrange("(t p) h -> p t h", p=128), iwst[:], ["iwst"], [("Iw", s, c)])
                    for h in range(5):
                        p1 = next_ps()
                        p2 = next_ps()
                        for (pp, cbase) in ((p1, 0), (p2, 480)):
                            for kc in range(2):
                                k.op("pe", ["wuq", "cqnT"], [PS[pp]],
                                     lambda e, kc=kc, pp=pp, cbase=cbase, h=h: e.matmul(
                                         ps[pp][0:96, :], lhsT=wuq[:, kc, cbase + h * 96:cbase + (h + 1) * 96],
                                         rhs=cqnT[:, kc, :], start=(kc == 0), stop=(kc == 1)))
                        tb = ctr["t"] % 2
                        ctr["t"] += 1
                        o = ctr["o"] % 4
                        ctr["o"] += 1
                        k.op("dve", [PS[p1], "CB"], ["t1_%d" % tb],
                             lambda e, p1=p1, tb=tb: e.tensor_tensor(out=t1[tb][0:96, :], in0=ps[p1][0:96, :],
                                                                     in1=tabs["CB"][0:96, :], op=ALU.mult))
                        k.op("dve", [PS[p2], "SB"], ["t2_%d" % tb],
                             lambda e, p2=p2, tb=tb: e.tensor_tensor(out=t2[tb][0:96, :], in0=ps[p2][0:96, :],
                                                                     in1=tabs["SB"][0:96, :], op=ALU.mult))
                        k.op("pool", ["t1_%d" % tb, "t2_%d" % tb], ["ost%d" % o],
                             lambda e, tb=tb, o=o: e.tensor_tensor(out=ost[o][0:96, :], in0=t1[tb][0:96, :],
                                                                   in1=t2[tb][0:96, :], op=ALU.add))
                        k.dma("pool", QbT[s][h][:, csl], ost[o][0:96, :], ["ost%d" % o], [("QbT", s, h, c)])
                    for i, M in enumerate((128, 128, 64)):
                        pi = next_ps()
                        k.op("pe", ["wukv", "ckvnT"], [PS[pi]],
                             lambda e, pi=pi, i=i, M=M: e.matmul(ps[pi][0:M, :], lhsT=wukv[:, i * 128:i * 128 + M],
                                                                 rhs=ckvnT[:, :], start=True, stop=True))
                        o = ctr["o"] % 4
                        ctr["o"] += 1
                        evac_copy(ost[o][0:M, :], ps[pi][0:M, :], [PS[pi]], ["ost%d" % o])
                        k.dma("pool", KnT[s][i * 128:i * 128 + M, csl], ost[o][0:M, :], ["ost%d" % o],
                              [("KnT", s, i, c)])
                    for t in range(4):
                        r0 = c * 512 + t * 128
                        tsl = slice(t * 128, (t + 1) * 128)
                        pi = next_ps()
                        k.op("pe", ["wukv", "ckvnT"], [PS[pi]],
                             lambda e, pi=pi, tsl=tsl: e.matmul(ps[pi][:, 0:320], lhsT=ckvnT[:, tsl], rhs=wukv[:, 320:640],
                                                                start=True, stop=True))
                        v = ctr["v"] % 3
                        ctr["v"] += 1
                        evac_copy(vst[v][:, 0:5, 0:64], ps[pi][:, 0:320].rearrange("p (h e) -> p h e", e=64),
                                  [PS[pi]], ["vst%d" % v], allow_act=False)
                        k.dma("pool", Vb[s][r0:r0 + 128, :], vst[v][:, 0:5, :].rearrange("p h e -> p (h e)"),
                              ["vst%d" % v], [("Vb", s, r0)])

    SC_IDX = float(8 ** -0.5 * 64 ** -0.5)

    def att_phase(l, only_heads=None, do_idx=True):
        with ExitStack() as es:
            qt = [sb(es, "qt%d" % i, [128, T], BF16) for i in range(2)]
            kt = [sb(es, "kt%d" % i, [128, T], BF16) for i in range(2)]
            vt = [sb(es, "vt%d" % i, [128, 16, 66], BF16) for i in range(2)]
            gt = [sb(es, "gt%d" % i, [128, XW], BF16) for i in range(2)]
            et = [sb(es, "et%d" % i, [128, 512], BF16) for i in range(2)]
            pt = [sb(es, "pt%d" % i, [128, 512], BF16) for i in range(2)]
            p2 = [sb(es, "p2%d" % i, [128, 512], BF16) for i in range(2)]
            stl = [sb(es, "stl%d" % i, [128, 512], BF16) for i in range(2)]
            ostg = [sb(es, "ostg%d" % i, [128, 4, 64], BF16) for i in range(2)]
            rc = [sb(es, "rc%d" % i, [128, 4], F32) for i in range(2)]
            iq_sb = sb(es, "iq_sb", [128, 4, T], BF16)
            ik2 = sb(es, "ik2", [128, T], BF16)
            iw_sb = sb(es, "iw_sb", [128, 16, 8], F32)
            NZ = 4
            scoreZ = [sb(es, "score%d" % i, [128, T], F32) for i in range(NZ)]
            workZ = [sb(es, "work%d" % i, [128, T], F32) for i in range(NZ)]
            m8Z = [sb(es, "m8_%d" % i, [128, 8], F32) for i in range(NZ)]
            selqZ = [sb(es, "selq%d" % i, [128, T], BF16) for i in range(NZ)]
            rr = [sb(es, "rr%d" % i, [128, 512], F32) for i in range(2)]
            m8 = sb(es, "m8", [128, 8], F32)
            selq = sb(es, "selq", [128, T], BF16)
            sstg = [sb(es, "sstg%d" % i, [128, 512], BF16) for i in range(2)]
            cn = {"h": 0, "i": 0, "t": 0, "o": 0}

            def idx_phase(s):
                k.dma("sp", iq_sb[:], IqT[s].rearrange("(j p) t -> p j t", p=128), [], ["iq_sb"])
                k.dma("sp", ik2[0:64, :], IkT[s], [], ["ik2"])
                k.dma("sp", ik2[64:128, :], IkT[s], [], ["ik2b"])
                k.dma("sp", iw_sb[:], Iw[s].rearrange("(t p) h -> p t h", p=128), [], ["iw_sb"])

                def scores(i, z):
                    sc, sk_ = scoreZ[z], "score%d" % z
                    nk = (i + 1) * 128
                    for kc in range((nk + 511) // 512):
                        n = min(512, nk - kc * 512)
                        ksl = slice(kc * 512, kc * 512 + n)
                        for h8 in range(8):
                            x = cn["i"] % 2
                            cn["i"] += 1
                            pview = psb[x][:, :].bitcast(F32)
                            r0 = (h8 % 2) * 64
                            k.op("pe", ["iq_sb", "ik2", "ik2b"], [PSB[x]],
                                 lambda e, pview=pview, r0=r0, h8=h8, n=n, ksl=ksl: e.matmul(
                                     pview[:, 0:n], lhsT=iq_sb[r0:r0 + 64, h8 // 2, i * 128:(i + 1) * 128],
                                     rhs=ik2[r0:r0 + 64, ksl], start=True, stop=True))
                            k.op("act", [PSB[x]], ["rr%d" % x],
                                 lambda e, pview=pview, x=x, n=n: e.activation(out=rr[x][:, 0:n], in_=pview[:, 0:n],
                                                                               func=AF.Relu, scale=SC_IDX))
                            if h8 == 0:
                                k.op("dve", ["rr%d" % x, "iw_sb"], [sk_],
                                     lambda e, x=x, n=n, ksl=ksl: e.tensor_scalar(
                                         out=sc[:, ksl], in0=rr[x][:, 0:n], scalar1=iw_sb[:, i, 0:1], scalar2=None,
                                         op0=ALU.mult))
                            else:
                                k.op("dve", ["rr%d" % x, "iw_sb", sk_], [sk_],
                                     lambda e, x=x, n=n, ksl=ksl, h8=h8: e.scalar_tensor_tensor(
                                         out=sc[:, ksl], in0=rr[x][:, 0:n], scalar=iw_sb[:, i, h8:h8 + 1],
                                         in1=sc[:, ksl], op0=ALU.mult, op1=ALU.add))
                    dsl = slice(i * 128, (i + 1) * 128)
                    k.op("dve", [sk_, "negm"], [sk_],
                         lambda e, dsl=dsl: e.tensor_tensor(out=sc[:, dsl], in0=sc[:, dsl], in1=negm[:], op=ALU.add))

                def select_and_store(i, z, use_thr):
                    sc, sk_ = scoreZ[z], "score%d" % z
                    sq, qk_ = selqZ[z], "selq%d" % z
                    nk = (i + 1) * 128
                    if use_thr:
                        k.op("dve", [sk_, "m8_%d" % z], [qk_],
                             lambda e: e.tensor_scalar(out=sq[:, 0:nk], in0=sc[:, 0:nk], scalar1=m8Z[z][:, 7:8],
                                                       scalar2=None, op0=ALU.is_ge))
                    else:
                        k.op("dve", [sk_], [qk_],
                             lambda e: e.tensor_scalar(out=sq[:, 0:nk], in0=sc[:, 0:nk], scalar1=-1.0e29,
                                                       scalar2=None, op0=ALU.is_ge))
                    for kb0 in range(0, i + 1, 4):
                        nb = min(4, i + 1 - kb0)
                        x = cn["t"] % 2
                        cn["t"] += 1
                        for j in range(nb):
                            k.op("pe", [qk_, "identb"], [PSB[x]],
                                 lambda e, x=x, j=j, kb0=kb0: e.transpose(
                                     out=psb[x][:, j * 128:(j + 1) * 128], in_=sq[:, (kb0 + j) * 128:(kb0 + j + 1) * 128],
                                     identity=identb[:]))
                        k.op("act", [PSB[x]], ["sstg%d" % x],
                             lambda e, x=x, nb=nb: e.activation(out=sstg[x][:, 0:nb * 128], in_=psb[x][:, 0:nb * 128],
                                                                func=AF.Copy))
                        dst = SelT[s][kb0 * 128:(kb0 + nb) * 128, i * 128:(i + 1) * 128].rearrange("(k p) q -> p k q", p=128)
                        k.dma("pool", dst, sstg[x][:, 0:nb * 128].rearrange("p (k q) -> p k q", q=128),
                              ["sstg%d" % x], [("SelT", s, i, kb0 // 4)])

                for i in (0, 1):
                    scores(i, i)
                    select_and_store(i, i, False)
                    yield
                for i0 in range(2, 16, NZ):
                    zs = list(range(min(NZ, 16 - i0)))
                    for z in zs:
                        nk = (i0 + z + 1) * 128
                        scores(i0 + z, z)
                        k.op("pool", ["score%d" % z], ["work%d" % z],
                             lambda e, z=z, nk=nk: e.tensor_copy(out=workZ[z][:, 0:nk], in_=scoreZ[z][:, 0:nk]))
                        yield
                    for r in range(32):
                        for z in zs:
                            nk = (i0 + z + 1) * 128
                            k.op("dve", ["work%d" % z], ["m8_%d" % z],
                                 lambda e, z=z, nk=nk: e.max(out=m8Z[z][:], in_=workZ[z][:, 0:nk]))
                        yield
                        if r < 31:
                            for z in zs:
                                nk = (i0 + z + 1) * 128
                                k.op("dve", ["work%d" % z, "m8_%d" % z], ["work%d" % z],
                                     lambda e, z=z, nk=nk: e.match_replace(
                                         out=workZ[z][:, 0:nk], in_to_replace=m8Z[z][:], in_values=workZ[z][:, 0:nk],
                                         imm_value=NEG))
                            yield
                    for z in zs:
                        select_and_store(i0 + z, z, True)
                        yield

            def head_attn(s, head):
                b = cn["h"] % 2
                cn["h"] += 1
                if head < 6:
                    d, hh = 64, head
                    qsrc = [(QaT[s][hh * 64:(hh + 1) * 64, :], 0, 64)]
                    ksrc = [(KaT[s][hh * 64:(hh + 1) * 64, :], 0, 64)]
                    vsrc = Va[s].rearrange("(t p) (h e) -> p t h e", p=128, e=66)[:, :, hh, :]
                    gidx, maskall, sel = hh, True, False
                elif head < 11:
                    d, hh = 96, head - 6
                    qsrc = [(QbT[s][hh], 0, 96)]
                    ksrc = [(KnT[s][hh * 64:(hh + 1) * 64, :], 0, 64), (KrT[s], 64, 96)]
                    vsrc = Vb[s].rearrange("(t p) (h e) -> p t h e", p=128, e=66)[:, :, hh, :]
                    gidx, maskall, sel = 11, False, False
                else:
                    d, hh = 64, head - 11
                    qsrc = [(QcT[s][hh * 64:(hh + 1) * 64, :], 0, 64)]
                    ksrc = [(KcT[s][hh * 64:(hh + 1) * 64, :], 0, 64)]
                    vsrc = Vc[s].rearrange("(t p) (h e) -> p t h e", p=128, e=66)[:, :, hh, :]
                    gidx, maskall, sel = 6 + hh, True, True
                scale = float(d ** -0.5)
                qk, kk_, vk, gk = "qt%d" % b, "kt%d" % b, "vt%d" % b, "gt%d" % b
                for (src, p0, p1) in qsrc:
                    k.dma("sp", qt[b][p0:p1, :], src, [], [qk])
                kkeys = []
                for j, (src, p0, p1) in enumerate(ksrc):
                    kkeys.append(kk_ + "_%d" % j)
                    k.dma("sp", kt[b][p0:p1, :], src, [], [kk_ + "_%d" % j])
                k.dma("sp", vt[b][:], vsrc, [], [vk])
                k.dma("sp", gt[b][:], Gs[gidx], [], [gk])
                for c in range(4):
                    nkb = 4 * c + 4
                    qsl = slice(c * 512, (c + 1) * 512)

                    def emit_S(kb):
                        pS = 4 + kb % 2
                        k.op("pe", [qk] + kkeys, [PS[pS]],
                             lambda e, pS=pS, kb=kb: e.matmul(ps[pS][:, :], lhsT=kt[b][0:d, kb * 128:(kb + 1) * 128],
                                                              rhs=qt[b][0:d, qsl], start=True, stop=True))

                    emit_S(0)
                    for kb in range(nkb):
                        if kb + 1 < nkb:
                            emit_S(kb + 1)
                        pS = 4 + kb % 2
                        eb = kb % 2
                        k.op("act", [PS[pS]], ["et%d" % eb],
                             lambda e, pS=pS, eb=eb: e.activation(out=et[eb][:], in_=ps[pS][:, :], func=AF.Exp, scale=scale))
                        cur, curk = et[eb], "et%d" % eb
                        if sel:
                            x0 = c * 512 - kb * 128 + OFF
                            skeys = [("SelT", s, 4 * c + q, kb // 4) for q in range(4) if kb <= 4 * c + q]
                            k.dma("sp", stl[eb][:], SelT[s][kb * 128:(kb + 1) * 128, qsl], skeys, ["stl%d" % eb])
                            k.op("dve", [curk, gk], ["pt%d" % eb],
                                 lambda e, eb=eb, x0=x0, cur=cur: e.tensor_tensor(out=pt[eb][:], in0=cur[:],
                                                                                  in1=gt[b][:, x0:x0 + 512], op=ALU.mult))
                            k.op("dve", ["pt%d" % eb, "stl%d" % eb], ["p2%d" % eb],
                                 lambda e, eb=eb: e.tensor_tensor(out=p2[eb][:], in0=pt[eb][:], in1=stl[eb][:],
                                                                  op=ALU.mult))
                            cur, curk = p2[eb], "p2%d" % eb
                        elif maskall or kb >= 4 * c:
                            x0 = c * 512 - kb * 128 + OFF
                            k.op("dve", [curk, gk], ["pt%d" % eb],
                                 lambda e, eb=eb, x0=x0, cur=cur: e.tensor_tensor(out=pt[eb][:], in0=cur[:],
                                                                                  in1=gt[b][:, x0:x0 + 512], op=ALU.mult))
                            cur, curk = pt[eb], "pt%d" % eb
                        for q in range(4):
                            i = 4 * c + q
                            if kb <= i:
                                k.op("pe", [curk, vk], [PS[q]],
                                     lambda e, q=q, kb=kb, i=i, cur=cur: e.matmul(
                                         ps[q][:, 0:65], lhsT=cur[:, q * 128:(q + 1) * 128], rhs=vt[b][:, kb, 0:65],
                                         start=(kb == 0), stop=(kb == i)))
                        yield
                    ob = cn["o"] % 2
                    cn["o"] += 1
                    for q in range(4):
                        k.op("dve", [PS[q]], ["rc%d" % ob],
                             lambda e, q=q, ob=ob: e.reciprocal(out=rc[ob][:, q:q + 1], in_=ps[q][:, 64:65]))
                        k.op("dve", [PS[q], "rc%d" % ob], ["ostg%d" % ob],
                             lambda e, q=q, ob=ob: e.tensor_scalar(out=ostg[ob][:, q, :], in0=ps[q][:, 0:64],
                                                                   scalar1=rc[ob][:, q:q + 1], scalar2=None, op0=ALU.mult))
                    dst = Oall[s][qsl, head * 64:(head + 1) * 64].rearrange("(t p) e -> p t e", p=128)
                    k.dma("pool", dst, ostg[ob][:], ["ostg%d" % ob], [("Oall", s, c, head)])

            def drain(g):
                for _ in g:
                    pass

            gis = [idx_phase(s) if do_idx else iter(()) for s in range(NSEQ)]
            for s in range(NSEQ):
                for head in range(16):
                    if only_heads is not None and head not in only_heads:
                        continue
                    if head >= 11:
                        drain(gis[s])
                    for _ in head_attn(s, head):
                        if head < 11:
                            next(gis[s], None)
                        elif s + 1 < NSEQ:
                            next(gis[s + 1], None)
                drain(gis[s])

    def outproj_phase(l):
        with ExitStack() as es:
            wout = sb(es, "wout", [128, 8, D], BF16)
            stg = [sb(es, "stg%d" % i, [128, 1024], F32) for i in range(2)]
            gbc = [sb(es, "gbc%d" % i, [128, D], F32) for i in range(2)]
            oc = [sb(es, "oc%d" % i, [128, 4, D], BF16) for i in range(2)]
            hc = [sb(es, "hc%d" % i, [128, 4, D], F32) for i in range(2)]
            oT = sb(es, "oT", [128, 8, 512], BF16)
            tmp = [sb(es, "tmp%d" % i, [128, 512], F32) for i in range(2)]
            for kc in range(8):
                b = kc % 2
                k.dma("sp", stg[b][:], wout_in[l, kc * 128:(kc + 1) * 128, :], [], ["stg%d" % b])
                emit_cast(cast_eng(), wout[:, kc, :], stg[b][:], ["stg%d" % b], ["wout"])
            for s in range(NSEQ):
                gate_bcast(gbc[s], "gbc%d" % s, l, 1, s, 1.0)
            ci = 0
            ti_ = 0
            for s in range(NSEQ):
                for c in range(4):
                    b = ci % 2
                    ci += 1
                    csl = slice(c * 512, (c + 1) * 512)
                    k.dma("sp", oc[b][:], Oall[s][csl, :].rearrange("(t p) f -> p t f", p=128), [], ["oc%d" % b])
                    k.dma("sp", hc[b][:], H[s][csl, :].rearrange("(t p) f -> p t f", p=128), [("H", s, c)], ["hc%d" % b])
                    for kc in range(8):
                        x = kc % 2
                        for t in range(4):
                            k.op("pe", ["oc%d" % b, "identb"], [PSB[x]],
                                 lambda e, x=x, t=t, kc=kc, b=b: e.transpose(
                                     out=psb[x][:, t * 128:(t + 1) * 128], in_=oc[b][:, t, kc * 128:(kc + 1) * 128],
                                     identity=identb[:]))
                        if kc % 2 == 0:
                            k.op("act", [PSB[x]], ["oT"],
                                 lambda e, x=x, kc=kc: e.activation(out=oT[:, kc, :], in_=psb[x][:, 0:512], func=AF.Copy))
                        else:
                            k.op("dve", [PSB[x]], ["oT"],
                                 lambda e, x=x, kc=kc: e.tensor_copy(out=oT[:, kc, :], in_=psb[x][:, 0:512]))
                    for t in range(4):
                        for half in range(2):
                            pi = next_ps()
                            for kc in range(8):
                                k.op("pe", ["oT", "wout"], [PS[pi]],
                                     lambda e, pi=pi, kc=kc, t=t, half=half: e.matmul(
                                         ps[pi][:, :], lhsT=oT[:, kc, t * 128:(t + 1) * 128],
                                         rhs=wout[:, kc, half * 512:(half + 1) * 512], start=(kc == 0), stop=(kc == 7)))
                            x = ti_ % 2
                            ti_ += 1
                            k.op("dve", [PS[pi], "gbc%d" % s], ["tmp%d" % x],
                                 lambda e, pi=pi, x=x, half=half, s=s: e.tensor_tensor(
                                     out=tmp[x][:], in0=ps[pi][:, :], in1=gbc[s][:, half * 512:(half + 1) * 512], op=ALU.mult))
                            k.op("pool", ["hc%d" % b, "tmp%d" % x], ["hc%d" % b],
                                 lambda e, x=x, t=t, half=half, b=b: e.tensor_tensor(
                                     out=hc[b][:, t, half * 512:(half + 1) * 512],
                                     in0=hc[b][:, t, half * 512:(half + 1) * 512], in1=tmp[x][:], op=ALU.add))
                    k.dma("pool", H[s][csl, :].rearrange("(t p) f -> p t f", p=128), hc[b][:], ["hc%d" % b], [("H", s, c)])

    stops = {"ffn1": 1, "mixproj": 2, "att": 3, "outproj": 4, "ffn2": 5}

    def done():
        k.finish()
        es_top.close()
        return nc, k

    for l in range(NL):
        ffn_phase(l, 0, x_in if l == 0 else H, False)
        k.barrier()
        if stop == "ffn1":
            return done()
        mixproj_phase(l)
        k.barrier()
        if stop == "mixproj":
            return done()
        if stop in ("attA", "attB", "attC"):
            heads = {"attA": [0], "attB": [6], "attC": [11]}[stop]
            att_phase(l, only_heads=heads, do_idx=(stop == "attC"))
            return done()
        att_phase(l)
        k.barrier()
        if stop == "att":
            return done()
        outproj_phase(l)
        k.barrier()
        if stop == "outproj":
            return done()
        ffn_phase(l, 2, H, l == NL - 1)
        k.barrier()
        if stop == "ffn2":
            return done()
    return done()


def host_prep(inputs):
    f32 = np.float32
    x = np.asarray(inputs["x"], f32)
    c = np.asarray(inputs["c"], f32)
    pos = np.asarray(inputs["positions"], np.int32)
    rel_bias = np.asarray(inputs["rel_bias"], f32)
    w_in = np.asarray(inputs["w_in"], f32)
    offs = np.cumsum([0, 384, 384, 384, 256, 128, 32, 320, 320, 320, 512, 64, 8])
    qa, ka, va, cq, ckv, kr, qc, kc_, vc, iq, ik, iw = [w_in[:, :, offs[i]:offs[i + 1]] for i in range(12)]
    iq_h = iq.reshape(NL, D, 8, 64)
    iq_sw = np.concatenate([iq_h[..., 16:32], iq_h[..., 0:16], iq_h[..., 32:64]], axis=-1).reshape(NL, D, 512)
    kr_sw = np.concatenate([kr[..., 16:32], kr[..., 0:16]], axis=-1)
    w_in_ext = np.ascontiguousarray(np.concatenate(
        [qa, ka, qc, kc_, iq, iq_sw, kr, kr_sw, va, vc, cq, ckv, ik, iw], axis=-1))
    assert w_in_ext.shape[-1] == WIE
    w_uq = np.asarray(inputs["mla_w_uq"], f32).reshape(NL, 256, 5, 96)
    w_uq_sw = np.concatenate([w_uq[..., 0:64], w_uq[..., 80:96], w_uq[..., 64:80]], axis=-1)
    w_uq2 = np.ascontiguousarray(np.concatenate([w_uq.reshape(NL, 256, 480), w_uq_sw.reshape(NL, 256, 480)], axis=-1))
    w_ukv = np.asarray(inputs["mla_w_ukv"], f32).reshape(NL, 128, 5, 128)
    w_ukv2 = np.ascontiguousarray(np.concatenate(
        [w_ukv[..., 0:64].reshape(NL, 128, 320), w_ukv[..., 64:128].reshape(NL, 128, 320)], axis=-1))
    gains = np.stack([np.asarray(inputs[n], f32) for n in ("norm_ffn1", "norm_mix", "norm_ffn2")], axis=1)
    gainsT = np.ascontiguousarray(gains.reshape(NL, 3, 8, 128).transpose(3, 0, 1, 2))
    fin_bc = np.ascontiguousarray(np.broadcast_to(np.asarray(inputs["final_norm"], f32)[None, :], (128, D)))
    ada_bT = np.ascontiguousarray(np.asarray(inputs["ada_b"], f32).reshape(NL, 72, 128).transpose(2, 0, 1))
    qnT = np.ascontiguousarray(np.asarray(inputs["mla_q_norm"], f32).reshape(NL, 2, 128).transpose(2, 0, 1))
    kvnT = np.ascontiguousarray(np.asarray(inputs["mla_kv_norm"], f32).T)
    ikn = np.asarray(inputs["idx_k_norm"], f32)
    ikn_sw = np.concatenate([ikn[:, 16:32], ikn[:, 0:16], ikn[:, 32:64]], axis=-1)
    iknT = np.ascontiguousarray(np.stack([ikn, ikn_sw], axis=-1).transpose(1, 0, 2))
    xs = np.arange(XW)[None, :]
    pp = np.arange(128)[:, None]
    delta = xs - pp - OFF
    bucket = t5_bucket_np(delta)
    bstrip = np.ascontiguousarray(rel_bias[bucket, :].transpose(2, 0, 1))
    dpos = delta >= 0
    cntA = (dpos & (delta <= 128)).astype(f32) + (dpos & (delta % 4 == 0) & (delta <= 512)).astype(f32) \
        + (dpos & (delta % 16 == 0) & (delta <= 2048)).astype(f32)
    cntC = dpos.astype(f32)
    cnts = np.ascontiguousarray(np.stack([cntA, cntC], axis=0))
    inv = (10000.0 ** (-np.arange(16, dtype=np.float64) / 16.0))
    p = np.arange(128)
    ropec = np.zeros((128, 8), f32)
    r64 = p % 64
    isx1 = r64 < 16
    isx2 = (r64 >= 16) & (r64 < 32)
    ropec[:, 0] = np.where(isx1 | isx2, inv[p % 16] / TWO_PI, 0.0)
    sgnA = np.where(isx1, -1.0, np.where(isx2, 1.0, 1.0))
    ropec[:, 2] = -TWO_PI * SHR * sgnA
    ropec[:, 3] = np.pi * SHR * sgnA
    bx1 = (p >= 64) & (p < 80)
    bx2 = (p >= 80) & (p < 96)
    ropec[:, 1] = np.where(bx1 | bx2, inv[p % 16] / TWO_PI, 0.0)
    sgnB = np.where(bx1, -1.0, 1.0)
    ropec[:, 4] = -TWO_PI * SHR * sgnB
    ropec[:, 5] = np.pi * SHR * sgnB
    ropec[:, 6] = -TWO_PI * SHR
    ropec[:, 7] = np.pi * SHR
    ident = np.eye(128, dtype=f32)
    qq = np.arange(128)[:, None]
    kk = np.arange(128)[None, :]
    negmask = np.where(kk <= qq, 0.0, NEG).astype(f32)
    shared = {
        "ada_w": np.asarray(inputs["ada_w"], f32), "ada_bT": ada_bT, "gainsT": gainsT, "fin_bc": fin_bc,
        "wgu1": np.asarray(inputs["ffn1_w_gu"], f32), "wd1": np.asarray(inputs["ffn1_w_down"], f32),
        "wgu2": np.asarray(inputs["ffn2_w_gu"], f32), "wd2": np.asarray(inputs["ffn2_w_down"], f32),
        "w_in_ext": w_in_ext, "w_uq2": w_uq2, "w_ukv2": w_ukv2, "qnT": qnT, "kvnT": kvnT, "iknT": iknT,
        "w_out": np.asarray(inputs["w_out"], f32), "bstrip": bstrip, "cnts": cnts, "ropec": ropec,
        "ident": ident, "negmask": negmask,
    }
    in_maps = []
    for r in range(8):
        b0 = 2 * r
        m = dict(shared)
        m["x"] = np.ascontiguousarray(x[b0:b0 + 2])
        m["cT"] = np.ascontiguousarray(c[b0:b0 + 2].reshape(2, 8, 128).transpose(2, 1, 0))
        m["posr"] = np.ascontiguousarray(np.broadcast_to(pos[b0:b0 + 2][:, None, :], (2, 128, T)))
        in_maps.append(m)
    return in_maps


def kernel(**inputs):
    in_maps = host_prep(inputs)
    nc, _ = build()
    res = run_bass_kernel_spmd(nc, in_maps, core_ids=list(range(8)))
    out = np.concatenate([np.asarray(r["out"]) for r in res.results], axis=0)
    return out.astype(np.float32)
```

```python
import numpy as np
from contextlib import ExitStack
import concourse.bass as bass
import concourse.mybir as mybir
from concourse.bass_utils import run_bass_kernel_spmd

F32 = mybir.dt.float32
BF16 = mybir.dt.bfloat16
I32 = mybir.dt.int32
AF = mybir.ActivationFunctionType
ALU = mybir.AluOpType

T = 2048
D = 1024
DFF = 2816
NL = 2
NSEQ = 2
XW = 2432
OFF = 384
NEG = -1.0e30
NDS = 24
WIE = 3656
C_QA, C_KA, C_QC, C_KC, C_IQ, C_IQS, C_KR, C_KRS, C_VA, C_VC, C_MISC = (
    0, 384, 768, 1088, 1408, 1920, 2432, 2464, 2496, 2880, 3200)
EPS = 1e-6
TWO_PI = 2.0 * np.pi
SHR = 0.999999


class K:
    def __init__(self, nc):
        self.nc = nc
        self.eng = {"pe": nc.tensor, "act": nc.scalar, "dve": nc.vector, "pool": nc.gpsimd, "sp": nc.sync}
        self.semh = {}
        self.cnt = {}
        for e in self.eng:
            self.semh[e] = nc.alloc_semaphore("s_" + e)
            self.cnt[e] = 0
        for i in range(NDS):
            self.semh["d%d" % i] = nc.alloc_semaphore("sd%d" % i)
            self.cnt["d%d" % i] = 0
        self.dnext = 0
        self.known = {e: {} for e in self.eng}
        self.writers = {}
        self.readers = {}
        self.n_ins = 0
        self.n_ops = 0
        self.limit = None
        self.floor = {}
        self.log = None

    def barrier(self):
        self.floor = {sk: v for sk, v in self.cnt.items() if v > 0}

    def _deps(self, E, reads, writes):
        deps = {sk: v for sk, v in self.floor.items() if not (E == "pe" and sk == "pe")}

        def need(d):
            for sk, v in d.items():
                if E == "pe" and sk == "pe":
                    continue
                if deps.get(sk, 0) < v:
                    deps[sk] = v

        for k in reads:
            need(self.writers.get(k, {}))
            if isinstance(k, str) and k.startswith("ps"):
                need({sk: v for sk, v in self.readers.get(k, {}).items() if sk != E})
        for k in writes:
            need(self.writers.get(k, {}))
            need(self.readers.get(k, {}))
        return deps

    def _wait(self, E, deps):
        kn = self.known[E]
        for sk, v in deps.items():
            if kn.get(sk, 0) >= v:
                continue
            self.eng[E].wait_ge(self.semh[sk], v)
            kn[sk] = v
            self.n_ins += 1

    def _record(self, me, reads, writes):
        sk, v = me
        for k in writes:
            self.writers[k] = {sk: v}
            self.readers[k] = {}
        for k in reads:
            if k in writes:
                continue
            r = self.readers.setdefault(k, {})
            r[sk] = v

    def op(self, E, reads, writes, fn):
        self.n_ops += 1
        if self.log is not None:
            self.log.append((self.n_ops, E, reads, writes))
        if self.limit is not None and self.n_ops > self.limit:
            return
        self._wait(E, self._deps(E, reads, writes))
        ins = fn(self.eng[E])
        self.cnt[E] += 1
        ins.then_inc(self.semh[E], 1)
        self.n_ins += 1
        self._record((E, self.cnt[E]), reads, writes)

    def dma(self, Q, out, in_, reads, writes):
        self.n_ops += 1
        if self.log is not None:
            self.log.append((self.n_ops, "dma-" + Q, reads, writes))
        if self.limit is not None and self.n_ops > self.limit:
            return
        k = self.dnext
        self.dnext = (k + 1) % NDS
        sk = "d%d" % k
        deps = self._deps(Q, reads, writes)
        if self.cnt[sk] > 0 and deps.get(sk, 0) < self.cnt[sk]:
            deps[sk] = self.cnt[sk]
        self._wait(Q, deps)
        ins = self.eng[Q].dma_start(out=out, in_=in_)
        self.cnt[sk] += 16
        ins.then_inc(self.semh[sk], 16)
        self.n_ins += 1
        self._record((sk, self.cnt[sk]), reads, writes)

    def finish(self):
        for E in self.eng:
            deps = {}
            for sk, v in self.cnt.items():
                if v > 0:
                    deps[sk] = v
            self._wait(E, deps)


def t5_bucket_np(dist):
    n = np.maximum(dist, 0)
    nf = np.maximum(n, 1).astype(np.float32)
    large = 16 + (np.log(nf / np.float32(16)) / np.float32(np.log(128 / 16)) * np.float32(16)).astype(np.int32)
    large = np.minimum(large, 31)
    return np.where(n < 16, n, large)


EXPERIMENT = None
LOG = False


def build(stop=None, dbg_out=(), limit=None):
    nc = bass.Bass("TRN2", target_bir_lowering=False)

    def din(name, shape, dt=F32):
        return nc.dram_tensor(name, list(shape), dt, kind="ExternalInput").ap()

    def dscr(name, shape, dt):
        kind = "ExternalOutput" if name in dbg_out else "Internal"
        return nc.dram_tensor(name, list(shape), dt, kind=kind).ap()

    x_in = din("x", [NSEQ, T, D])
    cT_in = din("cT", [128, 8, NSEQ])
    pos_in = din("posr", [NSEQ, 128, T], I32)
    adaw_in = din("ada_w", [NL, D, 9 * D])
    adab_in = din("ada_bT", [128, NL, 72])
    gains_in = din("gainsT", [128, NL, 3, 8])
    fin_in = din("fin_bc", [128, D])
    wgu_in = [din("wgu1", [NL, D, 2 * DFF]), None, din("wgu2", [NL, D, 2 * DFF])]
    wd_in = [din("wd1", [NL, DFF, D]), None, din("wd2", [NL, DFF, D])]
    wie_in = din("w_in_ext", [NL, D, WIE])
    wuq_in = din("w_uq2", [NL, 256, 960])
    wukv_in = din("w_ukv2", [NL, 128, 640])
    qn_in = din("qnT", [128, NL, 2])
    kvn_in = din("kvnT", [128, NL])
    ikn_in = din("iknT", [64, NL, 2])
    wout_in = din("w_out", [NL, D, D])
    bstrip_in = din("bstrip", [11, 128, XW])
    cnts_in = din("cnts", [2, 128, XW])
    ropec_in = din("ropec", [128, 8])
    ident_in = din("ident", [128, 128])
    negm_in = din("negmask", [128, 128])
    out_d = nc.dram_tensor("out", [NSEQ, T, D], F32, kind="ExternalOutput").ap()

    H = dscr("H", [NSEQ, T, D], F32)
    QaT = dscr("QaT", [NSEQ, 384, T], BF16)
    KaT = dscr("KaT", [NSEQ, 384, T], BF16)
    QcT = dscr("QcT", [NSEQ, 320, T], BF16)
    KcT = dscr("KcT", [NSEQ, 320, T], BF16)
    IqT = dscr("IqT", [NSEQ, 512, T], BF16)
    IkT = dscr("IkT", [NSEQ, 64, T], BF16)
    KrT = dscr("KrT", [NSEQ, 32, T], BF16)
    Va = dscr("Va", [NSEQ, T, 6 * 66], BF16)
    Vc = dscr("Vc", [NSEQ, T, 5 * 66], BF16)
    Vb = dscr("Vb", [NSEQ, T, 5 * 66], BF16)
    Iw = dscr("Iw", [NSEQ, T, 8], F32)
    QbT = dscr("QbT", [NSEQ, 5, 96, T], BF16)
    KnT = dscr("KnT", [NSEQ, 320, T], BF16)
    Oall = dscr("Oall", [NSEQ, T, D], BF16)
    SelT = dscr("SelT", [NSEQ, T, T], BF16)
    Gs = dscr("Gs", [12, 128, XW], BF16)

    k = K(nc)
    k.log = [] if LOG else None
    k.limit = limit
    es_top = ExitStack()

    uniq = [0]

    def sb(es, name, shape, dt):
        uniq[0] += 1
        return es.enter_context(nc.sbuf_tensor("t%d_%s" % (uniq[0], name), list(shape), dt))

    ps = [es_top.enter_context(nc.psum_tensor("ps%d" % i, [128, 512], F32)) for i in range(6)]
    psb = [es_top.enter_context(nc.psum_tensor("psb%d" % i, [128, 1024], BF16)) for i in range(2)]
    PS = ["ps%d" % i for i in range(6)]
    PSB = ["psb0", "psb1"]
    BK = [[p] for p in PS]

    ident = sb(es_top, "ident", [128, 128], F32)
    identb = sb(es_top, "identb", [128, 128], BF16)
    ones = sb(es_top, "ones", [128, 128], F32)
    negm = sb(es_top, "negm", [128, 128], F32)
    modT = sb(es_top, "modT", [128, NL, 72, NSEQ], F32)
    Amod = sb(es_top, "Amod", [128, NL, 3, NSEQ, 8], F32)
    gains = sb(es_top, "gains", [128, NL, 3, 8], F32)
    ropec = sb(es_top, "ropec", [128, 8], F32)
    qn = sb(es_top, "qn", [128, NL, 2], F32)
    kvn = sb(es_top, "kvn", [128, NL], F32)
    ikn = sb(es_top, "ikn", [64, NL, 2], F32)
    epsb = sb(es_top, "epsb", [128, 1], F32)

    k.dma("sp", ident[:], ident_in[:, :], [], ["ident"])
    k.dma("sp", negm[:], negm_in[:, :], [], ["negm"])
    k.dma("sp", gains[:], gains_in[:, :, :, :], [], ["gains"])
    k.dma("sp", ropec[:], ropec_in[:, :], [], ["ropec"])
    k.dma("sp", qn[:], qn_in[:, :, :], [], ["qn"])
    k.dma("sp", kvn[:], kvn_in[:, :], [], ["kvn"])
    k.dma("sp", ikn[:], ikn_in[:, :, :], [], ["ikn"])
    k.op("dve", ["ident"], ["identb"], lambda e: e.tensor_copy(out=identb[:], in_=ident[:]))
    k.op("dve", [], ["ones"], lambda e: e.memset(ones[:], 1.0))
    k.op("dve", [], ["epsb"], lambda e: e.memset(epsb[:], EPS))

    rot = {"ps": 0, "cast": 0}
    dumped = set()

    def dump(name, ap, shape, dt, keys):
        if name not in dbg_out or name in dumped:
            return
        dumped.add(name)
        dst = nc.dram_tensor("dbg_" + name, list(shape), dt, kind="ExternalOutput").ap()
        k.dma("sp", dst, ap, keys, [("dbg", name)])

    with ExitStack() as es:
        cnt_t = [sb(es, "cnt%d" % i, [128, XW], F32) for i in range(2)]
        st32 = [sb(es, "st32_%d" % i, [128, XW], F32) for i in range(2)]
        gb = [sb(es, "gb%d" % i, [128, XW], BF16) for i in range(2)]
        for i in range(2):
            k.dma("sp", cnt_t[i][:], cnts_in[i], [], ["cnt%d" % i])
        for h in range(12):
            b = h % 2
            if h < 11:
                grp = 0 if h < 6 else 1
                k.dma("sp", st32[b][:], bstrip_in[h], [], ["st32_%d" % b])
                k.op("act", ["st32_%d" % b], ["st32_%d" % b],
                     lambda e, b=b: e.activation(out=st32[b][:], in_=st32[b][:], func=AF.Exp))
                k.op("dve", ["st32_%d" % b, "cnt%d" % grp], ["gb%d" % b],
                     lambda e, b=b, grp=grp: e.tensor_tensor(out=gb[b][:], in0=st32[b][:], in1=cnt_t[grp][:], op=ALU.mult))
            else:
                k.op("dve", ["cnt1"], ["gb%d" % b], lambda e, b=b: e.tensor_copy(out=gb[b][:], in_=cnt_t[1][:]))
            k.dma("pool", Gs[h], gb[b][:], ["gb%d" % b], [("Gs", h)])

    k.barrier()
    if stop == "p0a":
        k.finish()
        es_top.close()
        return nc, k

    with ExitStack() as es:
        cT = sb(es, "cT", [128, 8, NSEQ], F32)
        condT = sb(es, "condT", [128, 8, NSEQ], F32)
        adab = sb(es, "adab", [128, NL, 72], F32)
        aw = [sb(es, "aw%d" % i, [128, 8, 1152], F32) for i in range(2)]
        k.dma("sp", cT[:], cT_in[:, :, :], [], ["cT"])
        k.dma("sp", adab[:], adab_in[:, :, :], [], ["adab"])
        k.op("act", ["cT"], ["condT"], lambda e: e.activation(out=condT[:], in_=cT[:], func=AF.Silu))
        slab_i = 0
        for l in range(NL):
            pm = ps[l]
            for slab in range(8):
                b = slab_i % 2
                slab_i += 1
                src = adaw_in[l].rearrange("(kc p) n -> p kc n", p=128)[:, :, slab * 1152:(slab + 1) * 1152]
                k.dma("sp", aw[b][:], src, [], ["aw%d" % b])
                for j in range(9):
                    ch = slab * 9 + j
                    for kc in range(8):
                        k.op("pe", ["aw%d" % b, "condT"], BK[l],
                             lambda e, b=b, j=j, kc=kc, ch=ch, pm=pm: e.matmul(
                                 pm[:, ch * 2:ch * 2 + 2], lhsT=aw[b][:, kc, j * 128:(j + 1) * 128],
                                 rhs=condT[:, kc, :], start=(kc == 0), stop=(kc == 7)))
            for s in range(NSEQ):
                pv = pm[:, 0:144].rearrange("p (c s) -> p c s", s=2)[:, :, s]
                k.op("dve", BK[l] + ["adab"], ["modT"],
                     lambda e, l=l, s=s, pv=pv: e.tensor_tensor(out=modT[:, l, :, s], in0=pv, in1=adab[:, l, :], op=ALU.add))
        for l in range(NL):
            for w in range(3):
                for s in range(NSEQ):
                    sc = modT[:, l, (3 * w + 1) * 8:(3 * w + 2) * 8, s]
                    k.op("dve", ["modT", "gains"], ["Amod"],
                         lambda e, l=l, w=w, s=s, sc=sc: e.scalar_tensor_tensor(
                             out=Amod[:, l, w, s, :], in0=sc, scalar=1.0, in1=gains[:, l, w, :],
                             op0=ALU.add, op1=ALU.mult))

    k.barrier()
    dump("modT", modT[:], [128, NL, 72, NSEQ], F32, ["modT"])
    dump("Amod", Amod[:], [128, NL, 3, NSEQ, 8], F32, ["Amod"])
    if stop == "p0b":
        k.finish()
        es_top.close()
        return nc, k

    def next_ps(lo=2, hi=6):
        i = lo + rot["ps"] % (hi - lo)
        rot["ps"] += 1
        return i

    def cast_eng():
        e = ("act", "dve", "pool")[rot["cast"] % 3]
        rot["cast"] += 1
        return e

    def emit_cast(E, dst, src, reads, writes):
        if E == "act":
            k.op("act", reads, writes, lambda e: e.copy(out=dst, in_=src))
        else:
            k.op(E, reads, writes, lambda e: e.tensor_copy(out=dst, in_=src))

    def gate_bcast(gt, gkey, l, w, s, mult):
        for half in range(2):
            pi = next_ps()
            for q in range(4):
                kc = half * 4 + q
                col = (3 * w + 2) * 8 + kc
                dt_key = "dtmp%d" % (kc % 2)
                dtile = dtmp[kc % 2]
                k.op("dve", ["ident", "modT"], [dt_key],
                     lambda e, dtile=dtile, col=col: e.tensor_scalar(
                         out=dtile[:], in0=ident[:], scalar1=modT[:, l, col, s:s + 1], scalar2=None, op0=ALU.mult))
                k.op("pe", [dt_key, "ones"], [PS[pi]],
                     lambda e, pi=pi, q=q, dtile=dtile: e.matmul(
                         ps[pi][:, q * 128:(q + 1) * 128], lhsT=ones[:], rhs=dtile[:], start=True, stop=True))
            k.op("act", [PS[pi]], [gkey],
                 lambda e, pi=pi, half=half: e.activation(
                     out=gt[:, half * 512:(half + 1) * 512], in_=ps[pi][:], func=AF.Copy, scale=float(mult)))

    dtmp = [sb(es_top, "dtmp%d" % i, [128, 128], F32) for i in range(2)]

    def norm_chunk(hc, hkey, nt, ytiles, xT, xkey, l, w, s, ssq, std, rstd, junk):
        for t in range(nt):
            k.op("act", [hkey], ["junk", "ssq"],
                 lambda e, t=t: e.activation(out=junk[:], in_=hc[:, t, :], func=AF.Square, accum_out=ssq[:, t:t + 1]))
        k.op("act", ["ssq", "epsb"], ["std"],
             lambda e: e.activation(out=std[:, 0:nt], in_=ssq[:, 0:nt], func=AF.Sqrt, bias=epsb[:], scale=1.0 / D))
        k.op("dve", ["std"], ["rstd"], lambda e: e.reciprocal(out=rstd[:, 0:nt], in_=std[:, 0:nt]))
        W = nt * 128
        per_bank = 512 // W
        gsz = 2 * per_bank
        for t in range(nt):
            k.op("dve", [hkey, "rstd"], ["y%d" % t],
                 lambda e, t=t: e.tensor_scalar(out=ytiles[t][:], in0=hc[:, t, :], scalar1=rstd[:, t:t + 1],
                                                scalar2=None, op0=ALU.mult))
        for g0 in range(0, 8, gsz):
            for kc in range(g0, g0 + gsz):
                bank = ((kc - g0) // per_bank) % 2
                if EXPERIMENT == "banks" and g0 > 0:
                    bank += 2
                slot = kc % per_bank
                for t in range(nt):
                    col = slot * W + t * 128
                    k.op("pe", ["y%d" % t, "ident"], [PS[bank]],
                         lambda e, bank=bank, col=col, kc=kc, t=t: e.transpose(
                             out=ps[bank][:, col:col + 128], in_=ytiles[t][:, kc * 128:(kc + 1) * 128],
                             identity=ident[:]))
            for kc in range(g0, g0 + gsz):
                bank = ((kc - g0) // per_bank) % 2
                if EXPERIMENT == "banks" and g0 > 0:
                    bank += 2
                slot = kc % per_bank
                col = slot * W
                k.op("act", [PS[bank], "Amod", "modT"], [xkey],
                     lambda e, bank=bank, col=col, kc=kc: e.activation(
                         out=xT[:, kc, 0:W], in_=ps[bank][:, col:col + W], func=AF.Identity,
                         bias=modT[:, l, 3 * w * 8 + kc, s:s + 1], scale=Amod[:, l, w, s, kc:kc + 1]))


    def ffn_phase(l, w, src, final):
        with ExitStack() as es:
            wgu = sb(es, "wgu", [128, 8, 2 * DFF], BF16)
            wd = sb(es, "wd", [128, 22, D], BF16)
            stg = [sb(es, "stg%d" % i, [128, 1024], F32) for i in range(2)]
            hc = [sb(es, "hc%d" % i, [128, 2, D], F32) for i in range(2)]
            yt = [sb(es, "y%d" % i, [128, D], F32) for i in range(2)]
            xT = sb(es, "xT", [128, 8, 256], BF16)
            hT = sb(es, "hT", [128, 22, 256], BF16)
            sg = [sb(es, "sg%d" % i, [128, 256], F32) for i in range(2)]
            tmp = [sb(es, "tmp%d" % i, [128, 512], F32) for i in range(2)]
            gbc = [sb(es, "gbc%d" % i, [128, D], F32) for i in range(2)]
            junk = sb(es, "junk", [128, D], BF16)
            ssq = sb(es, "ssq", [128, 4], F32)
            std = sb(es, "std", [128, 4], F32)
            rstd = sb(es, "rstd", [128, 4], F32)
            finb = sb(es, "finb", [128, D], F32) if final else None
            if final:
                k.dma("sp", finb[:], fin_in[:, :], [], ["finb"])
            for kc in range(8):
                for c0 in range(0, 2 * DFF, 1024):
                    n = min(1024, 2 * DFF - c0)
                    k.dma("pool", wgu[:, kc, c0:c0 + n], wgu_in[w][l, kc * 128:(kc + 1) * 128, c0:c0 + n], [],
                          [("wgu", kc, c0 // 1024)])
            for j in range(22):
                k.dma("pool", wd[:, j, :], wd_in[w][l, j * 128:(j + 1) * 128, :], [], [("wd", j)])
            for s in range(NSEQ):
                gate_bcast(gbc[s], "gbc%d" % s, l, w, s, 0.5)
            dump("gbc0", gbc[0][:], [128, D], F32, ["gbc0"])
            ci = 0
            for s in range(NSEQ):
                for c in range(8):
                    b = ci % 2
                    ci += 1
                    hk = "hc%d" % b
                    hsrc = src[s][c * 256:(c + 1) * 256, :].rearrange("(t p) f -> p t f", p=128)
                    k.dma("sp", hc[b][:], hsrc, [("H", s, c // 2)], [hk])
                    norm_chunk(hc[b], hk, 2, yt, xT, "xT", l, w, s, ssq, std, rstd, junk)
                    dump("xT", xT[:], [128, 8, 256], BF16, ["xT"])
                    dump("rstd", rstd[:], [128, 4], F32, ["rstd"])
                    for j in range(22):
                        pg = 2 + (j % 2)
                        pu = 4 + (j % 2)
                        for kc in range(8):
                            k.op("pe", [("wgu", kc, (j * 128) // 1024), "xT"], [PS[pg]],
                                 lambda e, pg=pg, j=j, kc=kc: e.matmul(
                                     ps[pg][:, 0:256], lhsT=wgu[:, kc, j * 128:(j + 1) * 128], rhs=xT[:, kc, :],
                                     start=(kc == 0), stop=(kc == 7)))
                        for kc in range(8):
                            k.op("pe", [("wgu", kc, (DFF + j * 128) // 1024), "xT"], [PS[pu]],
                                 lambda e, pu=pu, j=j, kc=kc: e.matmul(
                                     ps[pu][:, 0:256], lhsT=wgu[:, kc, DFF + j * 128:DFF + (j + 1) * 128],
                                     rhs=xT[:, kc, :], start=(kc == 0), stop=(kc == 7)))
                        k.op("act", [PS[pg]], ["sg%d" % (j % 2)],
                             lambda e, pg=pg, j=j: e.activation(out=sg[j % 2][:], in_=ps[pg][:, 0:256], func=AF.Silu))
                        k.op("dve", [PS[pu], "sg%d" % (j % 2)], [("hT", j)],
                             lambda e, pu=pu, j=j: e.tensor_tensor(out=hT[:, j, :], in0=ps[pu][:, 0:256],
                                                                   in1=sg[j % 2][:], op=ALU.mult))
                    hT_keys = [("hT", j) for j in range(22)]
                    dump("hT", hT[:], [128, 22, 256], BF16, hT_keys)
                    oi = 0
                    for t in range(2):
                        for half in range(2):
                            py = oi % 2
                            oi += 1
                            pykeys = BK[py]
                            for j in range(22):
                                k.op("pe", [("wd", j)] + hT_keys, pykeys,
                                     lambda e, py=py, j=j, t=t, half=half: e.matmul(
                                         ps[py][:, :], lhsT=hT[:, j, t * 128:(t + 1) * 128],
                                         rhs=wd[:, j, half * 512:(half + 1) * 512], start=(j == 0), stop=(j == 21)))
                            k.op("dve", pykeys + ["gbc%d" % s], ["tmp%d" % py],
                                 lambda e, py=py, half=half, s=s: e.tensor_tensor(
                                     out=tmp[py][:], in0=ps[py][:, :], in1=gbc[s][:, half * 512:(half + 1) * 512],
                                     op=ALU.mult))
                            k.op("pool", [hk, "tmp%d" % py], [hk],
                                 lambda e, py=py, t=t, half=half, b=b: e.tensor_tensor(
                                     out=hc[b][:, t, half * 512:(half + 1) * 512],
                                     in0=hc[b][:, t, half * 512:(half + 1) * 512], in1=tmp[py][:], op=ALU.add))
                    if not final:
                        hdst = H[s][c * 256:(c + 1) * 256, :].rearrange("(t p) f -> p t f", p=128)
                        k.dma("pool", hdst, hc[b][:], [hk], [("H", s, c // 2)])
                    else:
                        for t in range(2):
                            k.op("act", [hk], ["junk", "ssq"],
                                 lambda e, t=t, b=b: e.activation(out=junk[:], in_=hc[b][:, t, :], func=AF.Square,
                                                                  accum_out=ssq[:, t:t + 1]))
                        k.op("act", ["ssq", "epsb"], ["std"],
                             lambda e: e.activation(out=std[:, 0:2], in_=ssq[:, 0:2], func=AF.Sqrt, bias=epsb[:],
                                                    scale=1.0 / D))
                        k.op("dve", ["std"], ["rstd"], lambda e: e.reciprocal(out=rstd[:, 0:2], in_=std[:, 0:2]))
                        for t in range(2):
                            k.op("dve", [hk, "rstd", "finb"], [hk],
                                 lambda e, t=t, b=b: e.scalar_tensor_tensor(
                                     out=hc[b][:, t, :], in0=hc[b][:, t, :], scalar=rstd[:, t:t + 1], in1=finb[:],
                                     op0=ALU.mult, op1=ALU.mult))
                        odst = out_d[s][c * 256:(c + 1) * 256, :].rearrange("(t p) f -> p t f", p=128)
                        k.dma("pool", odst, hc[b][:], [hk], [("OUT", s, c)])

    def mixproj_phase(l):
        w = 1
        with ExitStack() as es:
            wie = sb(es, "wie", [128, 8, WIE], BF16)
            wuq = sb(es, "wuq", [128, 2, 960], BF16)
            wukv = sb(es, "wukv", [128, 640], BF16)
            stg = [sb(es, "stg%d" % i, [128, 1024], F32) for i in range(2)]
            hc = [sb(es, "hc%d" % i, [128, 4, D], F32) for i in range(2)]
            yt = [sb(es, "y%d" % i, [128, D], F32) for i in range(4)]
            xT = sb(es, "xT", [128, 8, 512], BF16)
            junk = sb(es, "junk", [128, D], BF16)
            ssq = sb(es, "ssq", [128, 4], F32)
            std = sb(es, "std", [128, 4], F32)
            rstd = sb(es, "rstd", [128, 4], F32)
            posi = sb(es, "posi", [128, 512], I32)
            posf = sb(es, "posf", [128, 512], F32)
            u_ = sb(es, "u_", [128, 512], F32)
            ki = sb(es, "ki", [128, 512], I32)
            kf = sb(es, "kf", [128, 512], F32)
            tabs = {n: sb(es, n, [128, 512], F32) for n in ("C128", "S128", "CB", "SB")}
            t1 = [sb(es, "t1_%d" % i, [128, 512], F32) for i in range(2)]
            t2 = [sb(es, "t2_%d" % i, [128, 512], F32) for i in range(2)]
            ost = [sb(es, "ost%d" % i, [128, 512], BF16) for i in range(4)]
            mtile = sb(es, "mtile", [128, 456], F32)
            mn = sb(es, "mn", [128, 512], F32)
            ss3 = sb(es, "ss3", [128, 4], F32)
            sd3 = sb(es, "sd3", [128, 4], F32)
            rs3 = sb(es, "rs3", [128, 4], F32)
            cqnT = sb(es, "cqnT", [128, 2, 512], BF16)
            ckvnT = sb(es, "ckvnT", [128, 512], BF16)
            vst = [sb(es, "vst%d" % i, [128, 6, 66], BF16) for i in range(3)]
            iwst = sb(es, "iwst", [128, 4, 8], F32)
            ikst = sb(es, "ikst", [64, 512], BF16)
            ti = [sb(es, "ti%d" % i, [64, 128], F32) for i in range(2)]
            for i in range(3):
                k.op("pool", [], ["vst%d" % i], lambda e, i=i: e.memset(vst[i][:], 1.0))
            si = 0
            for kc in range(8):
                for c0 in range(0, WIE, 1024):
                    n = min(1024, WIE - c0)
                    b = si % 2
                    si += 1
                    k.dma("sp", stg[b][:, 0:n], wie_in[l, kc * 128:(kc + 1) * 128, c0:c0 + n], [], ["stg%d" % b])
                    emit_cast(cast_eng(), wie[:, kc, c0:c0 + n], stg[b][:, 0:n], ["stg%d" % b], ["wie"])
            for kc in range(2):
                b = si % 2
                si += 1
                k.dma("sp", stg[b][:, 0:960], wuq_in[l, kc * 128:(kc + 1) * 128, :], [], ["stg%d" % b])
                emit_cast(cast_eng(), wuq[:, kc, :], stg[b][:, 0:960], ["stg%d" % b], ["wuq"])
            b = si % 2
            si += 1
            k.dma("sp", stg[b][:, 0:640], wukv_in[l, :, :], [], ["stg%d" % b])
            emit_cast(cast_eng(), wukv[:, :], stg[b][:, 0:640], ["stg%d" % b], ["wukv"])

            ctr = {"o": 0, "t": 0, "v": 0, "e": 0}

            def evac_copy(dst, src, reads, writes, allow_act=True):
                ctr["e"] += 1
                if allow_act and ctr["e"] % 2 == 0:
                    k.op("act", reads, writes, lambda e: e.activation(out=dst, in_=src, func=AF.Copy))
                else:
                    k.op("dve", reads, writes, lambda e: e.tensor_copy(out=dst, in_=src))

            def make_table(name, invcol, addc, sccol, bscol):
                k.op("dve", ["posf", "ropec"], ["u_"],
                     lambda e: e.tensor_scalar(out=u_[:], in0=posf[:], scalar1=ropec[:, invcol:invcol + 1],
                                               scalar2=float(addc), op0=ALU.mult, op1=ALU.add))
                k.op("dve", ["u_"], ["ki"], lambda e: e.tensor_copy(out=ki[:], in_=u_[:]))
                k.op("dve", ["ki"], ["kf"], lambda e: e.tensor_copy(out=kf[:], in_=ki[:]))
                k.op("dve", ["u_", "kf"], ["u_"],
                     lambda e: e.tensor_tensor(out=u_[:], in0=u_[:], in1=kf[:], op=ALU.subtract))
                k.op("dve", ["u_"], ["kf"],
                     lambda e: e.scalar_tensor_tensor(out=kf[:], in0=u_[:], scalar=0.0, in1=u_[:],
                                                      op0=ALU.is_lt, op1=ALU.add))
                k.op("act", ["kf", "ropec"], [name],
                     lambda e: e.activation(out=tabs[name][:], in_=kf[:], func=AF.Sin,
                                            bias=ropec[:, bscol:bscol + 1], scale=ropec[:, sccol:sccol + 1]))

            ci = 0
            for s in range(NSEQ):
                for c in range(4):
                    b = ci % 2
                    ci += 1
                    hk = "hc%d" % b
                    csl = slice(c * 512, (c + 1) * 512)
                    hsrc = H[s][csl, :].rearrange("(t p) f -> p t f", p=128)
                    k.dma("sp", hc[b][:], hsrc, [("H", s, c)], [hk])
                    k.dma("sp", posi[:], pos_in[s][:, csl], [], ["posi"])
                    norm_chunk(hc[b], hk, 4, yt, xT, "xT", l, w, s, ssq, std, rstd, junk)
                    k.op("dve", ["posi"], ["posf"], lambda e: e.tensor_copy(out=posf[:], in_=posi[:]))
                    make_table("C128", 0, 0.25, 6, 7)
                    make_table("S128", 0, 0.0, 2, 3)
                    make_table("CB", 1, 0.25, 6, 7)
                    make_table("SB", 1, 0.0, 4, 5)

                    def fm_proj(col0, M, pi):
                        for kc in range(8):
                            k.op("pe", ["wie", "xT"], [PS[pi]],
                                 lambda e, kc=kc: e.matmul(ps[pi][0:M, :], lhsT=wie[:, kc, col0:col0 + M],
                                                           rhs=xT[:, kc, :], start=(kc == 0), stop=(kc == 7)))

                    plain = []
                    for i in range(3):
                        plain.append((QaT, "QaT", i * 128, C_QA + i * 128, 128))
                    for i in range(3):
                        plain.append((KaT, "KaT", i * 128, C_KA + i * 128, 128))
                    for i, M in enumerate((128, 128, 64)):
                        plain.append((QcT, "QcT", i * 128, C_QC + i * 128, M))
                    for i, M in enumerate((128, 128, 64)):
                        plain.append((KcT, "KcT", i * 128, C_KC + i * 128, M))
                    for (dst, dname, r0, col0, M) in plain:
                        pi = next_ps()
                        fm_proj(col0, M, pi)
                        o = ctr["o"] % 4
                        ctr["o"] += 1
                        evac_copy(ost[o][0:M, :], ps[pi][0:M, :], [PS[pi]], ["ost%d" % o])
                        k.dma("pool", dst[s][r0:r0 + M, csl], ost[o][0:M, :], ["ost%d" % o], [(dname, s, r0, c)])
                    roped = [(IqT, "IqT", i * 128, C_IQ + i * 128, C_IQS + i * 128, 128) for i in range(4)]
                    roped.append((KrT, "KrT", 0, C_KR, C_KRS, 32))
                    for (dst, dname, r0, col0, col1, M) in roped:
                        p1 = next_ps()
                        fm_proj(col0, M, p1)
                        p2 = next_ps()
                        fm_proj(col1, M, p2)
                        tb = ctr["t"] % 2
                        ctr["t"] += 1
                        o = ctr["o"] % 4
                        ctr["o"] += 1
                        k.op("dve", [PS[p1], "C128"], ["t1_%d" % tb],
                             lambda e, p1=p1, tb=tb, M=M: e.tensor_tensor(out=t1[tb][0:M, :], in0=ps[p1][0:M, :],
                                                                          in1=tabs["C128"][0:M, :], op=ALU.mult))
                        k.op("dve", [PS[p2], "S128"], ["t2_%d" % tb],
                             lambda e, p2=p2, tb=tb, M=M: e.tensor_tensor(out=t2[tb][0:M, :], in0=ps[p2][0:M, :],
                                                                          in1=tabs["S128"][0:M, :], op=ALU.mult))
                        k.op("pool", ["t1_%d" % tb, "t2_%d" % tb], ["ost%d" % o],
                             lambda e, tb=tb, o=o, M=M: e.tensor_tensor(out=ost[o][0:M, :], in0=t1[tb][0:M, :],
                                                                        in1=t2[tb][0:M, :], op=ALU.add))
                        k.dma("pool", dst[s][r0:r0 + M, csl], ost[o][0:M, :], ["ost%d" % o], [(dname, s, r0, c)])
                    for t in range(4):
                        r0 = c * 512 + t * 128
                        tsl = slice(t * 128, (t + 1) * 128)
                        for (dst, dname, col0, nh) in ((Va, "Va", C_VA, 6), (Vc, "Vc", C_VC, 5)):
                            pi = next_ps()
                            for kc in range(8):
                                k.op("pe", ["wie", "xT"], [PS[pi]],
                                     lambda e, kc=kc, pi=pi, col0=col0, nh=nh: e.matmul(
                                         ps[pi][:, 0:nh * 64], lhsT=xT[:, kc, tsl], rhs=wie[:, kc, col0:col0 + nh * 64],
                                         start=(kc == 0), stop=(kc == 7)))
                            v = ctr["v"] % 3
                            ctr["v"] += 1
                            evac_copy(vst[v][:, 0:nh, 0:64],
                                      ps[pi][:, 0:nh * 64].rearrange("p (h e) -> p h e", e=64), [PS[pi]], ["vst%d" % v],
                                      allow_act=False)
                            k.dma("pool", dst[s][r0:r0 + 128, :],
                                  vst[v][:, 0:nh, :].rearrange("p h e -> p (h e)"), ["vst%d" % v], [(dname, s, r0)])
                        pi = next_ps()
                        for kc in range(8):
                            k.op("pe", ["wie", "xT"], [PS[pi]],
                                 lambda e, kc=kc, pi=pi: e.matmul(
                                     ps[pi][:, 0:456], lhsT=xT[:, kc, tsl], rhs=wie[:, kc, C_MISC:C_MISC + 456],
                                     start=(kc == 0), stop=(kc == 7)))
                        k.op("act", [PS[pi]], ["mtile"], lambda e, pi=pi: e.copy(out=mtile[:], in_=ps[pi][:, 0:456]))
                        segs = ((0, 256), (256, 384), (384, 448))
                        for j, (a0, a1) in enumerate(segs):
                            k.op("act", ["mtile"], ["junk", "ss3"],
                                 lambda e, j=j, a0=a0, a1=a1: e.activation(
                                     out=junk[:, 0:a1 - a0], in_=mtile[:, a0:a1], func=AF.Square,
                                     accum_out=ss3[:, j:j + 1]))
                        for j, (a0, a1) in enumerate(segs):
                            k.op("act", ["ss3", "epsb"], ["sd3"],
                                 lambda e, j=j, a0=a0, a1=a1: e.activation(
                                     out=sd3[:, j:j + 1], in_=ss3[:, j:j + 1], func=AF.Sqrt, bias=epsb[:],
                                     scale=1.0 / (a1 - a0)))
                        k.op("dve", ["sd3"], ["rs3"], lambda e: e.reciprocal(out=rs3[:, 0:3], in_=sd3[:, 0:3]))
                        for j, (a0, a1) in enumerate(segs):
                            k.op("dve", ["mtile", "rs3"], ["mn"],
                                 lambda e, j=j, a0=a0, a1=a1: e.tensor_scalar(
                                     out=mn[:, a0:a1], in0=mtile[:, a0:a1], scalar1=rs3[:, j:j + 1], scalar2=None,
                                     op0=ALU.mult))
                        for (d0, s0, n) in ((448, 400, 16), (464, 384, 16), (480, 416, 32)):
                            k.op("dve", ["mn"], ["mn"],
                                 lambda e, d0=d0, s0=s0, n=n: e.tensor_copy(out=mn[:, d0:d0 + n], in_=mn[:, s0:s0 + n]))
                        k.op("pool", ["mtile"], ["iwst"],
                             lambda e, t=t: e.tensor_copy(out=iwst[:, t, :], in_=mtile[:, 448:456]))
                        pa = next_ps()
                        pb = next_ps()
                        for j in range(3):
                            k.op("pe", ["mn", "ident"], [PS[pa]],
                                 lambda e, j=j, pa=pa: e.transpose(out=ps[pa][:, j * 128:(j + 1) * 128],
                                                                   in_=mn[:, j * 128:(j + 1) * 128], identity=ident[:]))
                        k.op("pe", ["mn", "ident"], [PS[pa]],
                             lambda e, pa=pa: e.transpose(out=ps[pa][0:64, 384:512], in_=mn[:, 384:448], identity=ident[:]))
                        k.op("pe", ["mn", "ident"], [PS[pb]],
                             lambda e, pb=pb: e.transpose(out=ps[pb][0:64, 0:128], in_=mn[:, 448:512], identity=ident[:]))
                        for j in range(2):
                            k.op("act", [PS[pa], "qn"], ["cqnT"],
                                 lambda e, j=j, pa=pa: e.activation(out=cqnT[:, j, tsl], in_=ps[pa][:, j * 128:(j + 1) * 128],
                                                                    func=AF.Copy, scale=qn[:, l, j:j + 1]))
                        k.op("act", [PS[pa], "kvn"], ["ckvnT"],
                             lambda e, pa=pa: e.activation(out=ckvnT[:, tsl], in_=ps[pa][:, 256:384], func=AF.Copy,
                                                           scale=kvn[:, l:l + 1]))
                        k.op("dve", [PS[pa], "ikn", "C128"], ["ti0"],
                             lambda e, pa=pa: e.scalar_tensor_tensor(
                                 out=ti[0][:], in0=ps[pa][0:64, 384:512], scalar=ikn[:, l, 0:1],
                                 in1=tabs["C128"][0:64, tsl], op0=ALU.mult, op1=ALU.mult))
                        k.op("dve", [PS[pb], "ikn", "S128"], ["ti1"],
                             lambda e, pb=pb: e.scalar_tensor_tensor(
                                 out=ti[1][:], in0=ps[pb][0:64, 0:128], scalar=ikn[:, l, 1:2],
                                 in1=tabs["S128"][0:64, tsl], op0=ALU.mult, op1=ALU.mult))
                        k.op("pool", ["ti0", "ti1"], ["ikst"],
                             lambda e: e.tensor_tensor(out=ikst[:, tsl], in0=ti[0][:], in1=ti[1][:], op=ALU.add))
                    k.dma("pool", IkT[s][:, csl], ikst[:], ["ikst"], [("IkT", s, c)])
                    k.dma("pool", Iw[s][csl, :].rearrange("(t p) h -> p t h", p=128), iwst[:], ["iwst"], [("Iw", s, c)])
                    for h in range(5):
                        p1 = next_ps()
                        p2 = next_ps()
                        for (pp, cbase) in ((p1, 0), (p2, 480)):
                            for kc in range(2):
                                k.op("pe", ["wuq", "cqnT"], [PS[pp]],
                                     lambda e, kc=kc, pp=pp, cbase=cbase, h=h: e.matmul(
                                         ps[pp][0:96, :], lhsT=wuq[:, kc, cbase + h * 96:cbase + (h + 1) * 96],
                                         rhs=cqnT[:, kc, :], start=(kc == 0), stop=(kc == 1)))
                        tb = ctr["t"] % 2
                        ctr["t"] += 1
                        o = ctr["o"] % 4
                        ctr["o"] += 1
                        k.op("dve", [PS[p1], "CB"], ["t1_%d" % tb],
                             lambda e, p1=p1, tb=tb: e.tensor_tensor(out=t1[tb][0:96, :], in0=ps[p1][0:96, :],
                                                                     in1=tabs["CB"][0:96, :], op=ALU.mult))
                        k.op("dve", [PS[p2], "SB"], ["t2_%d" % tb],
                             lambda e, p2=p2, tb=tb: e.tensor_tensor(out=t2[tb][0:96, :], in0=ps[p2][0:96, :],
                                                                     in1=tabs["SB"][0:96, :], op=ALU.mult))
                        k.op("pool", ["t1_%d" % tb, "t2_%d" % tb], ["ost%d" % o],
                             lambda e, tb=tb, o=o: e.tensor_tensor(out=ost[o][0:96, :], in0=t1[tb][0:96, :],
                                                                   in1=t2[tb][0:96, :], op=ALU.add))
                        k.dma("pool", QbT[s][h][:, csl], ost[o][0:96, :], ["ost%d" % o], [("QbT", s, h, c)])
                    for i, M in enumerate((128, 128, 64)):
                        pi = next_ps()
                        k.op("pe", ["wukv", "ckvnT"], [PS[pi]],
                             lambda e, pi=pi, i=i, M=M: e.matmul(ps[pi][0:M, :], lhsT=wukv[:, i * 128:i * 128 + M],
                                                                 rhs=ckvnT[:, :], start=True, stop=True))
                        o = ctr["o"] % 4
                        ctr["o"] += 1
                        evac_copy(ost[o][0:M, :], ps[pi][0:M, :], [PS[pi]], ["ost%d" % o])
                        k.dma("pool", KnT[s][i * 128:i * 128 + M, csl], ost[o][0:M, :], ["ost%d" % o],
                              [("KnT", s, i, c)])
                    for t in range(4):
                        r0 = c * 512 + t * 128
                        tsl = slice(t * 128, (t + 1) * 128)
                        pi = next_ps()
                        k.op("pe", ["wukv", "ckvnT"], [PS[pi]],
                             lambda e, pi=pi, tsl=tsl: e.matmul(ps[pi][:, 0:320], lhsT=ckvnT[:, tsl], rhs=wukv[:, 320:640],
                                                                start=True, stop=True))
                        v = ctr["v"] % 3
                        ctr["v"] += 1
                        evac_copy(vst[v][:, 0:5, 0:64], ps[pi][:, 0:320].rearrange("p (h e) -> p h e", e=64),
                                  [PS[pi]], ["vst%d" % v], allow_act=False)
                        k.dma("pool", Vb[s][r0:r0 + 128, :], vst[v][:, 0:5, :].rearrange("p h e -> p (h e)"),
                              ["vst%d" % v], [("Vb", s, r0)])

    SC_IDX = float(8 ** -0.5 * 64 ** -0.5)

    def att_phase(l, only_heads=None, do_idx=True):
        with ExitStack() as es:
            qt = [sb(es, "qt%d" % i, [128, T], BF16) for i in range(2)]
            kt = [sb(es, "kt%d" % i, [128, T], BF16) for i in range(2)]
            vt = [sb(es, "vt%d" % i, [128, 16, 66], BF16) for i in range(2)]
            gt = [sb(es, "gt%d" % i, [128, XW], BF16) for i in range(2)]
            et = [sb(es, "et%d" % i, [128, 512], BF16) for i in range(2)]
            pt = [sb(es, "pt%d" % i, [128, 512], BF16) for i in range(2)]
            p2 = [sb(es, "p2%d" % i, [128, 512], BF16) for i in range(2)]
            stl = [sb(es, "stl%d" % i, [128, 512], BF16) for i in range(2)]
            ostg = [sb(es, "ostg%d" % i, [128, 4, 64], BF16) for i in range(2)]
            rc = [sb(es, "rc%d" % i, [128, 4], F32) for i in range(2)]
            iq_sb = sb(es, "iq_sb", [128, 4, T], BF16)
            ik2 = sb(es, "ik2", [128, T], BF16)
            iw_sb = sb(es, "iw_sb", [128, 16, 8], F32)
            NZ = 4
            scoreZ = [sb(es, "score%d" % i, [128, T], F32) for i in range(NZ)]
            workZ = [sb(es, "work%d" % i, [128, T], F32) for i in range(NZ)]
            m8Z = [sb(es, "m8_%d" % i, [128, 8], F32) for i in range(NZ)]
            selqZ = [sb(es, "selq%d" % i, [128, T], BF16) for i in range(NZ)]
            rr = [sb(es, "rr%d" % i, [128, 512], F32) for i in range(2)]
            m8 = sb(es, "m8", [128, 8], F32)
            selq = sb(es, "selq", [128, T], BF16)
            sstg = [sb(es, "sstg%d" % i, [128, 512], BF16) for i in range(2)]
            cn = {"h": 0, "i": 0, "t": 0, "o": 0}

            def idx_phase(s):
                k.dma("sp", iq_sb[:], IqT[s].rearrange("(j p) t -> p j t", p=128), [], ["iq_sb"])
                k.dma("sp", ik2[0:64, :], IkT[s], [], ["ik2"])
                k.dma("sp", ik2[64:128, :], IkT[s], [], ["ik2b"])
                k.dma("sp", iw_sb[:], Iw[s].rearrange("(t p) h -> p t h", p=128), [], ["iw_sb"])

                def scores(i, z):
                    sc, sk_ = scoreZ[z], "score%d" % z
                    nk = (i + 1) * 128
                    for kc in range((nk + 511) // 512):
                        n = min(512, nk - kc * 512)
                        ksl = slice(kc * 512, kc * 512 + n)
                        for h8 in range(8):
                            x = cn["i"] % 2
                            cn["i"] += 1
                            pview = psb[x][:, :].bitcast(F32)
                            r0 = (h8 % 2) * 64
                            k.op("pe", ["iq_sb", "ik2", "ik2b"], [PSB[x]],
                                 lambda e, pview=pview, r0=r0, h8=h8, n=n, ksl=ksl: e.matmul(
                                     pview[:, 0:n], lhsT=iq_sb[r0:r0 + 64, h8 // 2, i * 128:(i + 1) * 128],
                                     rhs=ik2[r0:r0 + 64, ksl], start=True, stop=True))
                            k.op("act", [PSB[x]], ["rr%d" % x],
                                 lambda e, pview=pview, x=x, n=n: e.activation(out=rr[x][:, 0:n], in_=pview[:, 0:n],
                                                                               func=AF.Relu, scale=SC_IDX))
                            if h8 == 0:
                                k.op("dve", ["rr%d" % x, "iw_sb"], [sk_],
                                     lambda e, x=x, n=n, ksl=ksl: e.tensor_scalar(
                                         out=sc[:, ksl], in0=rr[x][:, 0:n], scalar1=iw_sb[:, i, 0:1], scalar2=None,
                                         op0=ALU.mult))
                            else:
                                k.op("dve", ["rr%d" % x, "iw_sb", sk_], [sk_],
                                     lambda e, x=x, n=n, ksl=ksl, h8=h8: e.scalar_tensor_tensor(
                                         out=sc[:, ksl], in0=rr[x][:, 0:n], scalar=iw_sb[:, i, h8:h8 + 1],
                                         in1=sc[:, ksl], op0=ALU.mult, op1=ALU.add))
                    dsl = slice(i * 128, (i + 1) * 128)
                    k.op("dve", [sk_, "negm"], [sk_],
                         lambda e, dsl=dsl: e.tensor_tensor(out=sc[:, dsl], in0=sc[:, dsl], in1=negm[:], op=ALU.add))

                def select_and_store(i, z, use_thr):
                    sc, sk_ = scoreZ[z], "score%d" % z
                    sq, qk_ = selqZ[z], "selq%d" % z
                    nk = (i + 1) * 128
                    if use_thr:
                        k.op("dve", [sk_, "m8_%d" % z], [qk_],
                             lambda e: e.tensor_scalar(out=sq[:, 0:nk], in0=sc[:, 0:nk], scalar1=m8Z[z][:, 7:8],
                                                       scalar2=None, op0=ALU.is_ge))
                    else:
                        k.op("dve", [sk_], [qk_],
                             lambda e: e.tensor_scalar(out=sq[:, 0:nk], in0=sc[:, 0:nk], scalar1=-1.0e29,
                                                       scalar2=None, op0=ALU.is_ge))
                    for kb0 in range(0, i + 1, 4):
                        nb = min(4, i + 1 - kb0)
                        x = cn["t"] % 2
                        cn["t"] += 1
                        for j in range(nb):
                            k.op("pe", [qk_, "identb"], [PSB[x]],
                                 lambda e, x=x, j=j, kb0=kb0: e.transpose(
                                     out=psb[x][:, j * 128:(j + 1) * 128], in_=sq[:, (kb0 + j) * 128:(kb0 + j + 1) * 128],
                                     identity=identb[:]))
                        k.op("act", [PSB[x]], ["sstg%d" % x],
                             lambda e, x=x, nb=nb: e.activation(out=sstg[x][:, 0:nb * 128], in_=psb[x][:, 0:nb * 128],
                                                                func=AF.Copy))
                        dst = SelT[s][kb0 * 128:(kb0 + nb) * 128, i * 128:(i + 1) * 128].rearrange("(k p) q -> p k q", p=128)
                        k.dma("pool", dst, sstg[x][:, 0:nb * 128].rearrange("p (k q) -> p k q", q=128),
                              ["sstg%d" % x], [("SelT", s, i, kb0 // 4)])

                for i in (0, 1):
                    scores(i, i)
                    select_and_store(i, i, False)
                    yield
                for i0 in range(2, 16, NZ):
                    zs = list(range(min(NZ, 16 - i0)))
                    for z in zs:
                        nk = (i0 + z + 1) * 128
                        scores(i0 + z, z)
                        k.op("pool", ["score%d" % z], ["work%d" % z],
                             lambda e, z=z, nk=nk: e.tensor_copy(out=workZ[z][:, 0:nk], in_=scoreZ[z][:, 0:nk]))
                        yield
                    for r in range(32):
                        for z in zs:
                            nk = (i0 + z + 1) * 128
                            k.op("dve", ["work%d" % z], ["m8_%d" % z],
                                 lambda e, z=z, nk=nk: e.max(out=m8Z[z][:], in_=workZ[z][:, 0:nk]))
                        yield
                        if r < 31:
                            for z in zs:
                                nk = (i0 + z + 1) * 128
                                k.op("dve", ["work%d" % z, "m8_%d" % z], ["work%d" % z],
                                     lambda e, z=z, nk=nk: e.match_replace(
                                         out=workZ[z][:, 0:nk], in_to_replace=m8Z[z][:], in_values=workZ[z][:, 0:nk],
                                         imm_value=NEG))
                            yield
                    for z in zs:
                        select_and_store(i0 + z, z, True)
                        yield

            def head_attn(s, head):
                b = cn["h"] % 2
                cn["h"] += 1
                if head < 6:
                    d, hh = 64, head
                    qsrc = [(QaT[s][hh * 64:(hh + 1) * 64, :], 0, 64)]
                    ksrc = [(KaT[s][hh * 64:(hh + 1) * 64, :], 0, 64)]
                    vsrc = Va[s].rearrange("(t p) (h e) -> p t h e", p=128, e=66)[:, :, hh, :]
                    gidx, maskall, sel = hh, True, False
                elif head < 11:
                    d, hh = 96, head - 6
                    qsrc = [(QbT[s][hh], 0, 96)]
                    ksrc = [(KnT[s][hh * 64:(hh + 1) * 64, :], 0, 64), (KrT[s], 64, 96)]
                    vsrc = Vb[s].rearrange("(t p) (h e) -> p t h e", p=128, e=66)[:, :, hh, :]
                    gidx, maskall, sel = 11, False, False
                else:
                    d, hh = 64, head - 11
                    qsrc = [(QcT[s][hh * 64:(hh + 1) * 64, :], 0, 64)]
                    ksrc = [(KcT[s][hh * 64:(hh + 1) * 64, :], 0, 64)]
                    vsrc = Vc[s].rearrange("(t p) (h e) -> p t h e", p=128, e=66)[:, :, hh, :]
                    gidx, maskall, sel = 6 + hh, True, True
                scale = float(d ** -0.5)
                qk, kk_, vk, gk = "qt%d" % b, "kt%d" % b, "vt%d" % b, "gt%d" % b
                for (src, p0, p1) in qsrc:
                    k.dma("sp", qt[b][p0:p1, :], src, [], [qk])
                kkeys = []
                for j, (src, p0, p1) in enumerate(ksrc):
                    kkeys.append(kk_ + "_%d" % j)
                    k.dma("sp", kt[b][p0:p1, :], src, [], [kk_ + "_%d" % j])
                k.dma("sp", vt[b][:], vsrc, [], [vk])
                k.dma("sp", gt[b][:], Gs[gidx], [], [gk])
                for c in range(4):
                    nkb = 4 * c + 4
                    qsl = slice(c * 512, (c + 1) * 512)

                    def emit_S(kb):
                        pS = 4 + kb % 2
                        k.op("pe", [qk] + kkeys, [PS[pS]],
                             lambda e, pS=pS, kb=kb: e.matmul(ps[pS][:, :], lhsT=kt[b][0:d, kb * 128:(kb + 1) * 128],
                                                              rhs=qt[b][0:d, qsl], start=True, stop=True))

                    emit_S(0)
                    for kb in range(nkb):
                        if kb + 1 < nkb:
                            emit_S(kb + 1)
                        pS = 4 + kb % 2
                        eb = kb % 2
                        k.op("act", [PS[pS]], ["et%d" % eb],
                             lambda e, pS=pS, eb=eb: e.activation(out=et[eb][:], in_=ps[pS][:, :], func=AF.Exp, scale=scale))
                        cur, curk = et[eb], "et%d" % eb
                        if sel:
                            x0 = c * 512 - kb * 128 + OFF
                            skeys = [("SelT", s, 4 * c + q, kb // 4) for q in range(4) if kb <= 4 * c + q]
                            k.dma("sp", stl[eb][:], SelT[s][kb * 128:(kb + 1) * 128, qsl], skeys, ["stl%d" % eb])
                            k.op("dve", [curk, gk], ["pt%d" % eb],
                                 lambda e, eb=eb, x0=x0, cur=cur: e.tensor_tensor(out=pt[eb][:], in0=cur[:],
                                                                                  in1=gt[b][:, x0:x0 + 512], op=ALU.mult))
                            k.op("dve", ["pt%d" % eb, "stl%d" % eb], ["p2%d" % eb],
                                 lambda e, eb=eb: e.tensor_tensor(out=p2[eb][:], in0=pt[eb][:], in1=stl[eb][:],
                                                                  op=ALU.mult))
                            cur, curk = p2[eb], "p2%d" % eb
                        elif maskall or kb >= 4 * c:
                            x0 = c * 512 - kb * 128 + OFF
                            k.op("dve", [curk, gk], ["pt%d" % eb],
                                 lambda e, eb=eb, x0=x0, cur=cur: e.tensor_tensor(out=pt[eb][:], in0=cur[:],
                                                                                  in1=gt[b][:, x0:x0 + 512], op=ALU.mult))
                            cur, curk = pt[eb], "pt%d" % eb
                        for q in range(4):
                            i = 4 * c + q
                            if kb <= i:
                                k.op("pe", [curk, vk], [PS[q]],
                                     lambda e, q=q, kb=kb, i=i, cur=cur: e.matmul(
                                         ps[q][:, 0:65], lhsT=cur[:, q * 128:(q + 1) * 128], rhs=vt[b][:, kb, 0:65],
                                         start=(kb == 0), stop=(kb == i)))
                        yield
                    ob = cn["o"] % 2
                    cn["o"] += 1
                    for q in range(4):
                        k.op("dve", [PS[q]], ["rc%d" % ob],
                             lambda e, q=q, ob=ob: e.reciprocal(out=rc[ob][:, q:q + 1], in_=ps[q][:, 64:65]))
                        k.op("dve", [PS[q], "rc%d" % ob], ["ostg%d" % ob],
                             lambda e, q=q, ob=ob: e.tensor_scalar(out=ostg[ob][:, q, :], in0=ps[q][:, 0:64],
                                                                   scalar1=rc[ob][:, q:q + 1], scalar2=None, op0=ALU.mult))
                    dst = Oall[s][qsl, head * 64:(head + 1) * 64].rearrange("(t p) e -> p t e", p=128)
                    k.dma("pool", dst, ostg[ob][:], ["ostg%d" % ob], [("Oall", s, c, head)])

            def drain(g):
                for _ in g:
                    pass

            gis = [idx_phase(s) if do_idx else iter(()) for s in range(NSEQ)]
            for s in range(NSEQ):
                for head in range(16):
                    if only_heads is not None and head not in only_heads:
                        continue
                    if head >= 11:
                        drain(gis[s])
                    for _ in head_attn(s, head):
                        if head < 11:
                            next(gis[s], None)
                        elif s + 1 < NSEQ:
                            next(gis[s + 1], None)
                drain(gis[s])

    def outproj_phase(l):
        with ExitStack() as es:
            wout = sb(es, "wout", [128, 8, D], BF16)
            stg = [sb(es, "stg%d" % i, [128, 1024], F32) for i in range(2)]
            gbc = [sb(es, "gbc%d" % i, [128, D], F32) for i in range(2)]
            oc = [sb(es, "oc%d" % i, [128, 4, D], BF16) for i in range(2)]
            hc = [sb(es, "hc%d" % i, [128, 4, D], F32) for i in range(2)]
            oT = sb(es, "oT", [128, 8, 512], BF16)
            tmp = [sb(es, "tmp%d" % i, [128, 512], F32) for i in range(2)]
            for kc in range(8):
                b = kc % 2
                k.dma("sp", stg[b][:], wout_in[l, kc * 128:(kc + 1) * 128, :], [], ["stg%d" % b])
                emit_cast(cast_eng(), wout[:, kc, :], stg[b][:], ["stg%d" % b], ["wout"])
            for s in range(NSEQ):
                gate_bcast(gbc[s], "gbc%d" % s, l, 1, s, 1.0)
            ci = 0
            ti_ = 0
            for s in range(NSEQ):
                for c in range(4):
                    b = ci % 2
                    ci += 1
                    csl = slice(c * 512, (c + 1) * 512)
                    k.dma("sp", oc[b][:], Oall[s][csl, :].rearrange("(t p) f -> p t f", p=128), [], ["oc%d" % b])
                    k.dma("sp", hc[b][:], H[s][csl, :].rearrange("(t p) f -> p t f", p=128), [("H", s, c)], ["hc%d" % b])
                    for kc in range(8):
                        x = kc % 2
                        for t in range(4):
                            k.op("pe", ["oc%d" % b, "identb"], [PSB[x]],
                                 lambda e, x=x, t=t, kc=kc, b=b: e.transpose(
                                     out=psb[x][:, t * 128:(t + 1) * 128], in_=oc[b][:, t, kc * 128:(kc + 1) * 128],
                                     identity=identb[:]))
                        if kc % 2 == 0:
                            k.op("act", [PSB[x]], ["oT"],
                                 lambda e, x=x, kc=kc: e.activation(out=oT[:, kc, :], in_=psb[x][:, 0:512], func=AF.Copy))
                        else:
                            k.op("dve", [PSB[x]], ["oT"],
                                 lambda e, x=x, kc=kc: e.tensor_copy(out=oT[:, kc, :], in_=psb[x][:, 0:512]))
                    for t in range(4):
                        for half in range(2):
                            pi = next_ps()
                            for kc in range(8):
                                k.op("pe", ["oT", "wout"], [PS[pi]],
                                     lambda e, pi=pi, kc=kc, t=t, half=half: e.matmul(
                                         ps[pi][:, :], lhsT=oT[:, kc, t * 128:(t + 1) * 128],
                                         rhs=wout[:, kc, half * 512:(half + 1) * 512], start=(kc == 0), stop=(kc == 7)))
                            x = ti_ % 2
                            ti_ += 1
                            k.op("dve", [PS[pi], "gbc%d" % s], ["tmp%d" % x],
                                 lambda e, pi=pi, x=x, half=half, s=s: e.tensor_tensor(
                                     out=tmp[x][:], in0=ps[pi][:, :], in1=gbc[s][:, half * 512:(half + 1) * 512], op=ALU.mult))
                            k.op("pool", ["hc%d" % b, "tmp%d" % x], ["hc%d" % b],
                                 lambda e, x=x, t=t, half=half, b=b: e.tensor_tensor(
                                     out=hc[b][:, t, half * 512:(half + 1) * 512],
                                     in0=hc[b][:, t, half * 512:(half + 1) * 512], in1=tmp[x][:], op=ALU.add))
                    k.dma("pool", H[s][csl, :].rearrange("(t p) f -> p t f", p=128), hc[b][:], ["hc%d" % b], [("H", s, c)])

    stops = {"ffn1": 1, "mixproj": 2, "att": 3, "outproj": 4, "ffn2": 5}

    def done():
        k.finish()
        es_top.close()
        return nc, k

    for l in range(NL):
        ffn_phase(l, 0, x_in if l == 0 else H, False)
        k.barrier()
        if stop == "ffn1":
            return done()
        mixproj_phase(l)
        k.barrier()
        if stop == "mixproj":
            return done()
        if stop in ("attA", "attB", "attC"):
            heads = {"attA": [0], "attB": [6], "attC": [11]}[stop]
            att_phase(l, only_heads=heads, do_idx=(stop == "attC"))
            return done()
        att_phase(l)
        k.barrier()
        if stop == "att":
            return done()
        outproj_phase(l)
        k.barrier()
        if stop == "outproj":
            return done()
        ffn_phase(l, 2, H, l == NL - 1)
        k.barrier()
        if stop == "ffn2":
            return done()
    return done()


def host_prep(inputs):
    f32 = np.float32
    x = np.asarray(inputs["x"], f32)
    c = np.asarray(inputs["c"], f32)
    pos = np.asarray(inputs["positions"], np.int32)
    rel_bias = np.asarray(inputs["rel_bias"], f32)
    w_in = np.asarray(inputs["w_in"], f32)
    offs = np.cumsum([0, 384, 384, 384, 256, 128, 32, 320, 320, 320, 512, 64, 8])
    qa, ka, va, cq, ckv, kr, qc, kc_, vc, iq, ik, iw = [w_in[:, :, offs[i]:offs[i + 1]] for i in range(12)]
    iq_h = iq.reshape(NL, D, 8, 64)
    iq_sw = np.concatenate([iq_h[..., 16:32], iq_h[..., 0:16], iq_h[..., 32:64]], axis=-1).reshape(NL, D, 512)
    kr_sw = np.concatenate([kr[..., 16:32], kr[..., 0:16]], axis=-1)
    w_in_ext = np.ascontiguousarray(np.concatenate(
        [qa, ka, qc, kc_, iq, iq_sw, kr, kr_sw, va, vc, cq, ckv, ik, iw], axis=-1))
    assert w_in_ext.shape[-1] == WIE
    w_uq = np.asarray(inputs["mla_w_uq"], f32).reshape(NL, 256, 5, 96)
    w_uq_sw = np.concatenate([w_uq[..., 0:64], w_uq[..., 80:96], w_uq[..., 64:80]], axis=-1)
    w_uq2 = np.ascontiguousarray(np.concatenate([w_uq.reshape(NL, 256, 480), w_uq_sw.reshape(NL, 256, 480)], axis=-1))
    w_ukv = np.asarray(inputs["mla_w_ukv"], f32).reshape(NL, 128, 5, 128)
    w_ukv2 = np.ascontiguousarray(np.concatenate(
        [w_ukv[..., 0:64].reshape(NL, 128, 320), w_ukv[..., 64:128].reshape(NL, 128, 320)], axis=-1))
    gains = np.stack([np.asarray(inputs[n], f32) for n in ("norm_ffn1", "norm_mix", "norm_ffn2")], axis=1)
    gainsT = np.ascontiguousarray(gains.reshape(NL, 3, 8, 128).transpose(3, 0, 1, 2))
    fin_bc = np.ascontiguousarray(np.broadcast_to(np.asarray(inputs["final_norm"], f32)[None, :], (128, D)))
    ada_bT = np.ascontiguousarray(np.asarray(inputs["ada_b"], f32).reshape(NL, 72, 128).transpose(2, 0, 1))
    qnT = np.ascontiguousarray(np.asarray(inputs["mla_q_norm"], f32).reshape(NL, 2, 128).transpose(2, 0, 1))
    kvnT = np.ascontiguousarray(np.asarray(inputs["mla_kv_norm"], f32).T)
    ikn = np.asarray(inputs["idx_k_norm"], f32)
    ikn_sw = np.concatenate([ikn[:, 16:32], ikn[:, 0:16], ikn[:, 32:64]], axis=-1)
    iknT = np.ascontiguousarray(np.stack([ikn, ikn_sw], axis=-1).transpose(1, 0, 2))
    xs = np.arange(XW)[None, :]
    pp = np.arange(128)[:, None]
    delta = xs - pp - OFF
    bucket = t5_bucket_np(delta)
    bstrip = np.ascontiguousarray(rel_bias[bucket, :].transpose(2, 0, 1))
    dpos = delta >= 0
    cntA = (dpos & (delta <= 128)).astype(f32) + (dpos & (delta % 4 == 0) & (delta <= 512)).astype(f32) \
        + (dpos & (delta % 16 == 0) & (delta <= 2048)).astype(f32)
    cntC = dpos.astype(f32)
    cnts = np.ascontiguousarray(np.stack([cntA, cntC], axis=0))
    inv = (10000.0 ** (-np.arange(16, dtype=np.float64) / 16.0))
    p = np.arange(128)
    ropec = np.zeros((128, 8), f32)
    r64 = p % 64
    isx1 = r64 < 16
    isx2 = (r64 >= 16) & (r64 < 32)
    ropec[:, 0] = np.where(isx1 | isx2, inv[p % 16] / TWO_PI, 0.0)
    sgnA = np.where(isx1, -1.0, np.where(isx2, 1.0, 1.0))
    ropec[:, 2] = -TWO_PI * SHR * sgnA
    ropec[:, 3] = np.pi * SHR * sgnA
    bx1 = (p >= 64) & (p < 80)
    bx2 = (p >= 80) & (p < 96)
    ropec[:, 1] = np.where(bx1 | bx2, inv[p % 16] / TWO_PI, 0.0)
    sgnB = np.where(bx1, -1.0, 1.0)
    ropec[:, 4] = -TWO_PI * SHR * sgnB
    ropec[:, 5] = np.pi * SHR * sgnB
    ropec[:, 6] = -TWO_PI * SHR
    ropec[:, 7] = np.pi * SHR
    ident = np.eye(128, dtype=f32)
    qq = np.arange(128)[:, None]
    kk = np.arange(128)[None, :]
    negmask = np.where(kk <= qq, 0.0, NEG).astype(f32)
    shared = {
        "ada_w": np.asarray(inputs["ada_w"], f32), "ada_bT": ada_bT, "gainsT": gainsT, "fin_bc": fin_bc,
        "wgu1": np.asarray(inputs["ffn1_w_gu"], f32), "wd1": np.asarray(inputs["ffn1_w_down"], f32),
        "wgu2": np.asarray(inputs["ffn2_w_gu"], f32), "wd2": np.asarray(inputs["ffn2_w_down"], f32),
        "w_in_ext": w_in_ext, "w_uq2": w_uq2, "w_ukv2": w_ukv2, "qnT": qnT, "kvnT": kvnT, "iknT": iknT,
        "w_out": np.asarray(inputs["w_out"], f32), "bstrip": bstrip, "cnts": cnts, "ropec": ropec,
        "ident": ident, "negmask": negmask,
    }
    in_maps = []
    for r in range(8):
        b0 = 2 * r
        m = dict(shared)
        m["x"] = np.ascontiguousarray(x[b0:b0 + 2])
        m["cT"] = np.ascontiguousarray(c[b0:b0 + 2].reshape(2, 8, 128).transpose(2, 1, 0))
        m["posr"] = np.ascontiguousarray(np.broadcast_to(pos[b0:b0 + 2][:, None, :], (2, 128, T)))
        in_maps.append(m)
    return in_maps


def kernel(**inputs):
    in_maps = host_prep(inputs)
    nc, _ = build()
    res = run_bass_kernel_spmd(nc, in_maps, core_ids=list(range(8)))
    out = np.concatenate([np.asarray(r["out"]) for r in res.results], axis=0)
    return out.astype(np.float32)
```
